# Optimizing a Trainium2 kernel written in Bass

```python
import math
import jax, jax.numpy as jnp
from jax import lax
import numpy as np

D_MODEL = 2048
BATCH = 4
SEQ = 2048
DEPTH = 4
DEC_BATCH = 8
DEC_SEQ = 8
PAST_LEN = 16384
PAGE_SIZE = 128

A_HEADS = 8
A_HEAD_DIM = D_MODEL // 16
A_WIDTH = A_HEADS * A_HEAD_DIM
MOBA_BLOCK = 256
MOBA_TOPK = 3
MOBA_Q_CHUNK = 32
REL_BUCKETS = 32
REL_MAX_DIST = 128
B_HEADS = 4
B_HEAD_DIM = D_MODEL // 8
B_WIDTH = B_HEADS * B_HEAD_DIM
MLSTM_CHUNK = 64
C_WIDTH = D_MODEL
CONV_K = 3
N_EVEN = (DEPTH + 1) // 2
N_ODD = DEPTH // 2
EVEN_SPLITS = (A_WIDTH,) * 4 + (B_WIDTH,) * 5 + (B_HEADS, B_HEADS)
EVEN_IN = sum(EVEN_SPLITS)
EVEN_MIX = A_WIDTH + B_WIDTH
ODD_IN = 4 * C_WIDTH
ALPHA = (2.0 * DEPTH) ** 0.25
BETA = (8.0 * DEPTH) ** -0.25
LN_EPS = 1e-5

kernel_name = 'moba_mlstm_shortconv_deepnorm_step'


def _split(z, sizes):
    offs = np.cumsum(sizes)[:-1]
    return jnp.split(z, [int(o) for o in offs], axis=-1)


def _layernorm(x, g, b):
    xf = x.astype(jnp.float32)
    mu = jnp.mean(xf, axis=-1, keepdims=True)
    var = jnp.mean(jnp.square(xf - mu), axis=-1, keepdims=True)
    return ((xf - mu) * lax.rsqrt(var + LN_EPS) * g.astype(jnp.float32) + b.astype(jnp.float32)).astype(x.dtype)


def _head_norm(h, g):
    mu = jnp.mean(h, axis=-1, keepdims=True)
    var = jnp.mean(jnp.square(h - mu), axis=-1, keepdims=True)
    return (h - mu) * lax.rsqrt(var + LN_EPS) * g.astype(jnp.float32).reshape(h.shape[2], h.shape[3])


def _t5_bucket(dist):
    n = jnp.maximum(dist, 0)
    max_exact = REL_BUCKETS // 2
    nf = jnp.maximum(n, 1).astype(jnp.float32)
    large = max_exact + (jnp.log(nf / max_exact) / math.log(REL_MAX_DIST / max_exact)
                         * (REL_BUCKETS - max_exact)).astype(jnp.int32)
    large = jnp.minimum(large, REL_BUCKETS - 1)
    return jnp.where(n < max_exact, n, large)


def _attend(s_own, v_own, s_sel, v_sel):
    f32 = jnp.float32
    if s_sel is None:
        p = jax.nn.softmax(s_own, axis=-1)
        return jnp.einsum('bhqk,bkhd->bhqd', p, v_own.astype(f32))
    bsz, nh, nq, n_sel, kb = s_sel.shape
    ko = s_own.shape[-1]
    p = jax.nn.softmax(jnp.concatenate([s_own, s_sel.reshape(bsz, nh, nq, n_sel * kb)], axis=-1), axis=-1)
    o = jnp.einsum('bhqk,bkhd->bhqd', p[..., :ko], v_own.astype(f32))
    o = o + jnp.einsum('bhqnk,bhqnkd->bhqd', p[..., ko:].reshape(bsz, nh, nq, n_sel, kb), v_sel.astype(f32))
    return o


def _moba_prompt(q, k, v, rel_bias):
    f32 = jnp.float32
    bsz, T, H, dh = q.shape
    nb = -(-T // MOBA_BLOCK)
    pad = nb * MOBA_BLOCK - T
    kb = jnp.pad(k, ((0, 0), (0, pad), (0, 0), (0, 0))).reshape(bsz, nb, MOBA_BLOCK, H, dh)
    vb = jnp.pad(v, ((0, 0), (0, pad), (0, 0), (0, 0))).reshape(bsz, nb, MOBA_BLOCK, H, dh)
    k_mean = jnp.mean(kb.astype(f32), axis=2)
    pos = jnp.arange(T, dtype=jnp.int32)
    gate = jnp.einsum('bthd,bnhd->bhtn', q.astype(f32), k_mean)
    fully_past = jnp.arange(nb, dtype=jnp.int32)[None, :] < (pos // MOBA_BLOCK)[:, None]
    gate = jnp.where(fully_past, gate, -jnp.inf)
    n_sel = max(1, min(MOBA_TOPK, nb - 1))
    sel_val, sel_idx = lax.top_k(gate, n_sel)
    sel_ok = jnp.isfinite(sel_val)
    qc = math.gcd(T, MOBA_Q_CHUNK)
    n_chunks = T // qc
    q_c = jnp.moveaxis(q.reshape(bsz, n_chunks, qc, H, dh), 1, 0)
    idx_c = jnp.moveaxis(sel_idx.reshape(bsz, H, n_chunks, qc, n_sel), 2, 0)
    ok_c = jnp.moveaxis(sel_ok.reshape(bsz, H, n_chunks, qc, n_sel), 2, 0)
    bi = jnp.arange(bsz)[:, None, None, None]
    hi = jnp.arange(H)[None, :, None, None]
    hi5 = hi[..., None]
    table = rel_bias.astype(f32)
    scale = dh ** -0.5
    offs = jnp.arange(MOBA_BLOCK, dtype=jnp.int32)

    def one_chunk(args):
        c, qch, idx, ok = args
        start = c * qc
        pos_q = start + jnp.arange(qc, dtype=jnp.int32)
        own = start // MOBA_BLOCK
        qh = jnp.swapaxes(qch, 1, 2).astype(f32) * scale
        k_own = lax.dynamic_index_in_dim(kb, own, axis=1, keepdims=False)
        v_own = lax.dynamic_index_in_dim(vb, own, axis=1, keepdims=False)
        d_own = pos_q[:, None] - (own * MOBA_BLOCK + offs)[None, :]
        s_own = jnp.einsum('bhqd,bkhd->bhqk', qh, k_own) + table[:, _t5_bucket(d_own)][None]
        s_own = jnp.where(d_own >= 0, s_own, -jnp.inf)
        k_sel = kb[bi, idx, :, hi, :]
        v_sel = vb[bi, idx, :, hi, :]
        d_sel = pos_q[None, None, :, None, None] - (idx[..., None] * MOBA_BLOCK + offs)
        s_sel = jnp.einsum('bhqd,bhqnkd->bhqnk', qh, k_sel) + table[hi5, _t5_bucket(d_sel)]
        s_sel = jnp.where(ok[..., None], s_sel, -jnp.inf)
        return jnp.swapaxes(_attend(s_own, v_own, s_sel, v_sel), 1, 2)

    out = lax.map(one_chunk, (jnp.arange(n_chunks, dtype=jnp.int32), q_c, idx_c, ok_c))
    return jnp.moveaxis(out, 0, 1).reshape(bsz, T, H, dh)


def _moba_sample(q, k, v, k_pool, v_pool, layer, page_table, rel_bias):
    f32 = jnp.float32
    bsz, S, H, dh = q.shape
    n_pages = page_table.shape[1]
    past = n_pages * PAGE_SIZE
    nbp = past // MOBA_BLOCK
    own_start = nbp * MOBA_BLOCK
    n_oc = past - own_start
    ppb = MOBA_BLOCK // PAGE_SIZE
    table = rel_bias.astype(f32)
    scale = dh ** -0.5
    pos_q = past + jnp.arange(S, dtype=jnp.int32)
    qh = jnp.swapaxes(q, 1, 2).astype(f32) * scale
    k_past = k_pool[layer, page_table].reshape(bsz, past, H, dh)
    k_own = jnp.concatenate([k_past[:, own_start:], k.astype(k_past.dtype)], axis=1)
    v_oc = v_pool[layer, page_table[:, own_start // PAGE_SIZE:]].reshape(bsz, n_oc, H, dh)
    v_own = jnp.concatenate([v_oc, v.astype(v_oc.dtype)], axis=1)
    d_own = pos_q[:, None] - (own_start + jnp.arange(n_oc + S, dtype=jnp.int32))[None, :]
    s_own = jnp.einsum('bhqd,bkhd->bhqk', qh, k_own) + table[:, _t5_bucket(d_own)][None]
    s_own = jnp.where(d_own >= 0, s_own, -jnp.inf)
    if nbp == 0:
        return jnp.swapaxes(_attend(s_own, v_own, None, None), 1, 2)
    n_sel = min(MOBA_TOPK, nbp)
    kb = k_past[:, :own_start].reshape(bsz, nbp, MOBA_BLOCK, H, dh)
    k_mean = jnp.mean(kb.astype(f32), axis=2)
    gate = jnp.einsum('bshd,bnhd->bhsn', q.astype(f32), k_mean)
    _, idx = lax.top_k(gate, n_sel)
    bi = jnp.arange(bsz)[:, None, None, None]
    hi = jnp.arange(H)[None, :, None, None]
    hi5 = hi[..., None]
    k_sel = kb[bi, idx, :, hi, :]
    phys = page_table[bi[..., None], idx[..., None] * ppb + jnp.arange(ppb, dtype=jnp.int32)]
    v_sel = v_pool[layer, phys, :, hi5, :].reshape(bsz, H, S, n_sel, MOBA_BLOCK, dh)
    d_sel = pos_q[None, None, :, None, None] - (idx[..., None] * MOBA_BLOCK + jnp.arange(MOBA_BLOCK, dtype=jnp.int32))
    s_sel = jnp.einsum('bhqd,bhqnkd->bhqnk', qh, k_sel) + table[hi5, _t5_bucket(d_sel)]
    return jnp.swapaxes(_attend(s_own, v_own, s_sel, v_sel), 1, 2)


def _to_chunks(a, nc, L):
    bsz, T, H = a.shape[:3]
    a = a.reshape((bsz, nc, L, H) + a.shape[3:])
    return jnp.moveaxis(a, (1, 3), (0, 2))


def _mlstm(q, k, v, i_pre, f_pre, c0, n0, m0):
    f32 = jnp.float32
    bsz, T, H, dk = q.shape
    L = math.gcd(T, MLSTM_CHUNK)
    nc = T // L
    qs = _to_chunks(q.astype(f32), nc, L)
    ks = _to_chunks(k.astype(f32) * (dk ** -0.5), nc, L)
    vs = _to_chunks(v.astype(f32), nc, L)
    lf = _to_chunks(jax.nn.log_sigmoid(f_pre.astype(f32)), nc, L)
    li = _to_chunks(i_pre.astype(f32), nc, L)
    causal = jnp.tril(jnp.ones((L, L), dtype=bool))

    def step(carry, inp):
        c, n, m = carry
        qc, kc, vc, lfc, lic = inp
        b = jnp.cumsum(lfc, axis=-1)
        log_d = jnp.where(causal, b[..., :, None] - b[..., None, :] + lic[..., None, :], -jnp.inf)
        m_inter = b + m[..., None]
        m_t = jnp.maximum(m_inter, jnp.max(log_d, axis=-1))
        s = jnp.einsum('bhtd,bhsd->bhts', qc, kc) * jnp.exp(log_d - m_t[..., None])
        w_inter = jnp.exp(m_inter - m_t)
        num = jnp.einsum('bhts,bhse->bhte', s, vc) + w_inter[..., None] * jnp.einsum('bhtd,bhde->bhte', qc, c)
        den = jnp.sum(s, axis=-1) + w_inter * jnp.einsum('bhtd,bhd->bht', qc, n)
        h = num / jnp.maximum(jnp.abs(den), jnp.exp(-m_t))[..., None]
        m_new = m_t[..., -1]
        decay = jnp.exp(b[..., -1] + m - m_new)
        wk = jnp.exp(b[..., -1:] - b + lic - m_new[..., None])
        c_new = decay[..., None, None] * c + jnp.einsum('bhsd,bhse->bhde', kc * wk[..., None], vc)
        n_new = decay[..., None] * n + jnp.einsum('bhs,bhsd->bhd', wk, kc)
        return (c_new, n_new, m_new), h

    (c1, n1, m1), hs = lax.scan(step, (c0.astype(f32), n0.astype(f32), m0.astype(f32)), (qs, ks, vs, lf, li))
    h = jnp.moveaxis(hs, (0, 2), (1, 3)).reshape(bsz, T, H, v.shape[-1])
    return h, (c1, n1, m1)


def _short_conv(u, buf, w):
    T = u.shape[1]
    uf = jnp.concatenate([buf.astype(u.dtype), u], axis=1)
    y = uf[:, 0:T] * w[0]
    for j in range(1, CONV_K):
        y = y + uf[:, j:j + T] * w[j]
    return y, uf[:, T:]


def _even_layer(x, w_in, w_out, gate_bias, norm_g, rel_bias, past):
    f32 = jnp.float32
    bsz, T, _ = x.shape
    z = jnp.einsum('btd,de->bte', x, w_in)
    qa, ka, va, ga, qm, km, vm, om, gm, im, fm = _split(z, EVEN_SPLITS)
    qa, ka, va = [a.reshape(bsz, T, A_HEADS, A_HEAD_DIM) for a in (qa, ka, va)]
    qm, km, vm = [a.reshape(bsz, T, B_HEADS, B_HEAD_DIM) for a in (qm, km, vm)]
    if past is None:
        attn = _moba_prompt(qa, ka, va, rel_bias)
        c0 = jnp.zeros((bsz, B_HEADS, B_HEAD_DIM, B_HEAD_DIM), f32)
        n0 = jnp.zeros((bsz, B_HEADS, B_HEAD_DIM), f32)
        m0 = jnp.zeros((bsz, B_HEADS), f32)
    else:
        k_pool, v_pool, layer, page_table, c0, n0, m0 = past
        attn = _moba_sample(qa, ka, va, k_pool, v_pool, layer, page_table, rel_bias)
    hm, (c1, n1, m1) = _mlstm(qm, km, vm, im + gate_bias[:B_HEADS], fm + gate_bias[B_HEADS:], c0, n0, m0)
    hm = _head_norm(hm, norm_g).reshape(bsz, T, B_WIDTH) * jax.nn.sigmoid(om.astype(f32))
    mix = jnp.concatenate([attn.reshape(bsz, T, A_WIDTH) * jax.nn.silu(ga.astype(f32)),
                           hm * jax.nn.silu(gm.astype(f32))], axis=-1).astype(x.dtype)
    return jnp.einsum('bte,ed->btd', mix, w_out), (ka, va, c1, n1, m1)


def _odd_layer(x, w_in, w_out, conv_w, buf):
    bsz = x.shape[0]
    z = jnp.einsum('btd,de->bte', x, w_in)
    b_gate, c_gate, xin, g = _split(z, (C_WIDTH,) * 4)
    u = c_gate * xin
    if buf is None:
        buf = jnp.zeros((bsz, CONV_K - 1, C_WIDTH), u.dtype)
    y, new_buf = _short_conv(u, buf, conv_w)
    mix = (b_gate * y * jax.nn.silu(g)).astype(x.dtype)
    return jnp.einsum('bte,ed->btd', mix, w_out), new_buf


def setup_inputs(seed: int = 0) -> dict:
    key = jax.random.key(seed)
    ks = jax.random.split(key, 20)
    nrm = jax.random.normal
    f32 = jnp.float32
    n_pages = PAST_LEN // PAGE_SIZE
    n_pool = (DEC_BATCH * n_pages * 5) // 4
    x_prompt = nrm(ks[0], (BATCH, SEQ, D_MODEL), f32)
    x_sample = nrm(ks[1], (DEC_BATCH, DEC_SEQ, D_MODEL), f32)
    cache_k = nrm(ks[2], (N_EVEN, n_pool, PAGE_SIZE, A_HEADS, A_HEAD_DIM), f32)
    cache_v = nrm(ks[3], (N_EVEN, n_pool, PAGE_SIZE, A_HEADS, A_HEAD_DIM), f32)
    page_table = jax.random.permutation(ks[4], n_pool)[:DEC_BATCH * n_pages].reshape(DEC_BATCH, n_pages).astype(jnp.int32)
    state_C = 0.1 * nrm(ks[5], (N_EVEN, DEC_BATCH, B_HEADS, B_HEAD_DIM, B_HEAD_DIM), f32)
    state_n = 0.5 * nrm(ks[6], (N_EVEN, DEC_BATCH, B_HEADS, B_HEAD_DIM), f32)
    state_m = nrm(ks[7], (N_EVEN, DEC_BATCH, B_HEADS), f32)
    state_conv = nrm(ks[8], (N_ODD, DEC_BATCH, CONV_K - 1, C_WIDTH), f32)
    w_in_even = nrm(ks[9], (N_EVEN, D_MODEL, EVEN_IN), f32) * D_MODEL ** -0.5
    w_out_even = nrm(ks[10], (N_EVEN, EVEN_MIX, D_MODEL), f32) * (EVEN_MIX ** -0.5 * BETA)
    i_bias = 0.1 * nrm(ks[11], (N_EVEN, B_HEADS), f32)
    f_bias = jnp.linspace(3.0, 6.0, B_HEADS, dtype=f32)[None, :] + 0.1 * nrm(ks[12], (N_EVEN, B_HEADS), f32)
    mlstm_gate_bias = jnp.concatenate([i_bias, f_bias], axis=-1)
    mlstm_norm_g = 1.0 + 0.05 * nrm(ks[13], (N_EVEN, B_WIDTH), f32)
    rel_bias = 0.5 * nrm(ks[14], (A_HEADS, REL_BUCKETS), f32)
    w_in_odd = nrm(ks[15], (N_ODD, D_MODEL, ODD_IN), f32) * D_MODEL ** -0.5
    w_out_odd = nrm(ks[16], (N_ODD, C_WIDTH, D_MODEL), f32) * (C_WIDTH ** -0.5 * BETA)
    conv_w = nrm(ks[17], (N_ODD, CONV_K, C_WIDTH), f32) * CONV_K ** -0.5
    ln_g = 1.0 + 0.05 * nrm(ks[18], (DEPTH, D_MODEL), f32)
    ln_b = 0.02 * nrm(ks[19], (DEPTH, D_MODEL), f32)
    return {'x_prompt': x_prompt, 'x_sample': x_sample, 'cache_k': cache_k, 'cache_v': cache_v,
            'page_table': page_table, 'state_C': state_C, 'state_n': state_n, 'state_m': state_m,
            'state_conv': state_conv, 'w_in_even': w_in_even, 'w_out_even': w_out_even,
            'mlstm_gate_bias': mlstm_gate_bias, 'mlstm_norm_g': mlstm_norm_g, 'rel_bias': rel_bias,
            'w_in_odd': w_in_odd, 'w_out_odd': w_out_odd, 'conv_w': conv_w, 'ln_g': ln_g, 'ln_b': ln_b}


def reference(x_prompt, x_sample, cache_k, cache_v, page_table, state_C, state_n, state_m, state_conv,
              w_in_even, w_out_even, mlstm_gate_bias, mlstm_norm_g, rel_bias,
              w_in_odd, w_out_odd, conv_w, ln_g, ln_b):
    xp, xs = x_prompt, x_sample
    pk, pv, pc, pn, pm, pb = [], [], [], [], [], []
    sk, sv, sc, sn, sm, sb = [], [], [], [], [], []
    for layer in range(DEPTH):
        if layer % 2 == 0:
            e = layer // 2
            yp, (k1, v1, c1, n1, m1) = _even_layer(xp, w_in_even[e], w_out_even[e], mlstm_gate_bias[e],
                                                   mlstm_norm_g[e], rel_bias, None)
            ys, (k2, v2, c2, n2, m2) = _even_layer(xs, w_in_even[e], w_out_even[e], mlstm_gate_bias[e],
                                                   mlstm_norm_g[e], rel_bias,
                                                   (cache_k, cache_v, e, page_table, state_C[e], state_n[e], state_m[e]))
            pk.append(k1); pv.append(v1); pc.append(c1); pn.append(n1); pm.append(m1)
            sk.append(k2); sv.append(v2); sc.append(c2); sn.append(n2); sm.append(m2)
        else:
            o = layer // 2
            yp, b1 = _odd_layer(xp, w_in_odd[o], w_out_odd[o], conv_w[o], None)
            ys, b2 = _odd_layer(xs, w_in_odd[o], w_out_odd[o], conv_w[o], state_conv[o])
            pb.append(b1); sb.append(b2)
        xp = _layernorm(ALPHA * xp + yp, ln_g[layer], ln_b[layer])
        xs = _layernorm(ALPHA * xs + ys, ln_g[layer], ln_b[layer])
    return (xp, xs,
            jnp.stack(pk), jnp.stack(pv), jnp.stack(pc), jnp.stack(pn), jnp.stack(pm), jnp.stack(pb),
            jnp.stack(sk), jnp.stack(sv), jnp.stack(sc), jnp.stack(sn), jnp.stack(sm), jnp.stack(sb))
```

```python
import math
import numpy as np
from contextlib import ExitStack
import concourse.bass as bass
import concourse.mybir as mybir
from concourse.bass_utils import run_bass_kernel_spmd

F32 = mybir.dt.float32
BF16 = mybir.dt.bfloat16
I32 = mybir.dt.int32
AF = mybir.ActivationFunctionType
ALU = mybir.AluOpType
AX = mybir.AxisListType

T = 2048
D = 2048
S = 8
NPOOL = 1280
NPAGES = 128
ALPHA = 8.0 ** 0.25
LN_EPS = 1e-5
NEG = -30000.0
RING = 5


class Buf:
    __slots__ = ("name", "w", "r")

    def __init__(self, name):
        self.name = name
        self.w = {}
        self.r = {}


class Eng:
    def __init__(self, h, sem, selfsync):
        self.h = h
        self.sem = sem
        self.selfsync = selfsync
        self.cnt = 0
        self.waited = {}


class KB:
    def __init__(self, nc, es):
        self.nc = nc
        self.es = es
        self.sems = {}
        self.dtot = {}
        self.eng = {}
        import os as _os
        _ss = _os.environ.get("KSELFSYNC", "1") == "1"
        for n, h, ss in (("pe", nc.tensor, False), ("act", nc.scalar, _ss), ("dve", nc.vector, _ss),
                         ("pool", nc.gpsimd, True), ("sp", nc.sync, False)):
            self.eng[n] = Eng(h, self.mksem("e_" + n), ss)

    def mksem(self, n):
        self.sems[n] = self.es.enter_context(self.nc.semaphore(n))
        return n

    def dsem(self, n):
        self.mksem(n)
        self.dtot[n] = 0
        return n

    def _waits(self, e, r, w):
        need = {}
        for b in r:
            for s, v in b.w.items():
                if v > need.get(s, 0):
                    need[s] = v
            if b.name[0] == "#":
                for s, v in b.r.items():
                    if s != e.sem and v > need.get(s, 0):
                        need[s] = v
        for b in w:
            for s, v in b.w.items():
                if v > need.get(s, 0):
                    need[s] = v
            for s, v in b.r.items():
                if s == e.sem:
                    continue
                if v > need.get(s, 0):
                    need[s] = v
        for s, v in need.items():
            if s in self.dtot:
                v = self.dtot[s]
            if s == e.sem and not e.selfsync:
                continue
            if e.waited.get(s, 0) < v:
                e.h.wait_ge(self.sems[s], v)
                e.waited[s] = v

    def op(self, en, fn, r=(), w=()):
        e = self.eng[en]
        self._waits(e, r, w)
        ins = fn(e.h)
        e.cnt += 1
        ins.then_inc(self.sems[e.sem], 1)
        for b in r:
            b.r[e.sem] = e.cnt
        for b in w:
            b.w = {e.sem: e.cnt}
            b.r = {}
        return ins

    def dma(self, qn, ds, fn, r=(), w=()):
        e = self.eng[qn]
        self._waits(e, r, w)
        if self.dtot[ds] > 0 and e.waited.get(ds, 0) < self.dtot[ds]:
            e.h.wait_ge(self.sems[ds], self.dtot[ds])
            e.waited[ds] = self.dtot[ds]
        ins = fn(e.h)
        self.dtot[ds] += 16
        ins.then_inc(self.sems[ds], 16)
        tok = self.dtot[ds]
        for b in r:
            b.r[ds] = tok
        for b in w:
            b.w = {ds: tok}
            b.r = {}
        return ins

    def barrier(self):
        tot = {e.sem: e.cnt for e in self.eng.values()}
        for s, v in self.dtot.items():
            tot[s] = v
        for e in self.eng.values():
            for s, v in tot.items():
                if s == e.sem or v == 0:
                    continue
                if e.waited.get(s, 0) < v:
                    e.h.wait_ge(self.sems[s], v)
                    e.waited[s] = v


class Stream:
    def __init__(self, name, n, groups, L):
        self.name = name
        self.n = n
        self.groups = groups
        self.L = L
        self.nch = n // L


def t5_bucket_np(dist):
    n = np.maximum(dist, 0)
    max_exact = 16
    nf = np.maximum(n, 1).astype(np.float32)
    large = max_exact + (np.log(nf / max_exact) / math.log(128 / max_exact) * (32 - max_exact)).astype(np.int32)
    large = np.minimum(large, 31)
    return np.where(n < max_exact, n, large)


def build(cfg):
    layers = cfg.get("layers", [0, 1, 2, 3])
    do_sample = cfg.get("sample", True)
    do_moba_s = cfg.get("moba_s", True)
    npool = cfg.get("npool", NPOOL)
    nc = bass.Bass("TRN2", target_bir_lowering=False)
    d = {}

    def din(name, shape, dt=F32):
        d[name] = nc.dram_tensor(name, shape, dt, kind="ExternalInput").ap()

    def dout(name, shape, dt=F32):
        d[name] = nc.dram_tensor(name, shape, dt, kind="ExternalOutput").ap()

    din("xpT", [D, T])
    din("xsT", [D, S])
    din("ck", [2 * npool * 128, 1024])
    din("cv", [2 * npool * 128, 1024])
    din("pt", [1, NPAGES], I32)
    din("sC", [2, 4, 256, 256])
    din("sn", [2, 4, 256, 1])
    din("sm", [2, 1, 4])
    din("sconv", [2, 128, 16, 2])
    din("wie", [2, D, 9224])
    din("woe", [2, D, D])
    din("gbrep", [128, 2, 8])
    din("ngT", [2, 128, 8])
    din("relp", [128, 8, 256])
    din("rels63", [64, 256])
    din("relso", [64, 8])
    din("c31", [128, 8])
    din("c31s", [64, 1])
    din("wio", [2, D, 8192])
    din("woo", [2, D, D])
    din("cwT", [2, 128, 16, 3])
    din("lngT", [4, 128, 16])
    din("lnbT", [4, 128, 16])
    din("c_identf", [128, 128])
    din("c_trilT", [128, 128])
    din("c_cmask", [128, 256])
    din("c_gmask", [128, 16, 8])
    din("c_iota", [128, 1])
    din("c_bdiag", [64, 8, 128])
    din("c_cmasks", [64, 8])
    dout("ypT", [D, T])
    dout("ysT", [D, S])
    dout("pkT", [2, 1024, T])
    dout("pvT", [2, 1024, T])
    dout("pCa", [2, 4, 128, 2, 257])
    dout("pm", [2, 1, 4])
    dout("pbT", [2, 128, 16, 2])
    dout("skT", [2, 1024, S])
    dout("svT", [2, 1024, S])
    dout("sCa", [2, 4, 128, 2, 257])
    dout("smo", [2, 1, 4])
    dout("sbT", [2, 128, 16, 2])

    es = ExitStack()
    with es:
        k = KB(nc, es)
        uid = {"n": 0}

        def sb(name, shape, dt=F32):
            return es.enter_context(nc.sbuf_tensor("s_" + name, shape, dt))

        def ps(name, shape, dt=F32):
            return es.enter_context(nc.psum_tensor("ps_" + name, shape, dt))

        class Phase:
            def __enter__(self):
                k.barrier()
                self.es = ExitStack()
                self.es.__enter__()
                return self

            def sb(self, name, shape, dt=F32):
                uid["n"] += 1
                return self.es.enter_context(nc.sbuf_tensor(f"t_{name}_{uid['n']}", shape, dt))

            def __exit__(self, *a):
                k.barrier()
                return self.es.__exit__(*a)

        P = Stream("p", T, [(g * 512, 512) for g in range(4)], 128)
        Sm = Stream("s", S, [(0, S)], S)
        streams = [P, Sm] if do_sample else [P]

        big = sb("big", [128, 16384])
        P.mT = big[:].bitcast(BF16).rearrange("p (f t) -> p f t", f=16)
        P.xT = sb("xT", [128, 16, T], BF16)
        Sm.xT = sb("xsT_", [128, 16, S], BF16)
        Sm.mT = sb("msT_", [128, 16, S], BF16)
        for st in [P, Sm]:
            st.xb = [Buf(f"{st.name}x{g}") for g in range(len(st.groups))]
            st.mb = [Buf(f"{st.name}m{f}") for f in range(16)]
            st.accb = [Buf(f"{st.name}acc1_{g}") for g in range(len(st.groups))]
            st.acc2b = [Buf(f"{st.name}acc2_{g}") for g in range(len(st.groups))]
            st.gw = sb(f"gw{st.name}", [128, st.nch * 4])
            st.gf = sb(f"gf{st.name}", [128, st.nch * 4])
            st.gom = sb(f"gom{st.name}", [128, st.nch * 4])
            st.gateb = Buf(f"gate{st.name}")
        ring = sb("ring", [128, RING, 16, 128], BF16)
        ringb = [Buf(f"ring{i}") for i in range(RING)]
        ringd = [k.dsem(f"d_ring{i}") for i in range(RING)]
        work = sb("work", [128, 4, 512])
        workb = [Buf(f"work{i}") for i in range(4)]
        stg = sb("stg", [128, 2, 512])
        stgb = [Buf("stg0"), Buf("stg1")]
        stgd = [k.dsem("d_stg0"), k.dsem("d_stg1")]
        stgi = {"i": 0}
        identf = sb("identf", [128, 128])
        identb = sb("identb", [128, 128], BF16)
        onesf = sb("onesf", [128, 128])
        trilT = sb("trilT", [128, 128])
        BW = sb("BW", [128, 8, 256])
        c31 = sb("c31", [128, 8])
        gmask = sb("gmask", [128, 16, 8])
        gbrep = sb("gbrep", [128, 2, 8])
        ngT = sb("ngT", [128, 2, 8])
        cwT = sb("cwT", [128, 2, 16, 3])
        lngT = sb("lngT", [128, 4, 16])
        lnbT = sb("lnbT", [128, 4, 16])
        iota = sb("iota", [128, 1])
        attnS = sb("attnS", [128, 8, S])
        attnSb = Buf("attnS")
        qs = sb("qs", [128, 8, S], BF16)
        ks = sb("ks", [128, 8, S], BF16)
        vs = sb("vs", [128, 8, S], BF16)
        qkvb = Buf("qkvs")
        cb_ = Buf("consts")
        dconst = k.dsem("d_const")
        dmisc = k.dsem("d_misc")

        P4 = ps("P4", [128, 4, 512])
        p4b = [Buf(f"#p4_{i}") for i in range(4)]
        PT = ps("PT", [128, 8, 128], BF16)
        ptb = Buf("#PT")
        PT2 = ps("PT2", [128, 8, 128], BF16)
        pt2b = Buf("#PT2")
        PA = ps("PA", [128, 512])
        pab = Buf("#PA")
        PB = ps("PB", [128, 512])
        pbb = Buf("#PB")
        P4f = P4[:].rearrange("p a b -> p (a b)")

        def cload(dst, src):
            k.dma("sp", dconst, lambda h, dst=dst, src=src: h.dma_start(out=dst, in_=src), w=[cb_])

        cload(identf[:], d["c_identf"])
        cload(trilT[:], d["c_trilT"])
        cload(BW[:], d["relp"])
        cload(c31[:], d["c31"])
        cload(gmask[:], d["c_gmask"])
        cload(gbrep[:], d["gbrep"])
        cload(iota[:], d["c_iota"])
        for e in range(2):
            cload(ngT[:, e, :], d["ngT"][e])
            cload(cwT[:, e, :, :], d["cwT"][e])
        for l in range(4):
            cload(lngT[:, l, :], d["lngT"][l])
            cload(lnbT[:, l, :], d["lnbT"][l])
        k.op("dve", lambda h: h.tensor_copy(out=identb[:], in_=identf[:]), r=[cb_], w=[cb_])
        k.op("dve", lambda h: h.memset(onesf[:], 1.0), w=[cb_])
        cm = work[:, 0, 0:256]
        k.dma("sp", dconst, lambda h: h.dma_start(out=cm, in_=d["c_cmask"]), w=[workb[0]])
        for hh in range(8):
            k.op("dve", lambda h, hh=hh: h.scalar_tensor_tensor(out=BW[:, hh, :], in0=BW[:, hh, :], scalar=c31[:, hh:hh + 1],
                                                              in1=cm, op0=ALU.subtract, op1=ALU.add), r=[cb_, workb[0]], w=[cb_])

        dxin = k.dsem("d_xin")
        for kc in range(16):
            k.dma("pool", dxin, lambda h, kc=kc: h.dma_start(out=P.xT[:, kc, :], in_=d["xpT"][kc * 128:(kc + 1) * 128, :]), w=P.xb)
        if do_sample:
            with nc.allow_non_contiguous_dma(reason="tiny"):
                k.dma("pool", dxin, lambda h: h.dma_start(out=Sm.xT[:], in_=d["xsT"].rearrange("(kc p) s -> p kc s", p=128)), w=Sm.xb)

        plan = []

        def plan_even(e):
            W = d["wie"][e]
            pl = [("gif", W[:, 9216:9224], 8)]
            if do_sample:
                for h_ in range(8):
                    for nm, off in (("sq", 0), ("sk", 1024), ("sv", 2048)):
                        pl.append((nm, W[:, off + h_ * 128: off + (h_ + 1) * 128], 128))
            for h_ in range(8):
                for nm, off in (("ag", 3072), ("aq", 0), ("ak", 1024), ("av", 2048)):
                    pl.append((nm, W[:, off + h_ * 128: off + (h_ + 1) * 128], 128))
            for h_ in range(4):
                for nm, off in (("bo", 7168), ("bg", 8192), ("bq", 4096), ("bk", 5120), ("bv", 6144)):
                    for dc in range(2):
                        c0 = off + h_ * 256 + dc * 128
                        pl.append((nm, W[:, c0:c0 + 128], 128))
            for fb in range(16):
                pl.append(("o", d["woe"][e][:, fb * 128:(fb + 1) * 128], 128))
            return pl

        def plan_odd(o):
            W = d["wio"][o]
            pl = []
            for fb in range(16):
                for nm, off in (("cc", 2048), ("cx", 4096), ("cb", 0), ("cg", 6144)):
                    pl.append((nm, W[:, off + fb * 128: off + (fb + 1) * 128], 128))
            for fb in range(16):
                pl.append(("o", d["woo"][o][:, fb * 128:(fb + 1) * 128], 128))
            return pl

        for l in layers:
            plan += plan_even(l // 2) if l % 2 == 0 else plan_odd(l // 2)
        wstate = {"loaded": 0, "used": 0}

        def wprefetch(upto):
            while wstate["loaded"] < min(upto, len(plan)):
                i = wstate["loaded"]
                tag, ap, ncols = plan[i]
                sl = i % RING
                with nc.allow_non_contiguous_dma(reason="weight column block"):
                    k.dma("pool", ringd[sl],
                          lambda h, ap=ap, sl=sl, ncols=ncols: h.dma_start(out=ring[:, sl, :, 0:ncols], in_=ap.rearrange("(kc p) c -> p kc c", p=128)),
                          w=[ringb[sl]])
                wstate["loaded"] += 1

        def wnext(tag):
            i = wstate["used"]
            assert plan[i][0] == tag, (plan[i][0], tag, i)
            wprefetch(i + RING)
            wstate["used"] += 1
            sl = i % RING
            return ring[:, sl, :, :], ringb[sl], plan[i][2]

        def proj(tag, src, sts, cb):
            wap, wbuf, ncols = wnext(tag)
            for st in sts:
                srcT = st.xT if src == "x" else st.mT
                for gi, (g0, gs) in enumerate(st.groups):
                    if st is P:
                        pout, pbuf = P4[0:ncols, gi, 0:gs], p4b[gi]
                    else:
                        pout, pbuf = PB[0:ncols, 0:gs], pbb
                    for kc in range(16):
                        rb = [wbuf, st.xb[gi] if src == "x" else st.mb[kc]]
                        k.op("pe", lambda h, pout=pout, kc=kc, g0=g0, gs=gs, srcT=srcT: h.matmul(
                            out=pout, lhsT=wap[:, kc, 0:ncols], rhs=srcT[:, kc, g0:g0 + gs], start=(kc == 0), stop=(kc == 15)),
                            r=rb, w=[pbuf])
                    cb(st, gi, g0, gs, pout, pbuf)

        def act(out, in_, func, r, w, scale=1.0, bias=None, accum=None):
            kw = {}
            if bias is not None:
                kw["bias"] = bias
            if accum is not None:
                kw["accum_out"] = accum
            return k.op("act", lambda h: h.activation(out=out, in_=in_, func=func, scale=scale, **kw), r=r, w=w)

        def dve_tt(out, in0, in1, op, r, w):
            return k.op("dve", lambda h: h.tensor_tensor(out=out, in0=in0, in1=in1, op=op), r=r, w=w)

        def dve_ts(out, in0, s1, s2, op0, op1, r, w):
            if s2 is None:
                return k.op("dve", lambda h: h.tensor_scalar(out=out, in0=in0, scalar1=s1, scalar2=None, op0=op0), r=r, w=w)
            return k.op("dve", lambda h: h.tensor_scalar(out=out, in0=in0, scalar1=s1, scalar2=s2, op0=op0, op1=op1), r=r, w=w)

        def dve_stt(out, in0, sc, in1, op0, op1, r, w):
            return k.op("dve", lambda h: h.scalar_tensor_tensor(out=out, in0=in0, scalar=sc, in1=in1, op0=op0, op1=op1), r=r, w=w)

        def dve_cp(out, in_, r, w):
            return k.op("dve", lambda h: h.tensor_copy(out=out, in_=in_), r=r, w=w)

        def mm(out, lhsT, rhs, start, stop, r, w):
            return k.op("pe", lambda h: h.matmul(out=out, lhsT=lhsT, rhs=rhs, start=start, stop=stop), r=r, w=w)

        def tr(out, in_, ident, r, w):
            return k.op("pe", lambda h: h.transpose(out=out, in_=in_, identity=ident), r=r, w=w)

        def store(dst, src_psum, pbuf):
            si = stgi["i"] % 2
            stgi["i"] += 1
            np_, nf = src_psum.shape[0], src_psum.shape[1]
            so = stg[0:np_, si, 0:nf]
            dve_cp(so, src_psum, [pbuf], [stgb[si]])
            k.dma("sp", stgd[si], lambda h: h.dma_start(out=dst, in_=so), r=[stgb[si]])

        def out_ln(layer, last):
            with Phase() as ph:
                P.acc1 = ph.sb("acc1", [128, T])
                P.acc2 = ph.sb("acc2", [128, T])
                Sm.acc1 = ph.sb("acc1s", [128, S])
                Sm.acc2 = ph.sb("acc2s", [128, S])
                out_ln_(layer, last)

        def out_ln_(layer, last):
            for fb in range(16):
                def cb(st, gi, g0, gs, pin, pbuf, fb=fb):
                    xs_ = st.xT[:, fb, g0:g0 + gs]
                    dve_stt(xs_, xs_, ALPHA, pin, ALU.mult, ALU.add, [pbuf, st.xb[gi]], [st.xb[gi]])
                    a1 = st.acc1[:, g0:g0 + gs]
                    a2 = st.acc2[:, g0:g0 + gs]
                    wi = 3 if st is P else 2
                    sq = work[:, wi, 0:gs]
                    act(sq, xs_, AF.Square, [st.xb[gi]], [workb[wi]])
                    if fb == 0:
                        dve_cp(a1, xs_, [st.xb[gi]], [st.accb[gi]])
                        k.op("pool", lambda h: h.tensor_copy(out=a2, in_=sq), r=[workb[wi]], w=[st.acc2b[gi]])
                    else:
                        dve_tt(a1, a1, xs_, ALU.add, [st.xb[gi], st.accb[gi]], [st.accb[gi]])
                        k.op("pool", lambda h: h.tensor_tensor(out=a2, in0=a2, in1=sq, op=ALU.add), r=[workb[wi], st.acc2b[gi]], w=[st.acc2b[gi]])
                proj("o", "mix", streams, cb)
            for st in streams:
                for gi, (g0, gs) in enumerate(st.groups):
                    mm(PA[:, 0:gs], onesf[:], st.acc1[:, g0:g0 + gs], True, True, [st.accb[gi], cb_], [pab])
                    mm(PB[:, 0:gs], onesf[:], st.acc2[:, g0:g0 + gs], True, True, [st.acc2b[gi], cb_], [pbb])
                    meanb = work[:, 0, 0:gs]
                    rstdb = work[:, 1, 0:gs]
                    act(meanb, PA[:, 0:gs], AF.Copy, [pab], [workb[0]], scale=1.0 / D)
                    dve_tt(rstdb, meanb, meanb, ALU.mult, [workb[0]], [workb[1]])
                    dve_stt(rstdb, PB[:, 0:gs], 1.0 / D, rstdb, ALU.mult, ALU.subtract, [pbb, workb[1]], [workb[1]])
                    dve_ts(rstdb, rstdb, LN_EPS, None, ALU.add, None, [workb[1]], [workb[1]])
                    act(rstdb, rstdb, AF.Sqrt, [workb[1]], [workb[1]])
                    k.op("dve", lambda h: h.reciprocal(out=rstdb, in_=rstdb), r=[workb[1]], w=[workb[1]])
                    for fb in range(16):
                        xs_ = st.xT[:, fb, g0:g0 + gs]
                        tmp = work[:, 2, 0:gs]
                        dve_tt(tmp, xs_, meanb, ALU.subtract, [st.xb[gi], workb[0]], [workb[2]])
                        dve_tt(tmp, tmp, rstdb, ALU.mult, [workb[2], workb[1]], [workb[2]])
                        if not last:
                            act(xs_, tmp, AF.Identity, [workb[2], cb_], [st.xb[gi]], scale=lngT[:, layer, fb:fb + 1], bias=lnbT[:, layer, fb:fb + 1])
                        else:
                            si = stgi["i"] % 2
                            stgi["i"] += 1
                            so = stg[:, si, 0:gs]
                            act(so, tmp, AF.Identity, [workb[2], cb_], [stgb[si]], scale=lngT[:, layer, fb:fb + 1], bias=lnbT[:, layer, fb:fb + 1])
                            dst = d["ypT" if st is P else "ysT"][fb * 128:(fb + 1) * 128, g0:g0 + gs]
                            with nc.allow_non_contiguous_dma(reason="small"):
                                k.dma("sp", stgd[si], lambda h, so=so, dst=dst: h.dma_start(out=dst, in_=so), r=[stgb[si]])

        def odd_layer(o):
            with Phase() as ph:
                Uc = {P: ph.sb("Ucp", [128, T]), Sm: ph.sb("Ucs", [128, S])}
                U = {P: ph.sb("Up", [128, T + 2]), Sm: ph.sb("Us", [128, S + 2])}
                Y = {P: ph.sb("Yp", [128, T]), Sm: ph.sb("Ys", [128, S])}
                ucb = {P: Buf("ucp"), Sm: Buf("ucs")}
                ub = {P: Buf("up"), Sm: Buf("us")}
                yb = {P: Buf("yp"), Sm: Buf("ys")}
                for fb in range(16):
                    proj("cc", "x", streams, lambda st, gi, g0, gs, pin, pbuf: act(Uc[st][:, g0:g0 + gs], pin, AF.Copy, [pbuf], [ucb[st]]))
                    for st in streams:
                        if st is P:
                            k.op("dve", lambda h: h.memset(U[P][:, 0:2], 0.0), w=[ub[P]])
                        else:
                            k.dma("sp", dmisc, lambda h, fb=fb: h.dma_start(out=U[Sm][:, 0:2], in_=d["sconv"][o, :, fb, :]), w=[ub[Sm]])
                    proj("cx", "x", streams, lambda st, gi, g0, gs, pin, pbuf: dve_tt(U[st][:, 2 + g0:2 + g0 + gs], pin, Uc[st][:, g0:g0 + gs], ALU.mult, [pbuf, ucb[st]], [ub[st]]))
                    for st in streams:
                        n = st.n
                        dve_ts(Y[st][:, :], U[st][:, 2:2 + n], cwT[:, o, fb, 2:3], None, ALU.mult, None, [ub[st], cb_], [yb[st]])
                        dve_stt(Y[st][:, :], U[st][:, 1:1 + n], cwT[:, o, fb, 1:2], Y[st][:, :], ALU.mult, ALU.add, [ub[st], yb[st], cb_], [yb[st]])
                        dve_stt(Y[st][:, :], U[st][:, 0:n], cwT[:, o, fb, 0:1], Y[st][:, :], ALU.mult, ALU.add, [ub[st], yb[st], cb_], [yb[st]])
                        dst = d["pbT" if st is P else "sbT"][o, :, fb, :]
                        with nc.allow_non_contiguous_dma(reason="tiny"):
                            k.dma("sp", dmisc, lambda h, dst=dst, st=st, n=n: h.dma_start(out=dst, in_=U[st][:, n:n + 2]), r=[ub[st]])
                    proj("cb", "x", streams, lambda st, gi, g0, gs, pin, pbuf: dve_tt(Y[st][:, g0:g0 + gs], pin, Y[st][:, g0:g0 + gs], ALU.mult, [pbuf, yb[st]], [yb[st]]))

                    def cbg(st, gi, g0, gs, pin, pbuf, fb=fb):
                        wi = 0 if st is P else 1
                        act(work[:, wi, 0:gs], pin, AF.Silu, [pbuf], [workb[wi]])
                        dve_tt(st.mT[:, fb, g0:g0 + gs], work[:, wi, 0:gs], Y[st][:, g0:g0 + gs], ALU.mult, [workb[wi], yb[st]], [st.mb[fb]])
                    proj("cg", "x", streams, cbg)

        def gate_math(ph, st, e):
            L, nch = st.L, st.nch
            n64 = nch * 4
            gb = st.gateb
            tA = ph.sb("gtA", [128, 6, 64])
            tB = ph.sb("gtB", [64, 2])
            R = ph.sb("gtR", [1, 8, 64])
            tb = Buf("gt")
            G3 = PA[0:L, 0:nch * 8].rearrange("p (c g) -> p c g", g=8)

            def t3(i):
                return tA[0:L, i, 0:n64].rearrange("p (c h) -> p c h", h=4)

            def t2(i):
                return tA[0:L, i, 0:n64]
            bi = gbrep[0:L, e, 0:4].unsqueeze(1).broadcast_to([L, nch, 4])
            bf_ = gbrep[0:L, e, 4:8].unsqueeze(1).broadcast_to([L, nch, 4])
            dve_tt(t3(0), G3[:, :, 0:4], bi, ALU.add, [pab, cb_], [tb])
            dve_tt(t3(1), G3[:, :, 4:8], bf_, ALU.add, [pab, cb_], [tb])
            act(t2(1), t2(1), AF.Exp, [tb], [tb], scale=-1.0)
            act(t2(1), t2(1), AF.Ln, [tb], [tb], bias=1.0)
            mm(PB[0:L, 0:n64], trilT[0:L, 0:L], t2(1), True, True, [tb, cb_], [pbb])
            dve_cp(t2(2), PB[0:L, 0:n64], [pbb], [tb])
            dve_tt(t2(3), t2(0), t2(2), ALU.add, [tb], [tb])
            tr(PA[0:n64, 0:L], t2(3), identf[0:L, 0:L], [tb, cb_], [pab])
            tr(PB[0:n64, 0:L], t2(2), identf[0:L, 0:L], [tb, cb_], [pbb])
            k.op("dve", lambda h: h.tensor_reduce(out=tB[0:n64, 0:1], in_=PA[0:n64, 0:L], axis=AX.X, op=ALU.max), r=[pab], w=[tb])
            dve_cp(tB[0:n64, 1:2], PB[0:n64, L - 1:L], [pbb], [tb])
            tr(PA[0:1, 0:n64], tB[0:n64, 0:1], identf[0:n64, 0:n64], [tb, cb_], [pab])
            tr(PA[0:1, 64:64 + n64], tB[0:n64, 1:2], identf[0:n64, 0:n64], [tb, cb_], [pab])
            dve_cp(R[0:1, 0, 0:n64], PA[0:1, 0:n64], [pab], [tb])
            dve_cp(R[0:1, 1, 0:n64], PA[0:1, 64:64 + n64], [pab], [tb])
            dve_ts(R[0:1, 2, 0:n64], R[0:1, 1, 0:n64], -1.0, None, ALU.mult, None, [tb], [tb])
            if st is Sm:
                k.dma("sp", dmisc, lambda h: h.dma_start(out=R[0:1, 7, 0:4], in_=d["sm"][e]), w=[tb])
            else:
                k.op("dve", lambda h: h.memset(R[0:1, 7, 0:4], 0.0), w=[tb])

            def rv(i):
                return R[0:1, i, 0:n64].rearrange("p (c h) -> p c h", h=4)
            for hh in range(4):
                k.op("dve", lambda h, hh=hh: h.tensor_tensor_scan(out=rv(3)[:, :, hh], data0=rv(0)[:, :, hh], data1=rv(2)[:, :, hh],
                                                                initial=R[0:1, 7, hh:hh + 1], op0=ALU.max, op1=ALU.add), r=[tb], w=[tb])
            dve_tt(R[0:1, 4, 0:n64], R[0:1, 3, 0:n64], R[0:1, 1, 0:n64], ALU.add, [tb], [tb])
            dve_cp(rv(5)[:, 0, :], R[0:1, 7, 0:4], [tb], [tb])
            if nch > 1:
                dve_cp(rv(5)[:, 1:nch, :], rv(3)[:, 0:nch - 1, :], [tb], [tb])
            dve_tt(R[0:1, 6, 0:n64], R[0:1, 5, 0:n64], R[0:1, 4, 0:n64], ALU.subtract, [tb], [tb])
            act(R[0:1, 6, 0:n64], R[0:1, 6, 0:n64], AF.Exp, [tb], [tb])
            mm(PA[0:128, 0:n64], onesf[0:1, 0:128], R[0:1, 6, 0:n64], True, True, [tb, cb_], [pab])
            dve_cp(st.gom[:, 0:n64], PA[0:128, 0:n64], [pab], [gb])
            mm(PB[0:L, 0:n64], onesf[0:1, 0:L], R[0:1, 4, 0:n64], True, True, [tb, cb_], [pbb])
            dve_tt(t2(4), t2(3), PB[0:L, 0:n64], ALU.subtract, [tb, pbb], [tb])
            act(st.gw[0:L, 0:n64], t2(4), AF.Exp, [tb], [gb], bias=math.log(1.0 / 16.0))
            dve_tt(t2(5), t2(2), PB[0:L, 0:n64], ALU.subtract, [tb, pbb], [tb])
            act(st.gf[0:L, 0:n64], t2(5), AF.Exp, [tb], [gb])
            dst = d["pm" if st is P else "smo"][e]
            k.dma("sp", dmisc, lambda h: h.dma_start(out=dst, in_=rv(3)[:, nch - 1, :]), r=[tb])

        def gates_phase(e):
            with Phase() as ph:
                wap, wbuf, _ = wnext("gif")
                for st in streams:
                    L, nch = st.L, st.nch
                    for c in range(nch):
                        gi = (c * L) // 512
                        for kc in range(16):
                            mm(PA[0:L, c * 8:(c + 1) * 8], st.xT[:, kc, c * L:(c + 1) * L], wap[:, kc, 0:8], kc == 0, kc == 15, [wbuf, st.xb[gi]], [pab])
                    gate_math(ph, st, e)

        def mlstm_head(ph, st, e, hd, q2, k2, v2, qb, tl):
            L, nch = st.L, st.nch
            C32, Cb, ktl, vaug2, Stt, hn, sm_ = tl["C32"], tl["Cb"], tl["ktl"], tl["vaug2"], tl["St"], tl["hn"], tl["sm"]
            cbuf, tb = tl["cbuf"], tl["tb"]
            if st is P:
                k.op("dve", lambda h: h.memset(C32[:], 0.0), w=[cbuf])
            else:
                for dc in range(2):
                    k.dma("sp", dmisc, lambda h, dc=dc: h.dma_start(out=C32[:, dc, 0:256], in_=d["sC"][e, hd, dc * 128:(dc + 1) * 128, :]), w=[cbuf])
                    with nc.allow_non_contiguous_dma(reason="tiny"):
                        k.dma("sp", dmisc, lambda h, dc=dc: h.dma_start(out=C32[:, dc, 256:257], in_=d["sn"][e, hd, dc * 128:(dc + 1) * 128, :]), w=[cbuf])
            for c in range(nch):
                cs = slice(c * L, (c + 1) * L)
                col = c * 4 + hd
                om = st.gom[:, col:col + 1]
                wv = st.gw[0:L, col:col + 1]
                fv = st.gf[0:L, col:col + 1]
                dve_ts(Cb[:], C32[:], om, None, ALU.mult, None, [cbuf, st.gateb], [tb])
                for dc in range(2):
                    tr(PT[0:L, dc, :], k2[:, dc, cs], identb[:], [qb], [ptb])
                    tr(PT2[0:L, dc, :], v2[:, dc, cs], identb[:], [qb], [pt2b])
                act(ktl[0:L, :], PT[0:L, 0:2, :].rearrange("p a b -> p (a b)"), AF.Copy, [ptb, st.gateb], [tb], scale=wv)
                act(vaug2[0:L, 0:256], PT2[0:L, 0:2, :].rearrange("p a b -> p (a b)"), AF.Copy, [pt2b], [tb])
                for dc in range(2):
                    mm(PA[0:L, 0:L], k2[:, dc, cs], q2[:, dc, cs], dc == 0, dc == 1, [qb], [pab])
                dve_stt(Stt[0:L, 0:L], PA[0:L, 0:L], wv, trilT[0:L, 0:L], ALU.mult, ALU.mult, [pab, st.gateb, cb_], [tb])
                mm(PB[0:L, 0:257], Stt[0:L, 0:L], vaug2[0:L, 0:257], True, False, [tb], [pbb])
                for dc in range(2):
                    mm(PB[0:L, 0:257], q2[:, dc, cs], Cb[:, dc, :], False, dc == 1, [qb, tb], [pbb])
                act(sm_[0:L, 0:1], PB[0:L, 256:257], AF.Abs, [pbb], [tb])
                dve_tt(sm_[0:L, 0:1], sm_[0:L, 0:1], fv, ALU.max, [tb, st.gateb], [tb])
                k.op("dve", lambda h: h.bn_stats(out=sm_[0:L, 2:8], in_=PB[0:L, 0:256]), r=[pbb], w=[tb])
                k.op("dve", lambda h: h.bn_aggr(out=sm_[0:L, 8:10], in_=sm_[0:L, 2:8]), r=[tb], w=[tb])
                dve_tt(sm_[0:L, 1:2], sm_[0:L, 0:1], sm_[0:L, 0:1], ALU.mult, [tb], [tb])
                dve_stt(sm_[0:L, 1:2], sm_[0:L, 1:2], LN_EPS, sm_[0:L, 9:10], ALU.mult, ALU.add, [tb], [tb])
                act(sm_[0:L, 1:2], sm_[0:L, 1:2], AF.Sqrt, [tb], [tb])
                k.op("dve", lambda h: h.reciprocal(out=sm_[0:L, 1:2], in_=sm_[0:L, 1:2]), r=[tb], w=[tb])
                dve_ts(hn[0:L, :], PB[0:L, 0:256], sm_[0:L, 8:9], sm_[0:L, 1:2], ALU.subtract, ALU.mult, [pbb, tb], [tb])
                for dc in range(2):
                    tr(PT[:, 4 + dc, 0:L], hn[0:L, dc * 128:(dc + 1) * 128], identb[0:L, 0:L], [tb], [ptb])
                    fbi = 8 + 2 * hd + dc
                    ms = st.mT[:, fbi, cs]
                    dve_stt(ms, PT[:, 4 + dc, 0:L], ngT[:, e, 2 * hd + dc:2 * hd + dc + 1], ms, ALU.mult, ALU.mult, [ptb, cb_, st.mb[fbi]], [st.mb[fbi]])
                for dc in range(2):
                    mm(P4[:, dc, 0:257], ktl[0:L, dc * 128:(dc + 1) * 128], vaug2[0:L, 0:257], True, True, [tb], [p4b[dc]])
                    dve_stt(C32[:, dc, :], C32[:, dc, :], om, P4[:, dc, 0:257], ALU.mult, ALU.add, [cbuf, p4b[dc], st.gateb], [cbuf])
            dst = d["pCa" if st is P else "sCa"][e, hd]
            k.dma("sp", dmisc, lambda h: h.dma_start(out=dst, in_=C32[:]), r=[cbuf])

        def b_phase(e):
            with Phase() as ph:
                q2 = {P: ph.sb("q2p", [128, 2, T], BF16), Sm: ph.sb("q2s", [128, 2, S], BF16)}
                k2 = {P: ph.sb("k2p", [128, 2, T], BF16), Sm: ph.sb("k2s", [128, 2, S], BF16)}
                v2 = {P: ph.sb("v2p", [128, 2, T], BF16), Sm: ph.sb("v2s", [128, 2, S], BF16)}
                qb = {P: Buf("qkv2p"), Sm: Buf("qkv2s")}
                tl = {"C32": ph.sb("C32", [128, 2, 257]), "Cb": ph.sb("Cb", [128, 2, 257], BF16), "ktl": ph.sb("ktl", [128, 256], BF16),
                      "vaug2": ph.sb("vaug2", [128, 260], BF16), "St": ph.sb("St", [128, 128], BF16), "hn": ph.sb("hn", [128, 256], BF16),
                      "sm": ph.sb("sm", [128, 16]), "cbuf": Buf("C32"), "tb": Buf("mlt")}
                k.op("dve", lambda h: h.memset(tl["vaug2"][:, 256:257], 1.0), w=[tl["tb"]])
                for hd in range(4):
                    for dc in range(2):
                        fbi = 8 + 2 * hd + dc
                        proj("bo", "x", streams, lambda st, gi, g0, gs, pin, pbuf, fbi=fbi: act(st.mT[:, fbi, g0:g0 + gs], pin, AF.Sigmoid, [pbuf], [st.mb[fbi]]))
                    for dc in range(2):
                        fbi = 8 + 2 * hd + dc

                        def cbg(st, gi, g0, gs, pin, pbuf, fbi=fbi):
                            wi = 0 if st is P else 1
                            act(work[:, wi, 0:gs], pin, AF.Silu, [pbuf], [workb[wi]])
                            ms = st.mT[:, fbi, g0:g0 + gs]
                            dve_tt(ms, ms, work[:, wi, 0:gs], ALU.mult, [workb[wi], st.mb[fbi]], [st.mb[fbi]])
                        proj("bg", "x", streams, cbg)
                    for nm, dst_ in (("bq", q2), ("bk", k2), ("bv", v2)):
                        for dc in range(2):
                            proj(nm, "x", streams, lambda st, gi, g0, gs, pin, pbuf, dst_=dst_, dc=dc: act(dst_[st][:, dc, g0:g0 + gs], pin, AF.Copy, [pbuf], [qb[st]]))
                    for st in streams:
                        mlstm_head(ph, st, e, hd, q2[st], k2[st], v2[st], qb[st], tl)

        def a_phase(e):
            scale = 128.0 ** -0.5
            with Phase() as ph:
                qT = ph.sb("qT", [128, T], BF16)
                kT = ph.sb("kT", [128, T], BF16)
                vp = ph.sb("vp", [128, T], BF16)
                PTs = vp[:].rearrange("p (a b) -> p a b", b=128)
                vaug = ph.sb("vaug", [128, 16, 132], BF16)
                Pm = ph.sb("Pm", [128, T], BF16)
                ksum = ph.sb("ksum", [128, 8])
                kmhl = ph.sb("kmhl", [128, 2, 8], BF16)
                Gs = ph.sb("Gs", [128, 16, 8])
                sel = ph.sb("sel", [128, 16, 8])
                sm_ = ph.sb("sma", [128, 32])
                oh = ph.sb("oh", [128, 128], BF16)
                qb_, kb_, vb_, vab, pmb, gb_, tb = Buf("qT"), Buf("kT"), Buf("vp"), Buf("vaug"), Buf("Pm"), Buf("Gs"), Buf("smA")
                k.op("dve", lambda h: h.memset(vaug[:, :, 128:129], 1.0), w=[vab])
                asub = cfg.get("a_sub", 99)
                for hd in range(cfg.get("a_heads", 8)):
                    def cbg(st, gi, g0, gs, pin, pbuf, hd=hd):
                        if st is P:
                            act(st.mT[:, hd, g0:g0 + gs], pin, AF.Silu, [pbuf], [st.mb[hd]])
                        else:
                            act(work[:, 1, 0:gs], pin, AF.Silu, [pbuf], [workb[1]])
                            dve_tt(st.mT[:, hd, g0:g0 + gs], work[:, 1, 0:gs], attnS[:, hd, :], ALU.mult, [workb[1], attnSb], [st.mb[hd]])
                    proj("ag", "x", streams, cbg)
                    if asub < 1:
                        break
                    proj("aq", "x", [P], lambda st, gi, g0, gs, pin, pbuf: act(qT[:, g0:g0 + gs], pin, AF.Copy, [pbuf], [qb_], scale=scale))
                    if asub < 2:
                        break

                    def cbk(st, gi, g0, gs, pin, pbuf, hd=hd):
                        if cfg.get("kskip") not in ("store", "both"):
                            store(d["pkT"][e, hd * 128:(hd + 1) * 128, g0:g0 + gs], pin, pbuf)
                        for hf in range(2):
                            act(kT[:, g0 + hf * 256:g0 + (hf + 1) * 256], pin[:, hf * 256:(hf + 1) * 256], AF.Copy, [pbuf], [kb_, gb_], accum=ksum[:, 2 * gi + hf:2 * gi + hf + 1])
                    proj("ak", "x", [P], cbk)
                    if asub < 3:
                        break

                    def cbv(st, gi, g0, gs, pin, pbuf, hd=hd):
                        store(d["pvT"][e, hd * 128:(hd + 1) * 128, g0:g0 + gs], pin, pbuf)
                        act(vp[:, g0:g0 + gs], pin, AF.Copy, [pbuf], [vb_])
                    proj("av", "x", [P], cbv)
                    if asub < 4:
                        break
                    for half in range(2):
                        for j in range(8):
                            ti = half * 8 + j
                            tr(PT[:, j, :], vp[:, ti * 128:(ti + 1) * 128], identb[:], [vb_], [ptb])
                        act(vaug[:, half * 8:(half + 1) * 8, 0:128], PT[:, :, :], AF.Copy, [ptb], [vab])
                    if cfg.get("a_stop") == "proj":
                        continue
                    dve_cp(kmhl[:, 0, :], ksum[:], [gb_], [gb_])
                    dve_tt(kmhl[:, 1, :], ksum[:], kmhl[:, 0, :], ALU.subtract, [gb_], [gb_])
                    for i in range(8, 16):
                        mm(PA[:, i * 8:(i + 1) * 8], qT[:, i * 128:(i + 1) * 128], kmhl[:, 0, :], True, False, [qb_, gb_], [pab])
                        mm(PA[:, i * 8:(i + 1) * 8], qT[:, i * 128:(i + 1) * 128], kmhl[:, 1, :], False, True, [qb_, gb_], [pab])
                    dve_tt(Gs[:, 8:16, :], PA[:, 64:128].rearrange("p (a b) -> p a b", b=8), gmask[:, 8:16, :], ALU.add, [pab, cb_], [gb_])
                    for i in range(8, 16):
                        k.op("dve", lambda h, i=i: h.max(out=sm_[:, 8:16], in_=Gs[:, i, :]), r=[gb_], w=[tb])
                        dve_ts(sel[:, i, :], Gs[:, i, :], sm_[:, 10:11], None, ALU.is_ge, None, [gb_, tb], [gb_])
                    if cfg.get("a_stop") == "gate":
                        continue
                    for i in range(cfg.get("a_tiles", 16)):
                        j = i // 2
                        nk = (i + 1) * 128
                        t0 = i * 128
                        qs_ = qT[:, t0:t0 + 128]
                        nb_ = (nk + 511) // 512
                        for c in range(nb_):
                            w_ = min(512, nk - c * 512)
                            mm(P4[:, c, 0:w_], qs_, kT[:, c * 512:c * 512 + w_], True, True, [qb_, kb_], [p4b[c]])
                        banks = p4b[0:nb_]
                        if i == 0:
                            dve_tt(P4f[:, 0:128], P4f[:, 0:128], BW[:, hd, 128:256], ALU.add, banks + [cb_], banks)
                        else:
                            dve_tt(P4f[:, t0 - 128:t0], P4f[:, t0 - 128:t0], BW[:, hd, 0:128], ALU.add, banks + [cb_], banks)
                            dve_tt(P4f[:, t0:t0 + 128], P4f[:, t0:t0 + 128], BW[:, hd, 128:256], ALU.add, banks + [cb_], banks)
                        k.op("dve", lambda h, nk=nk: h.tensor_reduce(out=sm_[:, 0:1], in_=P4f[:, 0:nk], axis=AX.X, op=ALU.max, negate=True), r=banks, w=[tb])
                        if j < 4:
                            act(Pm[:, 0:nk], P4f[:, 0:nk], AF.Exp, banks + [tb], [pmb], bias=sm_[:, 0:1])
                        else:
                            dve_ts(sm_[:, 1:2], sm_[:, 0:1], NEG, None, ALU.add, None, [tb], [tb])
                            dve_ts(sm_[:, 16:16 + j], sel[:, i, 0:j], -NEG, sm_[:, 1:2], ALU.mult, ALU.add, [gb_, tb], [tb])
                            for n in range(j + 1):
                                c0, c1 = n * 256, min(n * 256 + 256, nk)
                                bcol = sm_[:, 16 + n:17 + n] if n < j else sm_[:, 0:1]
                                act(Pm[:, c0:c1], P4f[:, c0:c1], AF.Exp, banks + [tb], [pmb], bias=bcol)
                        for kt in range(i + 1):
                            tr(PT[:, kt % 8, :], Pm[:, kt * 128:(kt + 1) * 128], identb[:], [pmb], [ptb])
                            if kt % 8 == 7 or kt == i:
                                k0 = (kt // 8) * 8
                                act(PTs[:, k0:kt + 1, :], PT[:, 0:kt + 1 - k0, :], AF.Copy, [ptb], [vb_])
                        for kt in range(i + 1):
                            mm(PB[:, 0:129], PTs[:, kt, :], vaug[:, kt, 0:129], kt == 0, kt == i, [vb_, vab], [pbb])
                        k.op("dve", lambda h: h.reciprocal(out=sm_[:, 2:3], in_=PB[:, 128:129]), r=[pbb], w=[tb])
                        dve_ts(oh[:], PB[:, 0:128], sm_[:, 2:3], None, ALU.mult, None, [pbb, tb], [tb])
                        tr(PT2[:, 0, :], oh[:], identb[:], [tb], [pt2b])
                        ms = P.mT[:, hd, t0:t0 + 128]
                        dve_tt(ms, PT2[:, 0, :], ms, ALU.mult, [pt2b, P.mb[hd]], [P.mb[hd]])

        def sample_moba(e):
            scale = 128.0 ** -0.5
            for hd in range(8):
                proj("sq", "x", [Sm], lambda st, gi, g0, gs, pin, pbuf, hd=hd: act(qs[:, hd, :], pin, AF.Copy, [pbuf], [qkvb], scale=scale))

                def cbk(st, gi, g0, gs, pin, pbuf, hd=hd):
                    with nc.allow_non_contiguous_dma(reason="tiny"):
                        store(d["skT"][e, hd * 128:(hd + 1) * 128, :], pin, pbuf)
                    act(ks[:, hd, :], pin, AF.Copy, [pbuf], [qkvb])
                proj("sk", "x", [Sm], cbk)

                def cbv(st, gi, g0, gs, pin, pbuf, hd=hd):
                    with nc.allow_non_contiguous_dma(reason="tiny"):
                        store(d["svT"][e, hd * 128:(hd + 1) * 128, :], pin, pbuf)
                    act(vs[:, hd, :], pin, AF.Copy, [pbuf], [qkvb])
                proj("sv", "x", [Sm], cbv)
            if not do_moba_s:
                k.op("dve", lambda h: h.memset(attnS[:], 0.0), w=[attnSb])
                return
            with Phase() as ph:
                Sall = big[0:64, :]
                Pall = big[0:64, :].bitcast(BF16)
                sab = Buf("Sall")
                Kpg = ph.sb("Kpg", [128, 4, 1024])
                kpb = [Buf(f"kpg{i}") for i in range(4)]
                kpd = [k.dsem(f"d_kpg{e}_{i}") for i in range(4)]
                Vpg = Kpg[:, 2:4, :].rearrange("p a b -> p (a b)").bitcast(BF16).rearrange("p (a b) -> p a b", b=1024)
                vpb = [kpb[2], kpb[2], kpb[3], kpb[3]]
                vpb = [Buf(f"vpg{i}") for i in range(4)]
                vpd = kpd
                KT = ph.sb("KT", [128, 2, 256], BF16)
                ktb = [Buf("KT0"), Buf("KT1")]
                QP = ph.sb("QP", [128, 8, 64], BF16)
                ksS = ph.sb("ksS", [128, 8, 64])
                kmS = ph.sb("kmS", [128, 8, 2, 64], BF16)
                ptf = ph.sb("ptf", [128, 128])
                pti = ph.sb("pti", [128, 128], I32)
                idx = ph.sb("idx", [128, 128], I32)
                gS = ph.sb("gS", [64, 64])
                selS = ph.sb("selS", [64, 64])
                bcS = ph.sb("bcS", [64, 64])
                b63 = ph.sb("b63", [64, 256])
                bown = ph.sb("bown", [64, 8])
                cms = ph.sb("cms", [64, 8])
                c31s = ph.sb("c31s", [64, 1])
                bdg = Kpg[0:64, 1, :].rearrange("p (h d) -> p h d", d=128)
                Sown = ph.sb("Sown", [64, 8])
                Pown = ph.sb("Pown", [64, 8], BF16)
                PownT = ph.sb("PownT", [8, 64], BF16)
                dens = ph.sb("dens", [64, 66])
                smS = ph.sb("smS", [64, 16])
                PTp = ph.sb("PTp", [128, 2, 64], BF16)
                ptpb = [Buf("ptp0"), Buf("ptp1")]
                vstm = ph.sb("vstm", [8, 1024], BF16)
                Of = Kpg[0:64, 0, :].rearrange("p (h d) -> p h d", d=128)
                Od = ph.sb("Od", [64, 128])
                Odb = ph.sb("Odb", [64, 128], BF16)
                tb = Buf("smS")
                with nc.allow_non_contiguous_dma(reason="bcast"):
                    k.dma("sp", dmisc, lambda h: h.dma_start(out=pti[:], in_=d["pt"][0:1, :].partition_broadcast(128)), w=[tb])
                    k.dma("sp", dmisc, lambda h: h.dma_start(out=b63[:], in_=d["rels63"]), w=[tb])
                    k.dma("sp", dmisc, lambda h: h.dma_start(out=bown[:], in_=d["relso"]), w=[tb])
                    k.dma("sp", dmisc, lambda h: h.dma_start(out=cms[:], in_=d["c_cmasks"]), w=[tb])
                    k.dma("sp", dmisc, lambda h: h.dma_start(out=c31s[:], in_=d["c31s"]), w=[tb])
                dve_cp(ptf[:], pti[:], [tb], [tb])
                dve_ts(ptf[:], ptf[:], float(e * npool), 128.0, ALU.add, ALU.mult, [tb], [tb])
                dve_ts(ptf[:], ptf[:], iota[:, 0:1], None, ALU.add, None, [tb, cb_], [tb])
                dve_cp(idx[:], ptf[:], [tb], [tb])
                dve_ts(b63[:], b63[:], c31s[:, 0:1], None, ALU.subtract, None, [tb], [tb])
                dve_stt(bown[:], bown[:], c31s[:, 0:1], cms[:], ALU.subtract, ALU.add, [tb], [tb])
                k.op("dve", lambda h: h.memset(QP[:], 0.0), w=[tb])
                for hd in range(8):
                    dve_cp(QP[:, hd, hd * 8:(hd + 1) * 8], qs[:, hd, :], [qkvb, tb], [tb])
                idxb = Buf("idx")
                ksb = Buf("ksS")
                dve_cp(idx[:, 0:1], idx[:, 0:1], [tb], [idxb])

                def kload(n):
                    for half in range(2):
                        pg = 2 * n + half
                        bi_ = (n % 2) * 2 + half
                        k.dma("pool", kpd[bi_], lambda h, bi_=bi_, pg=pg: h.indirect_dma_start(
                            out=Kpg[:, bi_, :], out_offset=None, in_=d["ck"],
                            in_offset=bass.IndirectOffsetOnAxis(ap=idx[:, pg:pg + 1], axis=0)), r=[idxb], w=[kpb[bi_]])
                kload(0)
                for n in range(64):
                    if n + 1 < 64:
                        kload(n + 1)
                    for hd in range(8):
                        bk_ = hd % 4
                        for half in range(2):
                            bi_ = (n % 2) * 2 + half
                            tr(P4[:, bk_, half * 128:(half + 1) * 128], Kpg[:, bi_, hd * 128:(hd + 1) * 128], identf[:], [kpb[bi_], cb_], [p4b[bk_]])
                        act(KT[:, hd % 2, :], P4[:, bk_, 0:256], AF.Copy, [p4b[bk_]], [ktb[hd % 2], ksb], accum=ksS[:, hd, n:n + 1])
                        mm(PA[0:64, 0:256], QP[:, hd, :], KT[:, hd % 2, :], hd == 0, hd == 7, [tb, ktb[hd % 2]], [pab])
                    dve_cp(Sall[:, n * 256:(n + 1) * 256], PA[0:64, 0:256], [pab], [sab])
                dve_cp(ksS[:, 0, 0:1], ksS[:, 0, 0:1], [ksb, tb], [tb])
                for hd in range(8):
                    dve_cp(kmS[:, hd, 0, :], ksS[:, hd, :], [tb], [tb])
                    dve_tt(kmS[:, hd, 1, :], ksS[:, hd, :], kmS[:, hd, 0, :], ALU.subtract, [tb], [tb])
                for hd in range(8):
                    mm(PB[0:64, 0:64], QP[:, hd, :], kmS[:, hd, 0, :], hd == 0, False, [tb], [pbb])
                    mm(PB[0:64, 0:64], QP[:, hd, :], kmS[:, hd, 1, :], False, hd == 7, [tb], [pbb])
                dve_cp(gS[:], PB[0:64, 0:64], [pbb], [tb])
                k.op("dve", lambda h: h.max(out=smS[:, 8:16], in_=gS[:]), r=[tb], w=[tb])
                dve_ts(selS[:], gS[:], smS[:, 10:11], None, ALU.is_ge, None, [tb], [tb])
                for hd in range(8):
                    mm(PB[0:64, 64:72], QP[:, hd, :], ks[:, hd, :], hd == 0, hd == 7, [tb, qkvb], [pbb])
                dve_tt(Sown[:], PB[0:64, 64:72], bown[:], ALU.add, [pbb, tb], [tb])
                dve_tt(Sall[:, 63 * 256:64 * 256], Sall[:, 63 * 256:64 * 256], b63[:], ALU.add, [sab, tb], [sab])
                k.op("dve", lambda h: h.tensor_reduce(out=smS[:, 0:1], in_=Sall[:, :], axis=AX.X, op=ALU.max), r=[sab], w=[tb])
                k.op("dve", lambda h: h.tensor_reduce(out=smS[:, 1:2], in_=Sown[:], axis=AX.X, op=ALU.max), r=[tb], w=[tb])
                dve_tt(smS[:, 0:1], smS[:, 0:1], smS[:, 1:2], ALU.max, [tb], [tb])
                dve_ts(smS[:, 0:1], smS[:, 0:1], -1.0, None, ALU.mult, None, [tb], [tb])
                dve_ts(smS[:, 1:2], smS[:, 0:1], NEG, None, ALU.add, None, [tb], [tb])
                dve_ts(bcS[:], selS[:], -NEG, smS[:, 1:2], ALU.mult, ALU.add, [tb], [tb])
                k.op("dve", lambda h: h.memset(dens[:], 0.0), w=[tb])
                for n in range(64):
                    act(Pall[:, n * 256:(n + 1) * 256], Sall[:, n * 256:(n + 1) * 256], AF.Exp, [sab, tb], [sab], bias=bcS[:, n:n + 1], accum=dens[:, n:n + 1])
                act(Pown[:], Sown[:], AF.Exp, [tb], [tb], bias=smS[:, 0:1], accum=dens[:, 64:65])
                k.op("dve", lambda h: h.tensor_reduce(out=smS[:, 2:3], in_=dens[:, 0:65], axis=AX.X, op=ALU.add), r=[tb], w=[tb])
                k.op("dve", lambda h: h.reciprocal(out=smS[:, 3:4], in_=smS[:, 2:3]), r=[tb], w=[tb])
                k.op("dve", lambda h: h.memset(smS[:, 15:16], 0.0), w=[kpb[2], kpb[3]] + vpb)

                def vload(pg):
                    b4 = pg % 4
                    k.dma("pool", vpd[b4], lambda h, b4=b4, pg=pg: h.indirect_dma_start(
                        out=Vpg[:, b4, :], out_offset=None, in_=d["cv"],
                        in_offset=bass.IndirectOffsetOnAxis(ap=idx[:, pg:pg + 1], axis=0)), r=[idxb], w=[vpb[b4]])
                for pg in range(3):
                    vload(pg)
                for pg in range(128):
                    b2 = pg % 2
                    b4 = pg % 4
                    if pg + 3 < 128:
                        vload(pg + 3)
                    tr(PT[:, b2, 0:64], Pall[:, pg * 128:(pg + 1) * 128], identb[0:64, 0:64], [sab, cb_], [ptb if b2 == 0 else pt2b])
                    act(PTp[:, b2, :], PT[:, b2, 0:64], AF.Copy, [ptb if b2 == 0 else pt2b], [ptpb[b2]])
                    for c in range(2):
                        mm(P4[0:64, c, :], PTp[:, b2, :], Vpg[:, b4, c * 512:(c + 1) * 512], pg == 0, False, [ptpb[b2], vpb[b4]], [p4b[c]])
                k.dma("sp", dmisc, lambda h: h.dma_start(out=bdg, in_=d["c_bdiag"]), r=[kpb[1]], w=[kpb[1]])
                tr(PT2[0:8, 0, 0:64], Pown[:], identb[0:64, 0:64], [tb, cb_], [pt2b])
                act(PownT[:], PT2[0:8, 0, 0:64], AF.Copy, [pt2b], [tb])
                for hd in range(8):
                    tr(PT[0:8, hd, :], vs[:, hd, :], identb[:], [qkvb, cb_], [ptb])
                act(vstm[:], PT[0:8, :, :].rearrange("p a b -> p (a b)"), AF.Copy, [ptb], [tb])
                for c in range(2):
                    mm(P4[0:64, c, :], PownT[:], vstm[:, c * 512:(c + 1) * 512], False, True, [tb], [p4b[c]])
                dve_tt(Of, P4[0:64, 0:2, :].rearrange("p a (b c) -> p (a b) c", c=128), bdg, ALU.mult, [p4b[0], p4b[1], tb, kpb[1], kpb[0]], [kpb[0]])
                k.op("dve", lambda h: h.tensor_reduce(out=Od[:], in_=Of.rearrange("p h d -> p d h"), axis=AX.X, op=ALU.add), r=[kpb[0]], w=[tb])
                dve_ts(Odb[:], Od[:], smS[:, 3:4], None, ALU.mult, None, [tb], [tb])
                tr(PT2[:, 1, 0:64], Odb[:], identb[0:64, 0:64], [tb, cb_], [pt2b])
                dve_cp(attnS[:], PT2[:, 1, 0:64].rearrange("p (h q) -> p h q", q=8), [pt2b], [attnSb])

        def even_layer(e):
            stop = cfg.get("even_stop")
            gates_phase(e)
            if stop == "gates":
                return False
            if do_sample:
                sample_moba(e)
            if stop == "smoba":
                return False
            a_phase(e)
            if stop == "a":
                return False
            b_phase(e)
            return True

        for li, l in enumerate(layers):
            last = li == len(layers) - 1
            if l % 2 == 0:
                if not even_layer(l // 2):
                    break
            else:
                odd_layer(l // 2)
            out_ln(l, last)
        k.barrier()
    return nc


def _consts():
    c = {}
    c["c_identf"] = np.eye(128, dtype=np.float32)
    s_ = np.arange(128)
    c["c_trilT"] = (s_[:, None] <= s_[None, :]).astype(np.float32)
    ql = np.arange(128)[:, None]
    kw = np.arange(256)[None, :]
    dd = ql + 128 - kw
    c["c_cmask"] = np.where(dd >= 0, 0.0, NEG).astype(np.float32)
    gm = np.zeros((128, 16, 8), np.float32)
    for i in range(16):
        gm[:, i, (i // 2):] = -1e30
    c["c_gmask"] = gm
    c["c_iota"] = np.arange(128, dtype=np.float32)[:, None]
    bd = np.zeros((64, 8, 128), np.float32)
    for p in range(64):
        bd[p, p // 8, :] = 1.0
    c["c_bdiag"] = bd
    qq = (np.arange(64) % 8)[:, None]
    kk = np.arange(8)[None, :]
    c["c_cmasks"] = np.where(kk <= qq, 0.0, NEG).astype(np.float32)
    return c


def prep_inputs(inp, core, cfg):
    npool = cfg.get("npool", NPOOL)
    b = core % 4
    s = core
    m = {}
    m["xpT"] = np.ascontiguousarray(inp["x_prompt"][b].T)
    m["xsT"] = np.ascontiguousarray(inp["x_sample"][s].T)
    m["ck"] = inp["cache_k"].reshape(2 * npool * 128, 1024)
    m["cv"] = inp["cache_v"].reshape(2 * npool * 128, 1024)
    m["pt"] = np.ascontiguousarray(inp["page_table"][s:s + 1]).astype(np.int32)
    m["sC"] = np.ascontiguousarray(inp["state_C"][:, s])
    m["sn"] = np.ascontiguousarray(inp["state_n"][:, s]).reshape(2, 4, 256, 1)
    m["sm"] = np.ascontiguousarray(inp["state_m"][:, s]).reshape(2, 1, 4)
    m["sconv"] = np.ascontiguousarray(inp["state_conv"][:, s].reshape(2, 2, 16, 128).transpose(0, 3, 2, 1))
    m["wie"] = inp["w_in_even"]
    m["woe"] = inp["w_out_even"]
    m["gbrep"] = np.ascontiguousarray(np.broadcast_to(inp["mlstm_gate_bias"][None], (128, 2, 8)))
    m["ngT"] = np.ascontiguousarray(inp["mlstm_norm_g"].reshape(2, 8, 128).transpose(0, 2, 1))
    rb = inp["rel_bias"]
    ql = np.arange(128)[:, None]
    kw = np.arange(256)[None, :]
    bk = t5_bucket_np(ql + 128 - kw)
    m["relp"] = np.ascontiguousarray(rb[:, bk].transpose(1, 0, 2))
    hq = np.arange(64)
    dist63 = 256 + (hq % 8)[:, None] - np.arange(256)[None, :]
    m["rels63"] = np.ascontiguousarray(rb[(hq // 8)[:, None], t5_bucket_np(dist63)])
    disto = (hq % 8)[:, None] - np.arange(8)[None, :]
    m["relso"] = np.ascontiguousarray(rb[(hq // 8)[:, None], t5_bucket_np(disto)])
    m["c31"] = np.ascontiguousarray(np.broadcast_to(rb[:, 31][None, :], (128, 8)))
    m["c31s"] = np.ascontiguousarray(rb[hq // 8, 31][:, None])
    m["wio"] = inp["w_in_odd"]
    m["woo"] = inp["w_out_odd"]
    m["cwT"] = np.ascontiguousarray(inp["conv_w"].reshape(2, 3, 16, 128).transpose(0, 3, 2, 1))
    m["lngT"] = np.ascontiguousarray(inp["ln_g"].reshape(4, 16, 128).transpose(0, 2, 1))
    m["lnbT"] = np.ascontiguousarray(inp["ln_b"].reshape(4, 16, 128).transpose(0, 2, 1))
    m.update(_consts())
    return {kk: np.ascontiguousarray(vv) for kk, vv in m.items()}


_NC_CACHE = {}


def kernel(**inputs):
    cfg = {}
    inp = {kk: np.asarray(vv) for kk, vv in inputs.items()}
    if "full" not in _NC_CACHE:
        _NC_CACHE["full"] = build(cfg)
    nc = _NC_CACHE["full"]
    in_maps = [prep_inputs(inp, c, cfg) for c in range(8)]
    res = run_bass_kernel_spmd(nc, in_maps, core_ids=list(range(8))).results
    f = np.float32
    y_prompt = np.stack([res[b]["ypT"].T for b in range(4)]).astype(f)
    y_sample = np.stack([res[s]["ysT"].T for s in range(8)]).astype(f)

    def kv(name, cores, n):
        return np.stack([res[c][name].transpose(0, 2, 1).reshape(2, n, 8, 128) for c in cores], axis=1).astype(f)

    def cst(name, cores):
        a = np.stack([res[c][name] for c in cores], axis=1)
        a = a.transpose(0, 1, 2, 4, 3, 5).reshape(2, len(cores), 4, 256, 257)
        return np.ascontiguousarray(a[..., :256]).astype(f), np.ascontiguousarray(a[..., 256]).astype(f)

    def conv(name, cores):
        a = np.stack([res[c][name] for c in cores], axis=1)
        return np.ascontiguousarray(a.transpose(0, 1, 4, 3, 2).reshape(2, len(cores), 2, 2048)).astype(f)

    pc, pn = cst("pCa", range(4))
    sc, sn = cst("sCa", range(8))
    pm = np.stack([res[c]["pm"][:, 0, :] for c in range(4)], axis=1).astype(f)
    sm = np.stack([res[c]["smo"][:, 0, :] for c in range(8)], axis=1).astype(f)
    return (y_prompt, y_sample, kv("pkT", range(4), T), kv("pvT", range(4), T), pc, pn, pm, conv("pbT", range(4)),
            kv("skT", range(8), S), kv("svT", range(8), S), sc, sn, sm, conv("sbT", range(8)))
```

```python
import math
import numpy as np
from contextlib import ExitStack
import concourse.bass as bass
import concourse.mybir as mybir
from concourse.bass_utils import run_bass_kernel_spmd

F32 = mybir.dt.float32
BF16 = mybir.dt.bfloat16
I32 = mybir.dt.int32
AF = mybir.ActivationFunctionType
ALU = mybir.AluOpType
AX = mybir.AxisListType

T = 2048
D = 2048
S = 8
NPOOL = 1280
NPAGES = 128
ALPHA = 8.0 ** 0.25
LN_EPS = 1e-5
NEG = -30000.0
RING = 5


class Buf:
    __slots__ = ("name", "w", "r")

    def __init__(self, name):
        self.name = name
        self.w = {}
        self.r = {}


class Eng:
    def __init__(self, h, sem, selfsync):
        self.h = h
        self.sem = sem
        self.selfsync = selfsync
        self.cnt = 0
        self.waited = {}


class KB:
    def __init__(self, nc, es):
        self.nc = nc
        self.es = es
        self.sems = {}
        self.dtot = {}
        self.eng = {}
        import os as _os
        _ss = _os.environ.get("KSELFSYNC", "1") == "1"
        for n, h, ss in (("pe", nc.tensor, False), ("act", nc.scalar, _ss), ("dve", nc.vector, _ss),
                         ("pool", nc.gpsimd, True), ("sp", nc.sync, False)):
            self.eng[n] = Eng(h, self.mksem("e_" + n), ss)

    def mksem(self, n):
        self.sems[n] = self.es.enter_context(self.nc.semaphore(n))
        return n

    def dsem(self, n):
        self.mksem(n)
        self.dtot[n] = 0
        return n

    def _waits(self, e, r, w):
        need = {}
        for b in r:
            for s, v in b.w.items():
                if v > need.get(s, 0):
                    need[s] = v
            if b.name[0] == "#":
                for s, v in b.r.items():
                    if s != e.sem and v > need.get(s, 0):
                        need[s] = v
        for b in w:
            for s, v in b.w.items():
                if v > need.get(s, 0):
                    need[s] = v
            for s, v in b.r.items():
                if s == e.sem:
                    continue
                if v > need.get(s, 0):
                    need[s] = v
        for s, v in need.items():
            if s in self.dtot:
                v = self.dtot[s]
            if s == e.sem and not e.selfsync:
                continue
            if e.waited.get(s, 0) < v:
                e.h.wait_ge(self.sems[s], v)
                e.waited[s] = v

    def op(self, en, fn, r=(), w=()):
        e = self.eng[en]
        self._waits(e, r, w)
        ins = fn(e.h)
        e.cnt += 1
        ins.then_inc(self.sems[e.sem], 1)
        for b in r:
            b.r[e.sem] = e.cnt
        for b in w:
            b.w = {e.sem: e.cnt}
            b.r = {}
        return ins

    def dma(self, qn, ds, fn, r=(), w=()):
        e = self.eng[qn]
        self._waits(e, r, w)
        if self.dtot[ds] > 0 and e.waited.get(ds, 0) < self.dtot[ds]:
            e.h.wait_ge(self.sems[ds], self.dtot[ds])
            e.waited[ds] = self.dtot[ds]
        ins = fn(e.h)
        self.dtot[ds] += 16
        ins.then_inc(self.sems[ds], 16)
        tok = self.dtot[ds]
        for b in r:
            b.r[ds] = tok
        for b in w:
            b.w = {ds: tok}
            b.r = {}
        return ins

    def barrier(self):
        tot = {e.sem: e.cnt for e in self.eng.values()}
        for s, v in self.dtot.items():
            tot[s] = v
        for e in self.eng.values():
            for s, v in tot.items():
                if s == e.sem or v == 0:
                    continue
                if e.waited.get(s, 0) < v:
                    e.h.wait_ge(self.sems[s], v)
                    e.waited[s] = v


class Stream:
    def __init__(self, name, n, groups, L):
        self.name = name
        self.n = n
        self.groups = groups
        self.L = L
        self.nch = n // L


def t5_bucket_np(dist):
    n = np.maximum(dist, 0)
    max_exact = 16
    nf = np.maximum(n, 1).astype(np.float32)
    large = max_exact + (np.log(nf / max_exact) / math.log(128 / max_exact) * (32 - max_exact)).astype(np.int32)
    large = np.minimum(large, 31)
    return np.where(n < max_exact, n, large)


def build(cfg):
    layers = cfg.get("layers", [0, 1, 2, 3])
    do_sample = cfg.get("sample", True)
    do_moba_s = cfg.get("moba_s", True)
    npool = cfg.get("npool", NPOOL)
    nc = bass.Bass("TRN2", target_bir_lowering=False)
    d = {}

    def din(name, shape, dt=F32):
        d[name] = nc.dram_tensor(name, shape, dt, kind="ExternalInput").ap()

    def dout(name, shape, dt=F32):
        d[name] = nc.dram_tensor(name, shape, dt, kind="ExternalOutput").ap()

    din("xpT", [D, T])
    din("xsT", [D, S])
    din("ck", [2 * npool * 128, 1024])
    din("cv", [2 * npool * 128, 1024])
    din("pt", [1, NPAGES], I32)
    din("sC", [2, 4, 256, 256])
    din("sn", [2, 4, 256, 1])
    din("sm", [2, 1, 4])
    din("sconv", [2, 128, 16, 2])
    din("wie", [2, D, 9224])
    din("woe", [2, D, D])
    din("gbrep", [128, 2, 8])
    din("ngT", [2, 128, 8])
    din("relp", [128, 8, 256])
    din("rels63", [64, 256])
    din("relso", [64, 8])
    din("c31", [128, 8])
    din("c31s", [64, 1])
    din("wio", [2, D, 8192])
    din("woo", [2, D, D])
    din("cwT", [2, 128, 16, 3])
    din("lngT", [4, 128, 16])
    din("lnbT", [4, 128, 16])
    din("c_identf", [128, 128])
    din("c_trilT", [128, 128])
    din("c_cmask", [128, 256])
    din("c_gmask", [128, 16, 8])
    din("c_iota", [128, 1])
    din("c_bdiag", [64, 8, 128])
    din("c_cmasks", [64, 8])
    dout("ypT", [D, T])
    dout("ysT", [D, S])
    dout("pkT", [2, 1024, T])
    dout("pvT", [2, 1024, T])
    dout("pCa", [2, 4, 128, 2, 257])
    dout("pm", [2, 1, 4])
    dout("pbT", [2, 128, 16, 2])
    dout("skT", [2, 1024, S])
    dout("svT", [2, 1024, S])
    dout("sCa", [2, 4, 128, 2, 257])
    dout("smo", [2, 1, 4])
    dout("sbT", [2, 128, 16, 2])

    es = ExitStack()
    with es:
        k = KB(nc, es)
        uid = {"n": 0}

        def sb(name, shape, dt=F32):
            return es.enter_context(nc.sbuf_tensor("s_" + name, shape, dt))

        def ps(name, shape, dt=F32):
            return es.enter_context(nc.psum_tensor("ps_" + name, shape, dt))

        class Phase:
            def __enter__(self):
                k.barrier()
                self.es = ExitStack()
                self.es.__enter__()
                return self

            def sb(self, name, shape, dt=F32):
                uid["n"] += 1
                return self.es.enter_context(nc.sbuf_tensor(f"t_{name}_{uid['n']}", shape, dt))

            def __exit__(self, *a):
                k.barrier()
                return self.es.__exit__(*a)

        P = Stream("p", T, [(g * 512, 512) for g in range(4)], 128)
        Sm = Stream("s", S, [(0, S)], S)
        streams = [P, Sm] if do_sample else [P]

        big = sb("big", [128, 16384])
        P.mT = big[:].bitcast(BF16).rearrange("p (f t) -> p f t", f=16)
        P.xT = sb("xT", [128, 16, T], BF16)
        Sm.xT = sb("xsT_", [128, 16, S], BF16)
        Sm.mT = sb("msT_", [128, 16, S], BF16)
        for st in [P, Sm]:
            st.xb = [Buf(f"{st.name}x{g}") for g in range(len(st.groups))]
            st.mb = [Buf(f"{st.name}m{f}") for f in range(16)]
            st.accb = [Buf(f"{st.name}acc1_{g}") for g in range(len(st.groups))]
            st.acc2b = [Buf(f"{st.name}acc2_{g}") for g in range(len(st.groups))]
            st.gw = sb(f"gw{st.name}", [128, st.nch * 4])
            st.gf = sb(f"gf{st.name}", [128, st.nch * 4])
            st.gom = sb(f"gom{st.name}", [128, st.nch * 4])
            st.gateb = Buf(f"gate{st.name}")
        ring = sb("ring", [128, RING, 16, 128], BF16)
        ringb = [Buf(f"ring{i}") for i in range(RING)]
        ringd = [k.dsem(f"d_ring{i}") for i in range(RING)]
        work = sb("work", [128, 4, 512])
        workb = [Buf(f"work{i}") for i in range(4)]
        stg = sb("stg", [128, 2, 512])
        stgb = [Buf("stg0"), Buf("stg1")]
        stgd = [k.dsem("d_stg0"), k.dsem("d_stg1")]
        stgi = {"i": 0}
        identf = sb("identf", [128, 128])
        identb = sb("identb", [128, 128], BF16)
        onesf = sb("onesf", [128, 128])
        trilT = sb("trilT", [128, 128])
        BW = sb("BW", [128, 8, 256])
        c31 = sb("c31", [128, 8])
        gmask = sb("gmask", [128, 16, 8])
        gbrep = sb("gbrep", [128, 2, 8])
        ngT = sb("ngT", [128, 2, 8])
        cwT = sb("cwT", [128, 2, 16, 3])
        lngT = sb("lngT", [128, 4, 16])
        lnbT = sb("lnbT", [128, 4, 16])
        iota = sb("iota", [128, 1])
        attnS = sb("attnS", [128, 8, S])
        attnSb = Buf("attnS")
        qs = sb("qs", [128, 8, S], BF16)
        ks = sb("ks", [128, 8, S], BF16)
        vs = sb("vs", [128, 8, S], BF16)
        qkvb = Buf("qkvs")
        cb_ = Buf("consts")
        dconst = k.dsem("d_const")
        dmisc = k.dsem("d_misc")

        P4 = ps("P4", [128, 4, 512])
        p4b = [Buf(f"#p4_{i}") for i in range(4)]
        PT = ps("PT", [128, 8, 128], BF16)
        ptb = Buf("#PT")
        PT2 = ps("PT2", [128, 8, 128], BF16)
        pt2b = Buf("#PT2")
        PA = ps("PA", [128, 512])
        pab = Buf("#PA")
        PB = ps("PB", [128, 512])
        pbb = Buf("#PB")
        P4f = P4[:].rearrange("p a b -> p (a b)")

        def cload(dst, src):
            k.dma("sp", dconst, lambda h, dst=dst, src=src: h.dma_start(out=dst, in_=src), w=[cb_])

        cload(identf[:], d["c_identf"])
        cload(trilT[:], d["c_trilT"])
        cload(BW[:], d["relp"])
        cload(c31[:], d["c31"])
        cload(gmask[:], d["c_gmask"])
        cload(gbrep[:], d["gbrep"])
        cload(iota[:], d["c_iota"])
        for e in range(2):
            cload(ngT[:, e, :], d["ngT"][e])
            cload(cwT[:, e, :, :], d["cwT"][e])
        for l in range(4):
            cload(lngT[:, l, :], d["lngT"][l])
            cload(lnbT[:, l, :], d["lnbT"][l])
        k.op("dve", lambda h: h.tensor_copy(out=identb[:], in_=identf[:]), r=[cb_], w=[cb_])
        k.op("dve", lambda h: h.memset(onesf[:], 1.0), w=[cb_])
        cm = work[:, 0, 0:256]
        k.dma("sp", dconst, lambda h: h.dma_start(out=cm, in_=d["c_cmask"]), w=[workb[0]])
        for hh in range(8):
            k.op("dve", lambda h, hh=hh: h.scalar_tensor_tensor(out=BW[:, hh, :], in0=BW[:, hh, :], scalar=c31[:, hh:hh + 1],
                                                              in1=cm, op0=ALU.subtract, op1=ALU.add), r=[cb_, workb[0]], w=[cb_])

        dxin = k.dsem("d_xin")
        for kc in range(16):
            k.dma("pool", dxin, lambda h, kc=kc: h.dma_start(out=P.xT[:, kc, :], in_=d["xpT"][kc * 128:(kc + 1) * 128, :]), w=P.xb)
        if do_sample:
            with nc.allow_non_contiguous_dma(reason="tiny"):
                k.dma("pool", dxin, lambda h: h.dma_start(out=Sm.xT[:], in_=d["xsT"].rearrange("(kc p) s -> p kc s", p=128)), w=Sm.xb)

        plan = []

        def plan_even(e):
            W = d["wie"][e]
            pl = [("gif", W[:, 9216:9224], 8)]
            if do_sample:
                for h_ in range(8):
                    for nm, off in (("sq", 0), ("sk", 1024), ("sv", 2048)):
                        pl.append((nm, W[:, off + h_ * 128: off + (h_ + 1) * 128], 128))
            for h_ in range(8):
                for nm, off in (("ag", 3072), ("aq", 0), ("ak", 1024), ("av", 2048)):
                    pl.append((nm, W[:, off + h_ * 128: off + (h_ + 1) * 128], 128))
            for h_ in range(4):
                for nm, off in (("bo", 7168), ("bg", 8192), ("bq", 4096), ("bk", 5120), ("bv", 6144)):
                    for dc in range(2):
                        c0 = off + h_ * 256 + dc * 128
                        pl.append((nm, W[:, c0:c0 + 128], 128))
            for fb in range(16):
                pl.append(("o", d["woe"][e][:, fb * 128:(fb + 1) * 128], 128))
            return pl

        def plan_odd(o):
            W = d["wio"][o]
            pl = []
            for fb in range(16):
                for nm, off in (("cc", 2048), ("cx", 4096), ("cb", 0), ("cg", 6144)):
                    pl.append((nm, W[:, off + fb * 128: off + (fb + 1) * 128], 128))
            for fb in range(16):
                pl.append(("o", d["woo"][o][:, fb * 128:(fb + 1) * 128], 128))
            return pl

        for l in layers:
            plan += plan_even(l // 2) if l % 2 == 0 else plan_odd(l // 2)
        wstate = {"loaded": 0, "used": 0}

        def wprefetch(upto):
            while wstate["loaded"] < min(upto, len(plan)):
                i = wstate["loaded"]
                tag, ap, ncols = plan[i]
                sl = i % RING
                with nc.allow_non_contiguous_dma(reason="weight column block"):
                    k.dma("pool", ringd[sl],
                          lambda h, ap=ap, sl=sl, ncols=ncols: h.dma_start(out=ring[:, sl, :, 0:ncols], in_=ap.rearrange("(kc p) c -> p kc c", p=128)),
                          w=[ringb[sl]])
                wstate["loaded"] += 1

        def wnext(tag):
            i = wstate["used"]
            assert plan[i][0] == tag, (plan[i][0], tag, i)
            wprefetch(i + RING)
            wstate["used"] += 1
            sl = i % RING
            return ring[:, sl, :, :], ringb[sl], plan[i][2]

        def proj(tag, src, sts, cb):
            wap, wbuf, ncols = wnext(tag)
            for st in sts:
                srcT = st.xT if src == "x" else st.mT
                for gi, (g0, gs) in enumerate(st.groups):
                    if st is P:
                        pout, pbuf = P4[0:ncols, gi, 0:gs], p4b[gi]
                    else:
                        pout, pbuf = PB[0:ncols, 0:gs], pbb
                    for kc in range(16):
                        rb = [wbuf, st.xb[gi] if src == "x" else st.mb[kc]]
                        k.op("pe", lambda h, pout=pout, kc=kc, g0=g0, gs=gs, srcT=srcT: h.matmul(
                            out=pout, lhsT=wap[:, kc, 0:ncols], rhs=srcT[:, kc, g0:g0 + gs], start=(kc == 0), stop=(kc == 15)),
                            r=rb, w=[pbuf])
                    cb(st, gi, g0, gs, pout, pbuf)

        def act(out, in_, func, r, w, scale=1.0, bias=None, accum=None):
            kw = {}
            if bias is not None:
                kw["bias"] = bias
            if accum is not None:
                kw["accum_out"] = accum
            return k.op("act", lambda h: h.activation(out=out, in_=in_, func=func, scale=scale, **kw), r=r, w=w)

        def dve_tt(out, in0, in1, op, r, w):
            return k.op("dve", lambda h: h.tensor_tensor(out=out, in0=in0, in1=in1, op=op), r=r, w=w)

        def dve_ts(out, in0, s1, s2, op0, op1, r, w):
            if s2 is None:
                return k.op("dve", lambda h: h.tensor_scalar(out=out, in0=in0, scalar1=s1, scalar2=None, op0=op0), r=r, w=w)
            return k.op("dve", lambda h: h.tensor_scalar(out=out, in0=in0, scalar1=s1, scalar2=s2, op0=op0, op1=op1), r=r, w=w)

        def dve_stt(out, in0, sc, in1, op0, op1, r, w):
            return k.op("dve", lambda h: h.scalar_tensor_tensor(out=out, in0=in0, scalar=sc, in1=in1, op0=op0, op1=op1), r=r, w=w)

        def dve_cp(out, in_, r, w):
            return k.op("dve", lambda h: h.tensor_copy(out=out, in_=in_), r=r, w=w)

        def mm(out, lhsT, rhs, start, stop, r, w):
            return k.op("pe", lambda h: h.matmul(out=out, lhsT=lhsT, rhs=rhs, start=start, stop=stop), r=r, w=w)

        def tr(out, in_, ident, r, w):
            return k.op("pe", lambda h: h.transpose(out=out, in_=in_, identity=ident), r=r, w=w)

        def store(dst, src_psum, pbuf):
            si = stgi["i"] % 2
            stgi["i"] += 1
            np_, nf = src_psum.shape[0], src_psum.shape[1]
            so = stg[0:np_, si, 0:nf]
            dve_cp(so, src_psum, [pbuf], [stgb[si]])
            k.dma("sp", stgd[si], lambda h: h.dma_start(out=dst, in_=so), r=[stgb[si]])

        def out_ln(layer, last):
            with Phase() as ph:
                P.acc1 = ph.sb("acc1", [128, T])
                P.acc2 = ph.sb("acc2", [128, T])
                Sm.acc1 = ph.sb("acc1s", [128, S])
                Sm.acc2 = ph.sb("acc2s", [128, S])
                out_ln_(layer, last)

        def out_ln_(layer, last):
            for fb in range(16):
                def cb(st, gi, g0, gs, pin, pbuf, fb=fb):
                    xs_ = st.xT[:, fb, g0:g0 + gs]
                    dve_stt(xs_, xs_, ALPHA, pin, ALU.mult, ALU.add, [pbuf, st.xb[gi]], [st.xb[gi]])
                    a1 = st.acc1[:, g0:g0 + gs]
                    a2 = st.acc2[:, g0:g0 + gs]
                    wi = 3 if st is P else 2
                    sq = work[:, wi, 0:gs]
                    act(sq, xs_, AF.Square, [st.xb[gi]], [workb[wi]])
                    if fb == 0:
                        dve_cp(a1, xs_, [st.xb[gi]], [st.accb[gi]])
                        k.op("pool", lambda h: h.tensor_copy(out=a2, in_=sq), r=[workb[wi]], w=[st.acc2b[gi]])
                    else:
                        dve_tt(a1, a1, xs_, ALU.add, [st.xb[gi], st.accb[gi]], [st.accb[gi]])
                        k.op("pool", lambda h: h.tensor_tensor(out=a2, in0=a2, in1=sq, op=ALU.add), r=[workb[wi], st.acc2b[gi]], w=[st.acc2b[gi]])
                proj("o", "mix", streams, cb)
            for st in streams:
                for gi, (g0, gs) in enumerate(st.groups):
                    mm(PA[:, 0:gs], onesf[:], st.acc1[:, g0:g0 + gs], True, True, [st.accb[gi], cb_], [pab])
                    mm(PB[:, 0:gs], onesf[:], st.acc2[:, g0:g0 + gs], True, True, [st.acc2b[gi], cb_], [pbb])
                    meanb = work[:, 0, 0:gs]
                    rstdb = work[:, 1, 0:gs]
                    act(meanb, PA[:, 0:gs], AF.Copy, [pab], [workb[0]], scale=1.0 / D)
                    dve_tt(rstdb, meanb, meanb, ALU.mult, [workb[0]], [workb[1]])
                    dve_stt(rstdb, PB[:, 0:gs], 1.0 / D, rstdb, ALU.mult, ALU.subtract, [pbb, workb[1]], [workb[1]])
                    dve_ts(rstdb, rstdb, LN_EPS, None, ALU.add, None, [workb[1]], [workb[1]])
                    act(rstdb, rstdb, AF.Sqrt, [workb[1]], [workb[1]])
                    k.op("dve", lambda h: h.reciprocal(out=rstdb, in_=rstdb), r=[workb[1]], w=[workb[1]])
                    for fb in range(16):
                        xs_ = st.xT[:, fb, g0:g0 + gs]
                        tmp = work[:, 2, 0:gs]
                        dve_tt(tmp, xs_, meanb, ALU.subtract, [st.xb[gi], workb[0]], [workb[2]])
                        dve_tt(tmp, tmp, rstdb, ALU.mult, [workb[2], workb[1]], [workb[2]])
                        if not last:
                            act(xs_, tmp, AF.Identity, [workb[2], cb_], [st.xb[gi]], scale=lngT[:, layer, fb:fb + 1], bias=lnbT[:, layer, fb:fb + 1])
                        else:
                            si = stgi["i"] % 2
                            stgi["i"] += 1
                            so = stg[:, si, 0:gs]
                            act(so, tmp, AF.Identity, [workb[2], cb_], [stgb[si]], scale=lngT[:, layer, fb:fb + 1], bias=lnbT[:, layer, fb:fb + 1])
                            dst = d["ypT" if st is P else "ysT"][fb * 128:(fb + 1) * 128, g0:g0 + gs]
                            with nc.allow_non_contiguous_dma(reason="small"):
                                k.dma("sp", stgd[si], lambda h, so=so, dst=dst: h.dma_start(out=dst, in_=so), r=[stgb[si]])

        def odd_layer(o):
            with Phase() as ph:
                Uc = {P: ph.sb("Ucp", [128, T]), Sm: ph.sb("Ucs", [128, S])}
                U = {P: ph.sb("Up", [128, T + 2]), Sm: ph.sb("Us", [128, S + 2])}
                Y = {P: ph.sb("Yp", [128, T]), Sm: ph.sb("Ys", [128, S])}
                ucb = {P: Buf("ucp"), Sm: Buf("ucs")}
                ub = {P: Buf("up"), Sm: Buf("us")}
                yb = {P: Buf("yp"), Sm: Buf("ys")}
                for fb in range(16):
                    proj("cc", "x", streams, lambda st, gi, g0, gs, pin, pbuf: act(Uc[st][:, g0:g0 + gs], pin, AF.Copy, [pbuf], [ucb[st]]))
                    for st in streams:
                        if st is P:
                            k.op("dve", lambda h: h.memset(U[P][:, 0:2], 0.0), w=[ub[P]])
                        else:
                            k.dma("sp", dmisc, lambda h, fb=fb: h.dma_start(out=U[Sm][:, 0:2], in_=d["sconv"][o, :, fb, :]), w=[ub[Sm]])
                    proj("cx", "x", streams, lambda st, gi, g0, gs, pin, pbuf: dve_tt(U[st][:, 2 + g0:2 + g0 + gs], pin, Uc[st][:, g0:g0 + gs], ALU.mult, [pbuf, ucb[st]], [ub[st]]))
                    for st in streams:
                        n = st.n
                        dve_ts(Y[st][:, :], U[st][:, 2:2 + n], cwT[:, o, fb, 2:3], None, ALU.mult, None, [ub[st], cb_], [yb[st]])
                        dve_stt(Y[st][:, :], U[st][:, 1:1 + n], cwT[:, o, fb, 1:2], Y[st][:, :], ALU.mult, ALU.add, [ub[st], yb[st], cb_], [yb[st]])
                        dve_stt(Y[st][:, :], U[st][:, 0:n], cwT[:, o, fb, 0:1], Y[st][:, :], ALU.mult, ALU.add, [ub[st], yb[st], cb_], [yb[st]])
                        dst = d["pbT" if st is P else "sbT"][o, :, fb, :]
                        with nc.allow_non_contiguous_dma(reason="tiny"):
                            k.dma("sp", dmisc, lambda h, dst=dst, st=st, n=n: h.dma_start(out=dst, in_=U[st][:, n:n + 2]), r=[ub[st]])
                    proj("cb", "x", streams, lambda st, gi, g0, gs, pin, pbuf: dve_tt(Y[st][:, g0:g0 + gs], pin, Y[st][:, g0:g0 + gs], ALU.mult, [pbuf, yb[st]], [yb[st]]))

                    def cbg(st, gi, g0, gs, pin, pbuf, fb=fb):
                        wi = 0 if st is P else 1
                        act(work[:, wi, 0:gs], pin, AF.Silu, [pbuf], [workb[wi]])
                        dve_tt(st.mT[:, fb, g0:g0 + gs], work[:, wi, 0:gs], Y[st][:, g0:g0 + gs], ALU.mult, [workb[wi], yb[st]], [st.mb[fb]])
                    proj("cg", "x", streams, cbg)

        def gate_math(ph, st, e):
            L, nch = st.L, st.nch
            n64 = nch * 4
            gb = st.gateb
            tA = ph.sb("gtA", [128, 6, 64])
            tB = ph.sb("gtB", [64, 2])
            R = ph.sb("gtR", [1, 8, 64])
            tb = Buf("gt")
            G3 = PA[0:L, 0:nch * 8].rearrange("p (c g) -> p c g", g=8)

            def t3(i):
                return tA[0:L, i, 0:n64].rearrange("p (c h) -> p c h", h=4)

            def t2(i):
                return tA[0:L, i, 0:n64]
            bi = gbrep[0:L, e, 0:4].unsqueeze(1).broadcast_to([L, nch, 4])
            bf_ = gbrep[0:L, e, 4:8].unsqueeze(1).broadcast_to([L, nch, 4])
            dve_tt(t3(0), G3[:, :, 0:4], bi, ALU.add, [pab, cb_], [tb])
            dve_tt(t3(1), G3[:, :, 4:8], bf_, ALU.add, [pab, cb_], [tb])
            act(t2(1), t2(1), AF.Exp, [tb], [tb], scale=-1.0)
            act(t2(1), t2(1), AF.Ln, [tb], [tb], bias=1.0)
            mm(PB[0:L, 0:n64], trilT[0:L, 0:L], t2(1), True, True, [tb, cb_], [pbb])
            dve_cp(t2(2), PB[0:L, 0:n64], [pbb], [tb])
            dve_tt(t2(3), t2(0), t2(2), ALU.add, [tb], [tb])
            tr(PA[0:n64, 0:L], t2(3), identf[0:L, 0:L], [tb, cb_], [pab])
            tr(PB[0:n64, 0:L], t2(2), identf[0:L, 0:L], [tb, cb_], [pbb])
            k.op("dve", lambda h: h.tensor_reduce(out=tB[0:n64, 0:1], in_=PA[0:n64, 0:L], axis=AX.X, op=ALU.max), r=[pab], w=[tb])
            dve_cp(tB[0:n64, 1:2], PB[0:n64, L - 1:L], [pbb], [tb])
            tr(PA[0:1, 0:n64], tB[0:n64, 0:1], identf[0:n64, 0:n64], [tb, cb_], [pab])
            tr(PA[0:1, 64:64 + n64], tB[0:n64, 1:2], identf[0:n64, 0:n64], [tb, cb_], [pab])
            dve_cp(R[0:1, 0, 0:n64], PA[0:1, 0:n64], [pab], [tb])
            dve_cp(R[0:1, 1, 0:n64], PA[0:1, 64:64 + n64], [pab], [tb])
            dve_ts(R[0:1, 2, 0:n64], R[0:1, 1, 0:n64], -1.0, None, ALU.mult, None, [tb], [tb])
            if st is Sm:
                k.dma("sp", dmisc, lambda h: h.dma_start(out=R[0:1, 7, 0:4], in_=d["sm"][e]), w=[tb])
            else:
                k.op("dve", lambda h: h.memset(R[0:1, 7, 0:4], 0.0), w=[tb])

            def rv(i):
                return R[0:1, i, 0:n64].rearrange("p (c h) -> p c h", h=4)
            for hh in range(4):
                k.op("dve", lambda h, hh=hh: h.tensor_tensor_scan(out=rv(3)[:, :, hh], data0=rv(0)[:, :, hh], data1=rv(2)[:, :, hh],
                                                                initial=R[0:1, 7, hh:hh + 1], op0=ALU.max, op1=ALU.add), r=[tb], w=[tb])
            dve_tt(R[0:1, 4, 0:n64], R[0:1, 3, 0:n64], R[0:1, 1, 0:n64], ALU.add, [tb], [tb])
            dve_cp(rv(5)[:, 0, :], R[0:1, 7, 0:4], [tb], [tb])
            if nch > 1:
                dve_cp(rv(5)[:, 1:nch, :], rv(3)[:, 0:nch - 1, :], [tb], [tb])
            dve_tt(R[0:1, 6, 0:n64], R[0:1, 5, 0:n64], R[0:1, 4, 0:n64], ALU.subtract, [tb], [tb])
            act(R[0:1, 6, 0:n64], R[0:1, 6, 0:n64], AF.Exp, [tb], [tb])
            mm(PA[0:128, 0:n64], onesf[0:1, 0:128], R[0:1, 6, 0:n64], True, True, [tb, cb_], [pab])
            dve_cp(st.gom[:, 0:n64], PA[0:128, 0:n64], [pab], [gb])
            mm(PB[0:L, 0:n64], onesf[0:1, 0:L], R[0:1, 4, 0:n64], True, True, [tb, cb_], [pbb])
            dve_tt(t2(4), t2(3), PB[0:L, 0:n64], ALU.subtract, [tb, pbb], [tb])
            act(st.gw[0:L, 0:n64], t2(4), AF.Exp, [tb], [gb], bias=math.log(1.0 / 16.0))
            dve_tt(t2(5), t2(2), PB[0:L, 0:n64], ALU.subtract, [tb, pbb], [tb])
            act(st.gf[0:L, 0:n64], t2(5), AF.Exp, [tb], [gb])
            dst = d["pm" if st is P else "smo"][e]
            k.dma("sp", dmisc, lambda h: h.dma_start(out=dst, in_=rv(3)[:, nch - 1, :]), r=[tb])

        def gates_phase(e):
            with Phase() as ph:
                wap, wbuf, _ = wnext("gif")
                for st in streams:
                    L, nch = st.L, st.nch
                    for c in range(nch):
                        gi = (c * L) // 512
                        for kc in range(16):
                            mm(PA[0:L, c * 8:(c + 1) * 8], st.xT[:, kc, c * L:(c + 1) * L], wap[:, kc, 0:8], kc == 0, kc == 15, [wbuf, st.xb[gi]], [pab])
                    gate_math(ph, st, e)

        def mlstm_head(ph, st, e, hd, q2, k2, v2, qb, tl):
            L, nch = st.L, st.nch
            C32, Cb, ktl, vaug2, Stt, hn, sm_ = tl["C32"], tl["Cb"], tl["ktl"], tl["vaug2"], tl["St"], tl["hn"], tl["sm"]
            cbuf, tb = tl["cbuf"], tl["tb"]
            if st is P:
                k.op("dve", lambda h: h.memset(C32[:], 0.0), w=[cbuf])
            else:
                for dc in range(2):
                    k.dma("sp", dmisc, lambda h, dc=dc: h.dma_start(out=C32[:, dc, 0:256], in_=d["sC"][e, hd, dc * 128:(dc + 1) * 128, :]), w=[cbuf])
                    with nc.allow_non_contiguous_dma(reason="tiny"):
                        k.dma("sp", dmisc, lambda h, dc=dc: h.dma_start(out=C32[:, dc, 256:257], in_=d["sn"][e, hd, dc * 128:(dc + 1) * 128, :]), w=[cbuf])
            ktl2, vaug22, Stt2 = tl["ktl2"], tl["vaug22"], tl["St2"]
            kt_ = [ktl, ktl2]
            va_ = [vaug2, vaug22]
            st_ = [Stt, Stt2]
            tba = tl["tba"]

            def stage_a(c):
                cs = slice(c * L, (c + 1) * L)
                col = c * 4 + hd
                wv = st.gw[0:L, col:col + 1]
                p = c % 2
                for dc in range(2):
                    tr(PT[0:L, dc, :], k2[:, dc, cs], identb[:], [qb], [ptb])
                    tr(PT2[0:L, dc, :], v2[:, dc, cs], identb[:], [qb], [pt2b])
                act(kt_[p][0:L, :], PT[0:L, 0:2, :].rearrange("p a b -> p (a b)"), AF.Copy, [ptb, st.gateb], [tba[p]], scale=wv)
                act(va_[p][0:L, 0:256], PT2[0:L, 0:2, :].rearrange("p a b -> p (a b)"), AF.Copy, [pt2b], [tba[p]])
                for dc in range(2):
                    mm(PA[0:L, 0:L], k2[:, dc, cs], q2[:, dc, cs], dc == 0, dc == 1, [qb], [pab])
                dve_stt(st_[p][0:L, 0:L], PA[0:L, 0:L], wv, trilT[0:L, 0:L], ALU.mult, ALU.mult, [pab, st.gateb, cb_], [tba[p]])

            stage_a(0)
            for c in range(nch):
                cs = slice(c * L, (c + 1) * L)
                col = c * 4 + hd
                p = c % 2
                om = st.gom[:, col:col + 1]
                fv = st.gf[0:L, col:col + 1]
                dve_ts(Cb[:], C32[:], om, None, ALU.mult, None, [cbuf, st.gateb], [tb])
                if c + 1 < nch:
                    stage_a(c + 1)
                mm(PB[0:L, 0:257], st_[p][0:L, 0:L], va_[p][0:L, 0:257], True, False, [tba[p]], [pbb])
                for dc in range(2):
                    mm(PB[0:L, 0:257], q2[:, dc, cs], Cb[:, dc, :], False, dc == 1, [qb, tb], [pbb])
                act(sm_[0:L, 0:1], PB[0:L, 256:257], AF.Abs, [pbb], [tb])
                dve_tt(sm_[0:L, 0:1], sm_[0:L, 0:1], fv, ALU.max, [tb, st.gateb], [tb])
                k.op("dve", lambda h: h.bn_stats(out=sm_[0:L, 2:8], in_=PB[0:L, 0:256]), r=[pbb], w=[tb])
                k.op("dve", lambda h: h.bn_aggr(out=sm_[0:L, 8:10], in_=sm_[0:L, 2:8]), r=[tb], w=[tb])
                dve_tt(sm_[0:L, 1:2], sm_[0:L, 0:1], sm_[0:L, 0:1], ALU.mult, [tb], [tb])
                dve_stt(sm_[0:L, 1:2], sm_[0:L, 1:2], LN_EPS, sm_[0:L, 9:10], ALU.mult, ALU.add, [tb], [tb])
                act(sm_[0:L, 1:2], sm_[0:L, 1:2], AF.Sqrt, [tb], [tb])
                k.op("dve", lambda h: h.reciprocal(out=sm_[0:L, 1:2], in_=sm_[0:L, 1:2]), r=[tb], w=[tb])
                dve_ts(hn[0:L, :], PB[0:L, 0:256], sm_[0:L, 8:9], sm_[0:L, 1:2], ALU.subtract, ALU.mult, [pbb, tb], [tb])
                for dc in range(2):
                    tr(PT[:, 4 + dc, 0:L], hn[0:L, dc * 128:(dc + 1) * 128], identb[0:L, 0:L], [tb], [ptb])
                    fbi = 8 + 2 * hd + dc
                    ms = st.mT[:, fbi, cs]
                    dve_stt(ms, PT[:, 4 + dc, 0:L], ngT[:, e, 2 * hd + dc:2 * hd + dc + 1], ms, ALU.mult, ALU.mult, [ptb, cb_, st.mb[fbi]], [st.mb[fbi]])
                for dc in range(2):
                    mm(P4[:, dc, 0:257], kt_[p][0:L, dc * 128:(dc + 1) * 128], va_[p][0:L, 0:257], True, True, [tba[p]], [p4b[dc]])
                    dve_stt(C32[:, dc, :], C32[:, dc, :], om, P4[:, dc, 0:257], ALU.mult, ALU.add, [cbuf, p4b[dc], st.gateb], [cbuf])
            dst = d["pCa" if st is P else "sCa"][e, hd]
            k.dma("sp", dmisc, lambda h: h.dma_start(out=dst, in_=C32[:]), r=[cbuf])

        def b_phase(e):
            with Phase() as ph:
                q2 = {P: ph.sb("q2p", [128, 2, T], BF16), Sm: ph.sb("q2s", [128, 2, S], BF16)}
                k2 = {P: ph.sb("k2p", [128, 2, T], BF16), Sm: ph.sb("k2s", [128, 2, S], BF16)}
                v2 = {P: ph.sb("v2p", [128, 2, T], BF16), Sm: ph.sb("v2s", [128, 2, S], BF16)}
                qb = {P: Buf("qkv2p"), Sm: Buf("qkv2s")}
                tl = {"C32": ph.sb("C32", [128, 2, 257]), "Cb": ph.sb("Cb", [128, 2, 257], BF16), "ktl": ph.sb("ktl", [128, 256], BF16),
                      "vaug2": ph.sb("vaug2", [128, 260], BF16), "St": ph.sb("St", [128, 128], BF16), "hn": ph.sb("hn", [128, 256], BF16),
                      "sm": ph.sb("sm", [128, 16]), "cbuf": Buf("C32"), "tb": Buf("mlt"),
                      "ktl2": ph.sb("ktl2", [128, 256], BF16), "vaug22": ph.sb("vaug22", [128, 260], BF16), "St2": ph.sb("St2", [128, 128], BF16),
                      "tba": [Buf("mlta0"), Buf("mlta1")]}
                k.op("dve", lambda h: h.memset(tl["vaug2"][:, 256:257], 1.0), w=[tl["tba"][0]])
                k.op("dve", lambda h: h.memset(tl["vaug22"][:, 256:257], 1.0), w=[tl["tba"][1]])
                for hd in range(4):
                    for dc in range(2):
                        fbi = 8 + 2 * hd + dc
                        proj("bo", "x", streams, lambda st, gi, g0, gs, pin, pbuf, fbi=fbi: act(st.mT[:, fbi, g0:g0 + gs], pin, AF.Sigmoid, [pbuf], [st.mb[fbi]]))
                    for dc in range(2):
                        fbi = 8 + 2 * hd + dc

                        def cbg(st, gi, g0, gs, pin, pbuf, fbi=fbi):
                            wi = 0 if st is P else 1
                            act(work[:, wi, 0:gs], pin, AF.Silu, [pbuf], [workb[wi]])
                            ms = st.mT[:, fbi, g0:g0 + gs]
                            dve_tt(ms, ms, work[:, wi, 0:gs], ALU.mult, [workb[wi], st.mb[fbi]], [st.mb[fbi]])
                        proj("bg", "x", streams, cbg)
                    for nm, dst_ in (("bq", q2), ("bk", k2), ("bv", v2)):
                        for dc in range(2):
                            proj(nm, "x", streams, lambda st, gi, g0, gs, pin, pbuf, dst_=dst_, dc=dc: act(dst_[st][:, dc, g0:g0 + gs], pin, AF.Copy, [pbuf], [qb[st]]))
                    for st in streams:
                        mlstm_head(ph, st, e, hd, q2[st], k2[st], v2[st], qb[st], tl)

        def a_phase(e):
            scale = 128.0 ** -0.5
            with Phase() as ph:
                qT = ph.sb("qT", [128, T], BF16)
                kT = ph.sb("kT", [128, T], BF16)
                vp = ph.sb("vp", [128, T], BF16)
                PTs = vp[:].rearrange("p (a b) -> p a b", b=128)
                vaug = ph.sb("vaug", [128, 16, 132], BF16)
                Pm = ph.sb("Pm", [128, T], BF16)
                ksum = ph.sb("ksum", [128, 8])
                kmhl = ph.sb("kmhl", [128, 2, 8], BF16)
                Gs = ph.sb("Gs", [128, 16, 8])
                sel = ph.sb("sel", [128, 16, 8])
                sm_ = ph.sb("sma", [128, 32])
                oh = ph.sb("oh", [128, 128], BF16)
                qb_, kb_, vb_, vab, pmb, gb_, tb = Buf("qT"), Buf("kT"), Buf("vp"), Buf("vaug"), Buf("Pm"), Buf("Gs"), Buf("smA")
                k.op("dve", lambda h: h.memset(vaug[:, :, 128:129], 1.0), w=[vab])
                asub = cfg.get("a_sub", 99)
                for hd in range(cfg.get("a_heads", 8)):
                    def cbg(st, gi, g0, gs, pin, pbuf, hd=hd):
                        if st is P:
                            act(st.mT[:, hd, g0:g0 + gs], pin, AF.Silu, [pbuf], [st.mb[hd]])
                        else:
                            act(work[:, 1, 0:gs], pin, AF.Silu, [pbuf], [workb[1]])
                            dve_tt(st.mT[:, hd, g0:g0 + gs], work[:, 1, 0:gs], attnS[:, hd, :], ALU.mult, [workb[1], attnSb], [st.mb[hd]])
                    proj("ag", "x", streams, cbg)
                    if asub < 1:
                        break
                    proj("aq", "x", [P], lambda st, gi, g0, gs, pin, pbuf: act(qT[:, g0:g0 + gs], pin, AF.Copy, [pbuf], [qb_], scale=scale))
                    if asub < 2:
                        break

                    def cbk(st, gi, g0, gs, pin, pbuf, hd=hd):
                        if cfg.get("kskip") not in ("store", "both"):
                            store(d["pkT"][e, hd * 128:(hd + 1) * 128, g0:g0 + gs], pin, pbuf)
                        for hf in range(2):
                            act(kT[:, g0 + hf * 256:g0 + (hf + 1) * 256], pin[:, hf * 256:(hf + 1) * 256], AF.Copy, [pbuf], [kb_, gb_], accum=ksum[:, 2 * gi + hf:2 * gi + hf + 1])
                    proj("ak", "x", [P], cbk)
                    if asub < 3:
                        break

                    def cbv(st, gi, g0, gs, pin, pbuf, hd=hd):
                        store(d["pvT"][e, hd * 128:(hd + 1) * 128, g0:g0 + gs], pin, pbuf)
                        act(vp[:, g0:g0 + gs], pin, AF.Copy, [pbuf], [vb_])
                    proj("av", "x", [P], cbv)
                    if asub < 4:
                        break
                    for half in range(2):
                        for j in range(8):
                            ti = half * 8 + j
                            tr(PT[:, j, :], vp[:, ti * 128:(ti + 1) * 128], identb[:], [vb_], [ptb])
                        act(vaug[:, half * 8:(half + 1) * 8, 0:128], PT[:, :, :], AF.Copy, [ptb], [vab])
                    if cfg.get("a_stop") == "proj":
                        continue
                    dve_cp(kmhl[:, 0, :], ksum[:], [gb_], [gb_])
                    dve_tt(kmhl[:, 1, :], ksum[:], kmhl[:, 0, :], ALU.subtract, [gb_], [gb_])
                    for i in range(8, 16):
                        mm(PA[:, i * 8:(i + 1) * 8], qT[:, i * 128:(i + 1) * 128], kmhl[:, 0, :], True, False, [qb_, gb_], [pab])
                        mm(PA[:, i * 8:(i + 1) * 8], qT[:, i * 128:(i + 1) * 128], kmhl[:, 1, :], False, True, [qb_, gb_], [pab])
                    dve_tt(Gs[:, 8:16, :], PA[:, 64:128].rearrange("p (a b) -> p a b", b=8), gmask[:, 8:16, :], ALU.add, [pab, cb_], [gb_])
                    for i in range(8, 16):
                        k.op("dve", lambda h, i=i: h.max(out=sm_[:, 8:16], in_=Gs[:, i, :]), r=[gb_], w=[tb])
                        dve_ts(sel[:, i, :], Gs[:, i, :], sm_[:, 10:11], None, ALU.is_ge, None, [gb_, tb], [gb_])
                    if cfg.get("a_stop") == "gate":
                        continue
                    for i in range(cfg.get("a_tiles", 16)):
                        j = i // 2
                        nk = (i + 1) * 128
                        t0 = i * 128
                        qs_ = qT[:, t0:t0 + 128]
                        nb_ = (nk + 511) // 512
                        for c in range(nb_):
                            w_ = min(512, nk - c * 512)
                            mm(P4[:, c, 0:w_], qs_, kT[:, c * 512:c * 512 + w_], True, True, [qb_, kb_], [p4b[c]])
                        banks = p4b[0:nb_]
                        if i == 0:
                            dve_tt(P4f[:, 0:128], P4f[:, 0:128], BW[:, hd, 128:256], ALU.add, banks + [cb_], banks)
                        else:
                            dve_tt(P4f[:, t0 - 128:t0], P4f[:, t0 - 128:t0], BW[:, hd, 0:128], ALU.add, banks + [cb_], banks)
                            dve_tt(P4f[:, t0:t0 + 128], P4f[:, t0:t0 + 128], BW[:, hd, 128:256], ALU.add, banks + [cb_], banks)
                        k.op("dve", lambda h, nk=nk: h.tensor_reduce(out=sm_[:, 0:1], in_=P4f[:, 0:nk], axis=AX.X, op=ALU.max, negate=True), r=banks, w=[tb])
                        if j < 4:
                            act(Pm[:, 0:nk], P4f[:, 0:nk], AF.Exp, banks + [tb], [pmb], bias=sm_[:, 0:1])
                        else:
                            dve_ts(sm_[:, 1:2], sm_[:, 0:1], NEG, None, ALU.add, None, [tb], [tb])
                            dve_ts(sm_[:, 16:16 + j], sel[:, i, 0:j], -NEG, sm_[:, 1:2], ALU.mult, ALU.add, [gb_, tb], [tb])
                            for n in range(j + 1):
                                c0, c1 = n * 256, min(n * 256 + 256, nk)
                                bcol = sm_[:, 16 + n:17 + n] if n < j else sm_[:, 0:1]
                                act(Pm[:, c0:c1], P4f[:, c0:c1], AF.Exp, banks + [tb], [pmb], bias=bcol)
                        for kt in range(i + 1):
                            tr(PT[:, kt % 8, :], Pm[:, kt * 128:(kt + 1) * 128], identb[:], [pmb], [ptb])
                            if kt % 8 == 7 or kt == i:
                                k0 = (kt // 8) * 8
                                act(PTs[:, k0:kt + 1, :], PT[:, 0:kt + 1 - k0, :], AF.Copy, [ptb], [vb_])
                        for kt in range(i + 1):
                            mm(PB[:, 0:129], PTs[:, kt, :], vaug[:, kt, 0:129], kt == 0, kt == i, [vb_, vab], [pbb])
                        k.op("dve", lambda h: h.reciprocal(out=sm_[:, 2:3], in_=PB[:, 128:129]), r=[pbb], w=[tb])
                        dve_ts(oh[:], PB[:, 0:128], sm_[:, 2:3], None, ALU.mult, None, [pbb, tb], [tb])
                        tr(PT2[:, 0, :], oh[:], identb[:], [tb], [pt2b])
                        ms = P.mT[:, hd, t0:t0 + 128]
                        dve_tt(ms, PT2[:, 0, :], ms, ALU.mult, [pt2b, P.mb[hd]], [P.mb[hd]])

        def sample_moba(e):
            scale = 128.0 ** -0.5
            for hd in range(8):
                proj("sq", "x", [Sm], lambda st, gi, g0, gs, pin, pbuf, hd=hd: act(qs[:, hd, :], pin, AF.Copy, [pbuf], [qkvb], scale=scale))

                def cbk(st, gi, g0, gs, pin, pbuf, hd=hd):
                    with nc.allow_non_contiguous_dma(reason="tiny"):
                        store(d["skT"][e, hd * 128:(hd + 1) * 128, :], pin, pbuf)
                    act(ks[:, hd, :], pin, AF.Copy, [pbuf], [qkvb])
                proj("sk", "x", [Sm], cbk)

                def cbv(st, gi, g0, gs, pin, pbuf, hd=hd):
                    with nc.allow_non_contiguous_dma(reason="tiny"):
                        store(d["svT"][e, hd * 128:(hd + 1) * 128, :], pin, pbuf)
                    act(vs[:, hd, :], pin, AF.Copy, [pbuf], [qkvb])
                proj("sv", "x", [Sm], cbv)
            if not do_moba_s:
                k.op("dve", lambda h: h.memset(attnS[:], 0.0), w=[attnSb])
                return
            with Phase() as ph:
                Sall = big[0:64, :]
                Pall = big[0:64, :].bitcast(BF16)
                sab = Buf("Sall")
                Kpg = ph.sb("Kpg", [128, 4, 1024])
                kpb = [Buf(f"kpg{i}") for i in range(4)]
                kpd = [k.dsem(f"d_kpg{e}_{i}") for i in range(4)]
                Vpg = Kpg[:, 2:4, :].rearrange("p a b -> p (a b)").bitcast(BF16).rearrange("p (a b) -> p a b", b=1024)
                vpb = [kpb[2], kpb[2], kpb[3], kpb[3]]
                vpb = [Buf(f"vpg{i}") for i in range(4)]
                vpd = kpd
                KT = ph.sb("KT", [128, 2, 256], BF16)
                ktb = [Buf("KT0"), Buf("KT1")]
                QP = ph.sb("QP", [128, 8, 64], BF16)
                ksS = ph.sb("ksS", [128, 8, 64])
                kmS = ph.sb("kmS", [128, 8, 2, 64], BF16)
                ptf = ph.sb("ptf", [128, 128])
                pti = ph.sb("pti", [128, 128], I32)
                idx = ph.sb("idx", [128, 128], I32)
                gS = ph.sb("gS", [64, 64])
                selS = ph.sb("selS", [64, 64])
                bcS = ph.sb("bcS", [64, 64])
                b63 = ph.sb("b63", [64, 256])
                bown = ph.sb("bown", [64, 8])
                cms = ph.sb("cms", [64, 8])
                c31s = ph.sb("c31s", [64, 1])
                bdg = Kpg[0:64, 1, :].rearrange("p (h d) -> p h d", d=128)
                Sown = ph.sb("Sown", [64, 8])
                Pown = ph.sb("Pown", [64, 8], BF16)
                PownT = ph.sb("PownT", [8, 64], BF16)
                dens = ph.sb("dens", [64, 66])
                smS = ph.sb("smS", [64, 16])
                PTp = ph.sb("PTp", [128, 2, 64], BF16)
                ptpb = [Buf("ptp0"), Buf("ptp1")]
                vstm = ph.sb("vstm", [8, 1024], BF16)
                Of = Kpg[0:64, 0, :].rearrange("p (h d) -> p h d", d=128)
                Od = ph.sb("Od", [64, 128])
                Odb = ph.sb("Odb", [64, 128], BF16)
                tb = Buf("smS")
                with nc.allow_non_contiguous_dma(reason="bcast"):
                    k.dma("sp", dmisc, lambda h: h.dma_start(out=pti[:], in_=d["pt"][0:1, :].partition_broadcast(128)), w=[tb])
                    k.dma("sp", dmisc, lambda h: h.dma_start(out=b63[:], in_=d["rels63"]), w=[tb])
                    k.dma("sp", dmisc, lambda h: h.dma_start(out=bown[:], in_=d["relso"]), w=[tb])
                    k.dma("sp", dmisc, lambda h: h.dma_start(out=cms[:], in_=d["c_cmasks"]), w=[tb])
                    k.dma("sp", dmisc, lambda h: h.dma_start(out=c31s[:], in_=d["c31s"]), w=[tb])
                dve_cp(ptf[:], pti[:], [tb], [tb])
                dve_ts(ptf[:], ptf[:], float(e * npool), 128.0, ALU.add, ALU.mult, [tb], [tb])
                dve_ts(ptf[:], ptf[:], iota[:, 0:1], None, ALU.add, None, [tb, cb_], [tb])
                dve_cp(idx[:], ptf[:], [tb], [tb])
                dve_ts(b63[:], b63[:], c31s[:, 0:1], None, ALU.subtract, None, [tb], [tb])
                dve_stt(bown[:], bown[:], c31s[:, 0:1], cms[:], ALU.subtract, ALU.add, [tb], [tb])
                k.op("dve", lambda h: h.memset(QP[:], 0.0), w=[tb])
                for hd in range(8):
                    dve_cp(QP[:, hd, hd * 8:(hd + 1) * 8], qs[:, hd, :], [qkvb, tb], [tb])
                idxb = Buf("idx")
                ksb = Buf("ksS")
                dve_cp(idx[:, 0:1], idx[:, 0:1], [tb], [idxb])

                def kload(n):
                    for half in range(2):
                        pg = 2 * n + half
                        bi_ = (n % 2) * 2 + half
                        k.dma("pool", kpd[bi_], lambda h, bi_=bi_, pg=pg: h.indirect_dma_start(
                            out=Kpg[:, bi_, :], out_offset=None, in_=d["ck"],
                            in_offset=bass.IndirectOffsetOnAxis(ap=idx[:, pg:pg + 1], axis=0)), r=[idxb], w=[kpb[bi_]])
                kload(0)
                for n in range(64):
                    if n + 1 < 64:
                        kload(n + 1)
                    for hd in range(8):
                        bk_ = hd % 4
                        for half in range(2):
                            bi_ = (n % 2) * 2 + half
                            tr(P4[:, bk_, half * 128:(half + 1) * 128], Kpg[:, bi_, hd * 128:(hd + 1) * 128], identf[:], [kpb[bi_], cb_], [p4b[bk_]])
                        act(KT[:, hd % 2, :], P4[:, bk_, 0:256], AF.Copy, [p4b[bk_]], [ktb[hd % 2], ksb], accum=ksS[:, hd, n:n + 1])
                        mm(PA[0:64, 0:256], QP[:, hd, :], KT[:, hd % 2, :], hd == 0, hd == 7, [tb, ktb[hd % 2]], [pab])
                    dve_cp(Sall[:, n * 256:(n + 1) * 256], PA[0:64, 0:256], [pab], [sab])
                dve_cp(ksS[:, 0, 0:1], ksS[:, 0, 0:1], [ksb, tb], [tb])
                for hd in range(8):
                    dve_cp(kmS[:, hd, 0, :], ksS[:, hd, :], [tb], [tb])
                    dve_tt(kmS[:, hd, 1, :], ksS[:, hd, :], kmS[:, hd, 0, :], ALU.subtract, [tb], [tb])
                for hd in range(8):
                    mm(PB[0:64, 0:64], QP[:, hd, :], kmS[:, hd, 0, :], hd == 0, False, [tb], [pbb])
                    mm(PB[0:64, 0:64], QP[:, hd, :], kmS[:, hd, 1, :], False, hd == 7, [tb], [pbb])
                dve_cp(gS[:], PB[0:64, 0:64], [pbb], [tb])
                k.op("dve", lambda h: h.max(out=smS[:, 8:16], in_=gS[:]), r=[tb], w=[tb])
                dve_ts(selS[:], gS[:], smS[:, 10:11], None, ALU.is_ge, None, [tb], [tb])
                for hd in range(8):
                    mm(PB[0:64, 64:72], QP[:, hd, :], ks[:, hd, :], hd == 0, hd == 7, [tb, qkvb], [pbb])
                dve_tt(Sown[:], PB[0:64, 64:72], bown[:], ALU.add, [pbb, tb], [tb])
                dve_tt(Sall[:, 63 * 256:64 * 256], Sall[:, 63 * 256:64 * 256], b63[:], ALU.add, [sab, tb], [sab])
                k.op("dve", lambda h: h.tensor_reduce(out=smS[:, 0:1], in_=Sall[:, :], axis=AX.X, op=ALU.max), r=[sab], w=[tb])
                k.op("dve", lambda h: h.tensor_reduce(out=smS[:, 1:2], in_=Sown[:], axis=AX.X, op=ALU.max), r=[tb], w=[tb])
                dve_tt(smS[:, 0:1], smS[:, 0:1], smS[:, 1:2], ALU.max, [tb], [tb])
                dve_ts(smS[:, 0:1], smS[:, 0:1], -1.0, None, ALU.mult, None, [tb], [tb])
                dve_ts(smS[:, 1:2], smS[:, 0:1], NEG, None, ALU.add, None, [tb], [tb])
                dve_ts(bcS[:], selS[:], -NEG, smS[:, 1:2], ALU.mult, ALU.add, [tb], [tb])
                k.op("dve", lambda h: h.memset(dens[:], 0.0), w=[tb])
                for n in range(64):
                    act(Pall[:, n * 256:(n + 1) * 256], Sall[:, n * 256:(n + 1) * 256], AF.Exp, [sab, tb], [sab], bias=bcS[:, n:n + 1], accum=dens[:, n:n + 1])
                act(Pown[:], Sown[:], AF.Exp, [tb], [tb], bias=smS[:, 0:1], accum=dens[:, 64:65])
                k.op("dve", lambda h: h.tensor_reduce(out=smS[:, 2:3], in_=dens[:, 0:65], axis=AX.X, op=ALU.add), r=[tb], w=[tb])
                k.op("dve", lambda h: h.reciprocal(out=smS[:, 3:4], in_=smS[:, 2:3]), r=[tb], w=[tb])
                k.op("dve", lambda h: h.memset(smS[:, 15:16], 0.0), w=[kpb[2], kpb[3]] + vpb)

                def vload(pg):
                    b4 = pg % 4
                    k.dma("pool", vpd[b4], lambda h, b4=b4, pg=pg: h.indirect_dma_start(
                        out=Vpg[:, b4, :], out_offset=None, in_=d["cv"],
                        in_offset=bass.IndirectOffsetOnAxis(ap=idx[:, pg:pg + 1], axis=0)), r=[idxb], w=[vpb[b4]])
                for pg in range(3):
                    vload(pg)
                for pg in range(128):
                    b2 = pg % 2
                    b4 = pg % 4
                    if pg + 3 < 128:
                        vload(pg + 3)
                    tr(PT[:, b2, 0:64], Pall[:, pg * 128:(pg + 1) * 128], identb[0:64, 0:64], [sab, cb_], [ptb if b2 == 0 else pt2b])
                    act(PTp[:, b2, :], PT[:, b2, 0:64], AF.Copy, [ptb if b2 == 0 else pt2b], [ptpb[b2]])
                    for c in range(2):
                        mm(P4[0:64, c, :], PTp[:, b2, :], Vpg[:, b4, c * 512:(c + 1) * 512], pg == 0, False, [ptpb[b2], vpb[b4]], [p4b[c]])
                k.dma("sp", dmisc, lambda h: h.dma_start(out=bdg, in_=d["c_bdiag"]), r=[kpb[1]], w=[kpb[1]])
                tr(PT2[0:8, 0, 0:64], Pown[:], identb[0:64, 0:64], [tb, cb_], [pt2b])
                act(PownT[:], PT2[0:8, 0, 0:64], AF.Copy, [pt2b], [tb])
                for hd in range(8):
                    tr(PT[0:8, hd, :], vs[:, hd, :], identb[:], [qkvb, cb_], [ptb])
                act(vstm[:], PT[0:8, :, :].rearrange("p a b -> p (a b)"), AF.Copy, [ptb], [tb])
                for c in range(2):
                    mm(P4[0:64, c, :], PownT[:], vstm[:, c * 512:(c + 1) * 512], False, True, [tb], [p4b[c]])
                dve_tt(Of, P4[0:64, 0:2, :].rearrange("p a (b c) -> p (a b) c", c=128), bdg, ALU.mult, [p4b[0], p4b[1], tb, kpb[1], kpb[0]], [kpb[0]])
                k.op("dve", lambda h: h.tensor_reduce(out=Od[:], in_=Of.rearrange("p h d -> p d h"), axis=AX.X, op=ALU.add), r=[kpb[0]], w=[tb])
                dve_ts(Odb[:], Od[:], smS[:, 3:4], None, ALU.mult, None, [tb], [tb])
                tr(PT2[:, 1, 0:64], Odb[:], identb[0:64, 0:64], [tb, cb_], [pt2b])
                dve_cp(attnS[:], PT2[:, 1, 0:64].rearrange("p (h q) -> p h q", q=8), [pt2b], [attnSb])

        def even_layer(e):
            stop = cfg.get("even_stop")
            gates_phase(e)
            if stop == "gates":
                return False
            if do_sample:
                sample_moba(e)
            if stop == "smoba":
                return False
            a_phase(e)
            if stop == "a":
                return False
            b_phase(e)
            return True

        for li, l in enumerate(layers):
            last = li == len(layers) - 1
            if l % 2 == 0:
                if not even_layer(l // 2):
                    break
            else:
                odd_layer(l // 2)
            out_ln(l, last)
        k.barrier()
    return nc


def _consts():
    c = {}
    c["c_identf"] = np.eye(128, dtype=np.float32)
    s_ = np.arange(128)
    c["c_trilT"] = (s_[:, None] <= s_[None, :]).astype(np.float32)
    ql = np.arange(128)[:, None]
    kw = np.arange(256)[None, :]
    dd = ql + 128 - kw
    c["c_cmask"] = np.where(dd >= 0, 0.0, NEG).astype(np.float32)
    gm = np.zeros((128, 16, 8), np.float32)
    for i in range(16):
        gm[:, i, (i // 2):] = -1e30
    c["c_gmask"] = gm
    c["c_iota"] = np.arange(128, dtype=np.float32)[:, None]
    bd = np.zeros((64, 8, 128), np.float32)
    for p in range(64):
        bd[p, p // 8, :] = 1.0
    c["c_bdiag"] = bd
    qq = (np.arange(64) % 8)[:, None]
    kk = np.arange(8)[None, :]
    c["c_cmasks"] = np.where(kk <= qq, 0.0, NEG).astype(np.float32)
    return c


def prep_inputs(inp, core, cfg):
    npool = cfg.get("npool", NPOOL)
    b = core % 4
    s = core
    m = {}
    m["xpT"] = np.ascontiguousarray(inp["x_prompt"][b].T)
    m["xsT"] = np.ascontiguousarray(inp["x_sample"][s].T)
    m["ck"] = inp["cache_k"].reshape(2 * npool * 128, 1024)
    m["cv"] = inp["cache_v"].reshape(2 * npool * 128, 1024)
    m["pt"] = np.ascontiguousarray(inp["page_table"][s:s + 1]).astype(np.int32)
    m["sC"] = np.ascontiguousarray(inp["state_C"][:, s])
    m["sn"] = np.ascontiguousarray(inp["state_n"][:, s]).reshape(2, 4, 256, 1)
    m["sm"] = np.ascontiguousarray(inp["state_m"][:, s]).reshape(2, 1, 4)
    m["sconv"] = np.ascontiguousarray(inp["state_conv"][:, s].reshape(2, 2, 16, 128).transpose(0, 3, 2, 1))
    m["wie"] = inp["w_in_even"]
    m["woe"] = inp["w_out_even"]
    m["gbrep"] = np.ascontiguousarray(np.broadcast_to(inp["mlstm_gate_bias"][None], (128, 2, 8)))
    m["ngT"] = np.ascontiguousarray(inp["mlstm_norm_g"].reshape(2, 8, 128).transpose(0, 2, 1))
    rb = inp["rel_bias"]
    ql = np.arange(128)[:, None]
    kw = np.arange(256)[None, :]
    bk = t5_bucket_np(ql + 128 - kw)
    m["relp"] = np.ascontiguousarray(rb[:, bk].transpose(1, 0, 2))
    hq = np.arange(64)
    dist63 = 256 + (hq % 8)[:, None] - np.arange(256)[None, :]
    m["rels63"] = np.ascontiguousarray(rb[(hq // 8)[:, None], t5_bucket_np(dist63)])
    disto = (hq % 8)[:, None] - np.arange(8)[None, :]
    m["relso"] = np.ascontiguousarray(rb[(hq // 8)[:, None], t5_bucket_np(disto)])
    m["c31"] = np.ascontiguousarray(np.broadcast_to(rb[:, 31][None, :], (128, 8)))
    m["c31s"] = np.ascontiguousarray(rb[hq // 8, 31][:, None])
    m["wio"] = inp["w_in_odd"]
    m["woo"] = inp["w_out_odd"]
    m["cwT"] = np.ascontiguousarray(inp["conv_w"].reshape(2, 3, 16, 128).transpose(0, 3, 2, 1))
    m["lngT"] = np.ascontiguousarray(inp["ln_g"].reshape(4, 16, 128).transpose(0, 2, 1))
    m["lnbT"] = np.ascontiguousarray(inp["ln_b"].reshape(4, 16, 128).transpose(0, 2, 1))
    m.update(_consts())
    return {kk: np.ascontiguousarray(vv) for kk, vv in m.items()}


_NC_CACHE = {}


def kernel(**inputs):
    cfg = {}
    inp = {kk: np.asarray(vv) for kk, vv in inputs.items()}
    if "full" not in _NC_CACHE:
        _NC_CACHE["full"] = build(cfg)
    nc = _NC_CACHE["full"]
    in_maps = [prep_inputs(inp, c, cfg) for c in range(8)]
    res = run_bass_kernel_spmd(nc, in_maps, core_ids=list(range(8))).results
    f = np.float32
    y_prompt = np.stack([res[b]["ypT"].T for b in range(4)]).astype(f)
    y_sample = np.stack([res[s]["ysT"].T for s in range(8)]).astype(f)

    def kv(name, cores, n):
        return np.stack([res[c][name].transpose(0, 2, 1).reshape(2, n, 8, 128) for c in cores], axis=1).astype(f)

    def cst(name, cores):
        a = np.stack([res[c][name] for c in cores], axis=1)
        a = a.transpose(0, 1, 2, 4, 3, 5).reshape(2, len(cores), 4, 256, 257)
        return np.ascontiguousarray(a[..., :256]).astype(f), np.ascontiguousarray(a[..., 256]).astype(f)

    def conv(name, cores):
        a = np.stack([res[c][name] for c in cores], axis=1)
        return np.ascontiguousarray(a.transpose(0, 1, 4, 3, 2).reshape(2, len(cores), 2, 2048)).astype(f)

    pc, pn = cst("pCa", range(4))
    sc, sn = cst("sCa", range(8))
    pm = np.stack([res[c]["pm"][:, 0, :] for c in range(4)], axis=1).astype(f)
    sm = np.stack([res[c]["smo"][:, 0, :] for c in range(8)], axis=1).astype(f)
    return (y_prompt, y_sample, kv("pkT", range(4), T), kv("pvT", range(4), T), pc, pn, pm, conv("pbT", range(4)),
            kv("skT", range(8), S), kv("svT", range(8), S), sc, sn, sm, conv("sbT", range(8)))
```

```python
import math
import numpy as np
from contextlib import ExitStack
import concourse.bass as bass
import concourse.mybir as mybir
from concourse.bass_utils import run_bass_kernel_spmd

F32 = mybir.dt.float32
BF16 = mybir.dt.bfloat16
I32 = mybir.dt.int32
AF = mybir.ActivationFunctionType
ALU = mybir.AluOpType
AX = mybir.AxisListType

T = 2048
D = 2048
S = 8
NPOOL = 1280
NPAGES = 128
ALPHA = 8.0 ** 0.25
LN_EPS = 1e-5
NEG = -30000.0
RING = 5


class Buf:
    __slots__ = ("name", "w", "r")

    def __init__(self, name):
        self.name = name
        self.w = {}
        self.r = {}


class Eng:
    def __init__(self, h, sem, selfsync):
        self.h = h
        self.sem = sem
        self.selfsync = selfsync
        self.cnt = 0
        self.waited = {}


class KB:
    def __init__(self, nc, es):
        self.nc = nc
        self.es = es
        self.sems = {}
        self.dtot = {}
        self.eng = {}
        import os as _os
        _ss = _os.environ.get("KSELFSYNC", "1") == "1"
        for n, h, ss in (("pe", nc.tensor, False), ("act", nc.scalar, _ss), ("dve", nc.vector, _ss),
                         ("pool", nc.gpsimd, True), ("sp", nc.sync, False)):
            self.eng[n] = Eng(h, self.mksem("e_" + n), ss)

    def mksem(self, n):
        self.sems[n] = self.es.enter_context(self.nc.semaphore(n))
        return n

    def dsem(self, n):
        self.mksem(n)
        self.dtot[n] = 0
        return n

    def _waits(self, e, r, w):
        need = {}
        for b in r:
            for s, v in b.w.items():
                if v > need.get(s, 0):
                    need[s] = v
            if b.name[0] == "#":
                for s, v in b.r.items():
                    if s != e.sem and v > need.get(s, 0):
                        need[s] = v
        for b in w:
            for s, v in b.w.items():
                if v > need.get(s, 0):
                    need[s] = v
            for s, v in b.r.items():
                if s == e.sem:
                    continue
                if v > need.get(s, 0):
                    need[s] = v
        for s, v in need.items():
            if s in self.dtot:
                v = self.dtot[s]
            if s == e.sem and not e.selfsync:
                continue
            if e.waited.get(s, 0) < v:
                e.h.wait_ge(self.sems[s], v)
                e.waited[s] = v

    def op(self, en, fn, r=(), w=()):
        e = self.eng[en]
        self._waits(e, r, w)
        ins = fn(e.h)
        e.cnt += 1
        ins.then_inc(self.sems[e.sem], 1)
        for b in r:
            b.r[e.sem] = e.cnt
        for b in w:
            b.w = {e.sem: e.cnt}
            b.r = {}
        return ins

    def dma(self, qn, ds, fn, r=(), w=()):
        e = self.eng[qn]
        self._waits(e, r, w)
        if self.dtot[ds] > 0 and e.waited.get(ds, 0) < self.dtot[ds]:
            e.h.wait_ge(self.sems[ds], self.dtot[ds])
            e.waited[ds] = self.dtot[ds]
        ins = fn(e.h)
        self.dtot[ds] += 16
        ins.then_inc(self.sems[ds], 16)
        tok = self.dtot[ds]
        for b in r:
            b.r[ds] = tok
        for b in w:
            b.w = {ds: tok}
            b.r = {}
        return ins

    def barrier(self):
        tot = {e.sem: e.cnt for e in self.eng.values()}
        for s, v in self.dtot.items():
            tot[s] = v
        for e in self.eng.values():
            for s, v in tot.items():
                if s == e.sem or v == 0:
                    continue
                if e.waited.get(s, 0) < v:
                    e.h.wait_ge(self.sems[s], v)
                    e.waited[s] = v


class Stream:
    def __init__(self, name, n, groups, L):
        self.name = name
        self.n = n
        self.groups = groups
        self.L = L
        self.nch = n // L


def t5_bucket_np(dist):
    n = np.maximum(dist, 0)
    max_exact = 16
    nf = np.maximum(n, 1).astype(np.float32)
    large = max_exact + (np.log(nf / max_exact) / math.log(128 / max_exact) * (32 - max_exact)).astype(np.int32)
    large = np.minimum(large, 31)
    return np.where(n < max_exact, n, large)


def build(cfg):
    layers = cfg.get("layers", [0, 1, 2, 3])
    do_sample = cfg.get("sample", True)
    do_moba_s = cfg.get("moba_s", True)
    npool = cfg.get("npool", NPOOL)
    nc = bass.Bass("TRN2", target_bir_lowering=False)
    d = {}

    def din(name, shape, dt=F32):
        d[name] = nc.dram_tensor(name, shape, dt, kind="ExternalInput").ap()

    def dout(name, shape, dt=F32):
        d[name] = nc.dram_tensor(name, shape, dt, kind="ExternalOutput").ap()

    din("xpT", [D, T])
    din("xsT", [D, S])
    din("ck", [2 * npool * 128, 1024])
    din("cv", [2 * npool * 128, 1024])
    din("pt", [1, NPAGES], I32)
    din("sC", [2, 4, 256, 256])
    din("sn", [2, 4, 256, 1])
    din("sm", [2, 1, 4])
    din("sconv", [2, 128, 16, 2])
    din("wie", [2, D, 9224])
    din("woe", [2, D, D])
    din("gbrep", [128, 2, 8])
    din("ngT", [2, 128, 8])
    din("relp", [128, 8, 256])
    din("rels63", [64, 256])
    din("relso", [64, 8])
    din("c31", [128, 8])
    din("c31s", [64, 1])
    din("wio", [2, D, 8192])
    din("woo", [2, D, D])
    din("cwT", [2, 128, 16, 3])
    din("lngT", [4, 128, 16])
    din("lnbT", [4, 128, 16])
    din("c_identf", [128, 128])
    din("c_trilT", [128, 128])
    din("c_cmask", [128, 256])
    din("c_gmask", [128, 16, 8])
    din("c_iota", [128, 1])
    din("c_bdiag", [64, 8, 128])
    din("c_cmasks", [64, 8])
    dout("ypT", [D, T])
    dout("ysT", [D, S])
    dout("pkT", [2, 1024, T])
    dout("pvT", [2, 1024, T])
    dout("pCa", [2, 4, 128, 2, 257])
    dout("pm", [2, 1, 4])
    dout("pbT", [2, 128, 16, 2])
    dout("skT", [2, 1024, S])
    dout("svT", [2, 1024, S])
    dout("sCa", [2, 4, 128, 2, 257])
    dout("smo", [2, 1, 4])
    dout("sbT", [2, 128, 16, 2])

    es = ExitStack()
    with es:
        k = KB(nc, es)
        uid = {"n": 0}

        def sb(name, shape, dt=F32):
            return es.enter_context(nc.sbuf_tensor("s_" + name, shape, dt))

        def ps(name, shape, dt=F32):
            return es.enter_context(nc.psum_tensor("ps_" + name, shape, dt))

        class Phase:
            def __enter__(self):
                k.barrier()
                self.es = ExitStack()
                self.es.__enter__()
                return self

            def sb(self, name, shape, dt=F32):
                uid["n"] += 1
                return self.es.enter_context(nc.sbuf_tensor(f"t_{name}_{uid['n']}", shape, dt))

            def __exit__(self, *a):
                k.barrier()
                return self.es.__exit__(*a)

        P = Stream("p", T, [(g * 512, 512) for g in range(4)], 128)
        Sm = Stream("s", S, [(0, S)], S)
        streams = [P, Sm] if do_sample else [P]

        big = sb("big", [128, 16384])
        P.mT = big[:].bitcast(BF16).rearrange("p (f t) -> p f t", f=16)
        P.xT = sb("xT", [128, 16, T], BF16)
        Sm.xT = sb("xsT_", [128, 16, S], BF16)
        Sm.mT = sb("msT_", [128, 16, S], BF16)
        for st in [P, Sm]:
            st.xb = [Buf(f"{st.name}x{g}") for g in range(len(st.groups))]
            st.mb = [Buf(f"{st.name}m{f}") for f in range(16)]
            st.accb = [Buf(f"{st.name}acc1_{g}") for g in range(len(st.groups))]
            st.acc2b = [Buf(f"{st.name}acc2_{g}") for g in range(len(st.groups))]
            st.gw = sb(f"gw{st.name}", [128, st.nch * 4])
            st.gf = sb(f"gf{st.name}", [128, st.nch * 4])
            st.gom = sb(f"gom{st.name}", [128, st.nch * 4])
            st.gateb = Buf(f"gate{st.name}")
        ring = sb("ring", [128, RING, 16, 128], BF16)
        ringb = [Buf(f"ring{i}") for i in range(RING)]
        ringd = [k.dsem(f"d_ring{i}") for i in range(RING)]
        work = sb("work", [128, 4, 512])
        workb = [Buf(f"work{i}") for i in range(4)]
        stg = sb("stg", [128, 2, 512])
        stgb = [Buf("stg0"), Buf("stg1")]
        stgd = [k.dsem("d_stg0"), k.dsem("d_stg1")]
        stgi = {"i": 0}
        identf = sb("identf", [128, 128])
        identb = sb("identb", [128, 128], BF16)
        onesf = sb("onesf", [128, 128])
        trilT = sb("trilT", [128, 128])
        BW = sb("BW", [128, 8, 256])
        c31 = sb("c31", [128, 8])
        gmask = sb("gmask", [128, 16, 8])
        gbrep = sb("gbrep", [128, 2, 8])
        ngT = sb("ngT", [128, 2, 8])
        cwT = sb("cwT", [128, 2, 16, 3])
        lngT = sb("lngT", [128, 4, 16])
        lnbT = sb("lnbT", [128, 4, 16])
        iota = sb("iota", [128, 1])
        attnS = sb("attnS", [128, 8, S])
        attnSb = Buf("attnS")
        qs = sb("qs", [128, 8, S], BF16)
        ks = sb("ks", [128, 8, S], BF16)
        vs = sb("vs", [128, 8, S], BF16)
        qkvb = Buf("qkvs")
        cb_ = Buf("consts")
        dconst = k.dsem("d_const")
        dmisc = k.dsem("d_misc")

        P4 = ps("P4", [128, 4, 512])
        p4b = [Buf(f"#p4_{i}") for i in range(4)]
        PT = ps("PT", [128, 8, 128], BF16)
        ptb = Buf("#PT")
        PT2 = ps("PT2", [128, 8, 128], BF16)
        pt2b = Buf("#PT2")
        PA = ps("PA", [128, 512])
        pab = Buf("#PA")
        PB = ps("PB", [128, 512])
        pbb = Buf("#PB")
        P4f = P4[:].rearrange("p a b -> p (a b)")

        def cload(dst, src):
            k.dma("sp", dconst, lambda h, dst=dst, src=src: h.dma_start(out=dst, in_=src), w=[cb_])

        cload(identf[:], d["c_identf"])
        cload(trilT[:], d["c_trilT"])
        cload(BW[:], d["relp"])
        cload(c31[:], d["c31"])
        cload(gmask[:], d["c_gmask"])
        cload(gbrep[:], d["gbrep"])
        cload(iota[:], d["c_iota"])
        for e in range(2):
            cload(ngT[:, e, :], d["ngT"][e])
            cload(cwT[:, e, :, :], d["cwT"][e])
        for l in range(4):
            cload(lngT[:, l, :], d["lngT"][l])
            cload(lnbT[:, l, :], d["lnbT"][l])
        k.op("dve", lambda h: h.tensor_copy(out=identb[:], in_=identf[:]), r=[cb_], w=[cb_])
        k.op("dve", lambda h: h.memset(onesf[:], 1.0), w=[cb_])
        cm = work[:, 0, 0:256]
        k.dma("sp", dconst, lambda h: h.dma_start(out=cm, in_=d["c_cmask"]), w=[workb[0]])
        for hh in range(8):
            k.op("dve", lambda h, hh=hh: h.scalar_tensor_tensor(out=BW[:, hh, :], in0=BW[:, hh, :], scalar=c31[:, hh:hh + 1],
                                                              in1=cm, op0=ALU.subtract, op1=ALU.add), r=[cb_, workb[0]], w=[cb_])

        dxin = k.dsem("d_xin")
        for kc in range(16):
            k.dma("pool", dxin, lambda h, kc=kc: h.dma_start(out=P.xT[:, kc, :], in_=d["xpT"][kc * 128:(kc + 1) * 128, :]), w=P.xb)
        if do_sample:
            with nc.allow_non_contiguous_dma(reason="tiny"):
                k.dma("pool", dxin, lambda h: h.dma_start(out=Sm.xT[:], in_=d["xsT"].rearrange("(kc p) s -> p kc s", p=128)), w=Sm.xb)

        plan = []

        def plan_even(e):
            W = d["wie"][e]
            pl = [("gif", W[:, 9216:9224], 8)]
            if do_sample:
                for h_ in range(8):
                    for nm, off in (("sq", 0), ("sk", 1024), ("sv", 2048)):
                        pl.append((nm, W[:, off + h_ * 128: off + (h_ + 1) * 128], 128))
            for h_ in range(8):
                for nm, off in (("ag", 3072), ("aq", 0), ("ak", 1024), ("av", 2048)):
                    pl.append((nm, W[:, off + h_ * 128: off + (h_ + 1) * 128], 128))
            for h_ in range(4):
                for nm, off in (("bo", 7168), ("bg", 8192), ("bq", 4096), ("bk", 5120), ("bv", 6144)):
                    for dc in range(2):
                        c0 = off + h_ * 256 + dc * 128
                        pl.append((nm, W[:, c0:c0 + 128], 128))
            for fb in range(16):
                pl.append(("o", d["woe"][e][:, fb * 128:(fb + 1) * 128], 128))
            return pl

        def plan_odd(o):
            W = d["wio"][o]
            pl = []
            for fb in range(16):
                for nm, off in (("cc", 2048), ("cx", 4096), ("cb", 0), ("cg", 6144)):
                    pl.append((nm, W[:, off + fb * 128: off + (fb + 1) * 128], 128))
            for fb in range(16):
                pl.append(("o", d["woo"][o][:, fb * 128:(fb + 1) * 128], 128))
            return pl

        for l in layers:
            plan += plan_even(l // 2) if l % 2 == 0 else plan_odd(l // 2)
        wstate = {"loaded": 0, "used": 0}

        def wprefetch(upto):
            while wstate["loaded"] < min(upto, len(plan)):
                i = wstate["loaded"]
                tag, ap, ncols = plan[i]
                sl = i % RING
                with nc.allow_non_contiguous_dma(reason="weight column block"):
                    k.dma("pool", ringd[sl],
                          lambda h, ap=ap, sl=sl, ncols=ncols: h.dma_start(out=ring[:, sl, :, 0:ncols], in_=ap.rearrange("(kc p) c -> p kc c", p=128)),
                          w=[ringb[sl]])
                wstate["loaded"] += 1

        def wnext(tag):
            i = wstate["used"]
            assert plan[i][0] == tag, (plan[i][0], tag, i)
            wprefetch(i + RING)
            wstate["used"] += 1
            sl = i % RING
            return ring[:, sl, :, :], ringb[sl], plan[i][2]

        def proj(tag, src, sts, cb):
            wap, wbuf, ncols = wnext(tag)
            for st in sts:
                srcT = st.xT if src == "x" else st.mT
                for gi, (g0, gs) in enumerate(st.groups):
                    if st is P:
                        pout, pbuf = P4[0:ncols, gi, 0:gs], p4b[gi]
                    else:
                        pout, pbuf = PB[0:ncols, 0:gs], pbb
                    for kc in range(16):
                        rb = [wbuf, st.xb[gi] if src == "x" else st.mb[kc]]
                        k.op("pe", lambda h, pout=pout, kc=kc, g0=g0, gs=gs, srcT=srcT: h.matmul(
                            out=pout, lhsT=wap[:, kc, 0:ncols], rhs=srcT[:, kc, g0:g0 + gs], start=(kc == 0), stop=(kc == 15)),
                            r=rb, w=[pbuf])
                    cb(st, gi, g0, gs, pout, pbuf)

        def act(out, in_, func, r, w, scale=1.0, bias=None, accum=None):
            kw = {}
            if bias is not None:
                kw["bias"] = bias
            if accum is not None:
                kw["accum_out"] = accum
            return k.op("act", lambda h: h.activation(out=out, in_=in_, func=func, scale=scale, **kw), r=r, w=w)

        def dve_tt(out, in0, in1, op, r, w):
            return k.op("dve", lambda h: h.tensor_tensor(out=out, in0=in0, in1=in1, op=op), r=r, w=w)

        def dve_ts(out, in0, s1, s2, op0, op1, r, w):
            if s2 is None:
                return k.op("dve", lambda h: h.tensor_scalar(out=out, in0=in0, scalar1=s1, scalar2=None, op0=op0), r=r, w=w)
            return k.op("dve", lambda h: h.tensor_scalar(out=out, in0=in0, scalar1=s1, scalar2=s2, op0=op0, op1=op1), r=r, w=w)

        def dve_stt(out, in0, sc, in1, op0, op1, r, w):
            return k.op("dve", lambda h: h.scalar_tensor_tensor(out=out, in0=in0, scalar=sc, in1=in1, op0=op0, op1=op1), r=r, w=w)

        def dve_cp(out, in_, r, w):
            return k.op("dve", lambda h: h.tensor_copy(out=out, in_=in_), r=r, w=w)

        def mm(out, lhsT, rhs, start, stop, r, w):
            return k.op("pe", lambda h: h.matmul(out=out, lhsT=lhsT, rhs=rhs, start=start, stop=stop), r=r, w=w)

        def tr(out, in_, ident, r, w):
            return k.op("pe", lambda h: h.transpose(out=out, in_=in_, identity=ident), r=r, w=w)

        def store(dst, src_psum, pbuf):
            si = stgi["i"] % 2
            stgi["i"] += 1
            np_, nf = src_psum.shape[0], src_psum.shape[1]
            so = stg[0:np_, si, 0:nf]
            dve_cp(so, src_psum, [pbuf], [stgb[si]])
            k.dma("sp", stgd[si], lambda h: h.dma_start(out=dst, in_=so), r=[stgb[si]])

        def out_ln(layer, last):
            with Phase() as ph:
                P.acc1 = ph.sb("acc1", [128, T])
                P.acc2 = ph.sb("acc2", [128, T])
                Sm.acc1 = ph.sb("acc1s", [128, S])
                Sm.acc2 = ph.sb("acc2s", [128, S])
                out_ln_(layer, last)

        def out_ln_(layer, last):
            for fb in range(16):
                def cb(st, gi, g0, gs, pin, pbuf, fb=fb):
                    xs_ = st.xT[:, fb, g0:g0 + gs]
                    dve_stt(xs_, xs_, ALPHA, pin, ALU.mult, ALU.add, [pbuf, st.xb[gi]], [st.xb[gi]])
                    a1 = st.acc1[:, g0:g0 + gs]
                    a2 = st.acc2[:, g0:g0 + gs]
                    wi = 3 if st is P else 2
                    sq = work[:, wi, 0:gs]
                    act(sq, xs_, AF.Square, [st.xb[gi]], [workb[wi]])
                    if fb == 0:
                        dve_cp(a1, xs_, [st.xb[gi]], [st.accb[gi]])
                        k.op("pool", lambda h: h.tensor_copy(out=a2, in_=sq), r=[workb[wi]], w=[st.acc2b[gi]])
                    else:
                        dve_tt(a1, a1, xs_, ALU.add, [st.xb[gi], st.accb[gi]], [st.accb[gi]])
                        k.op("pool", lambda h: h.tensor_tensor(out=a2, in0=a2, in1=sq, op=ALU.add), r=[workb[wi], st.acc2b[gi]], w=[st.acc2b[gi]])
                proj("o", "mix", streams, cb)
            for st in streams:
                for gi, (g0, gs) in enumerate(st.groups):
                    mm(PA[:, 0:gs], onesf[:], st.acc1[:, g0:g0 + gs], True, True, [st.accb[gi], cb_], [pab])
                    mm(PB[:, 0:gs], onesf[:], st.acc2[:, g0:g0 + gs], True, True, [st.acc2b[gi], cb_], [pbb])
                    meanb = work[:, 0, 0:gs]
                    rstdb = work[:, 1, 0:gs]
                    act(meanb, PA[:, 0:gs], AF.Copy, [pab], [workb[0]], scale=1.0 / D)
                    dve_tt(rstdb, meanb, meanb, ALU.mult, [workb[0]], [workb[1]])
                    dve_stt(rstdb, PB[:, 0:gs], 1.0 / D, rstdb, ALU.mult, ALU.subtract, [pbb, workb[1]], [workb[1]])
                    dve_ts(rstdb, rstdb, LN_EPS, None, ALU.add, None, [workb[1]], [workb[1]])
                    act(rstdb, rstdb, AF.Sqrt, [workb[1]], [workb[1]])
                    k.op("dve", lambda h: h.reciprocal(out=rstdb, in_=rstdb), r=[workb[1]], w=[workb[1]])
                    for fb in range(16):
                        xs_ = st.xT[:, fb, g0:g0 + gs]
                        tmp = work[:, 2, 0:gs]
                        dve_tt(tmp, xs_, meanb, ALU.subtract, [st.xb[gi], workb[0]], [workb[2]])
                        dve_tt(tmp, tmp, rstdb, ALU.mult, [workb[2], workb[1]], [workb[2]])
                        if not last:
                            act(xs_, tmp, AF.Identity, [workb[2], cb_], [st.xb[gi]], scale=lngT[:, layer, fb:fb + 1], bias=lnbT[:, layer, fb:fb + 1])
                        else:
                            si = stgi["i"] % 2
                            stgi["i"] += 1
                            so = stg[:, si, 0:gs]
                            act(so, tmp, AF.Identity, [workb[2], cb_], [stgb[si]], scale=lngT[:, layer, fb:fb + 1], bias=lnbT[:, layer, fb:fb + 1])
                            dst = d["ypT" if st is P else "ysT"][fb * 128:(fb + 1) * 128, g0:g0 + gs]
                            with nc.allow_non_contiguous_dma(reason="small"):
                                k.dma("sp", stgd[si], lambda h, so=so, dst=dst: h.dma_start(out=dst, in_=so), r=[stgb[si]])

        def odd_layer(o):
            with Phase() as ph:
                Uc = {P: ph.sb("Ucp", [128, T]), Sm: ph.sb("Ucs", [128, S])}
                U = {P: ph.sb("Up", [128, T + 2]), Sm: ph.sb("Us", [128, S + 2])}
                Y = {P: ph.sb("Yp", [128, T]), Sm: ph.sb("Ys", [128, S])}
                ucb = {P: Buf("ucp"), Sm: Buf("ucs")}
                ub = {P: Buf("up"), Sm: Buf("us")}
                yb = {P: Buf("yp"), Sm: Buf("ys")}
                for fb in range(16):
                    proj("cc", "x", streams, lambda st, gi, g0, gs, pin, pbuf: act(Uc[st][:, g0:g0 + gs], pin, AF.Copy, [pbuf], [ucb[st]]))
                    for st in streams:
                        if st is P:
                            k.op("dve", lambda h: h.memset(U[P][:, 0:2], 0.0), w=[ub[P]])
                        else:
                            k.dma("sp", dmisc, lambda h, fb=fb: h.dma_start(out=U[Sm][:, 0:2], in_=d["sconv"][o, :, fb, :]), w=[ub[Sm]])
                    proj("cx", "x", streams, lambda st, gi, g0, gs, pin, pbuf: dve_tt(U[st][:, 2 + g0:2 + g0 + gs], pin, Uc[st][:, g0:g0 + gs], ALU.mult, [pbuf, ucb[st]], [ub[st]]))
                    for st in streams:
                        n = st.n
                        dve_ts(Y[st][:, :], U[st][:, 2:2 + n], cwT[:, o, fb, 2:3], None, ALU.mult, None, [ub[st], cb_], [yb[st]])
                        dve_stt(Y[st][:, :], U[st][:, 1:1 + n], cwT[:, o, fb, 1:2], Y[st][:, :], ALU.mult, ALU.add, [ub[st], yb[st], cb_], [yb[st]])
                        dve_stt(Y[st][:, :], U[st][:, 0:n], cwT[:, o, fb, 0:1], Y[st][:, :], ALU.mult, ALU.add, [ub[st], yb[st], cb_], [yb[st]])
                        dst = d["pbT" if st is P else "sbT"][o, :, fb, :]
                        with nc.allow_non_contiguous_dma(reason="tiny"):
                            k.dma("sp", dmisc, lambda h, dst=dst, st=st, n=n: h.dma_start(out=dst, in_=U[st][:, n:n + 2]), r=[ub[st]])
                    proj("cb", "x", streams, lambda st, gi, g0, gs, pin, pbuf: dve_tt(Y[st][:, g0:g0 + gs], pin, Y[st][:, g0:g0 + gs], ALU.mult, [pbuf, yb[st]], [yb[st]]))

                    def cbg(st, gi, g0, gs, pin, pbuf, fb=fb):
                        wi = 0 if st is P else 1
                        act(work[:, wi, 0:gs], pin, AF.Silu, [pbuf], [workb[wi]])
                        dve_tt(st.mT[:, fb, g0:g0 + gs], work[:, wi, 0:gs], Y[st][:, g0:g0 + gs], ALU.mult, [workb[wi], yb[st]], [st.mb[fb]])
                    proj("cg", "x", streams, cbg)

        def gate_math(ph, st, e):
            L, nch = st.L, st.nch
            n64 = nch * 4
            gb = st.gateb
            tA = ph.sb("gtA", [128, 6, 64])
            tB = ph.sb("gtB", [64, 2])
            R = ph.sb("gtR", [1, 8, 64])
            tb = Buf("gt")
            G3 = PA[0:L, 0:nch * 8].rearrange("p (c g) -> p c g", g=8)

            def t3(i):
                return tA[0:L, i, 0:n64].rearrange("p (c h) -> p c h", h=4)

            def t2(i):
                return tA[0:L, i, 0:n64]
            bi = gbrep[0:L, e, 0:4].unsqueeze(1).broadcast_to([L, nch, 4])
            bf_ = gbrep[0:L, e, 4:8].unsqueeze(1).broadcast_to([L, nch, 4])
            dve_tt(t3(0), G3[:, :, 0:4], bi, ALU.add, [pab, cb_], [tb])
            dve_tt(t3(1), G3[:, :, 4:8], bf_, ALU.add, [pab, cb_], [tb])
            act(t2(1), t2(1), AF.Exp, [tb], [tb], scale=-1.0)
            act(t2(1), t2(1), AF.Ln, [tb], [tb], bias=1.0)
            mm(PB[0:L, 0:n64], trilT[0:L, 0:L], t2(1), True, True, [tb, cb_], [pbb])
            dve_cp(t2(2), PB[0:L, 0:n64], [pbb], [tb])
            dve_tt(t2(3), t2(0), t2(2), ALU.add, [tb], [tb])
            tr(PA[0:n64, 0:L], t2(3), identf[0:L, 0:L], [tb, cb_], [pab])
            tr(PB[0:n64, 0:L], t2(2), identf[0:L, 0:L], [tb, cb_], [pbb])
            k.op("dve", lambda h: h.tensor_reduce(out=tB[0:n64, 0:1], in_=PA[0:n64, 0:L], axis=AX.X, op=ALU.max), r=[pab], w=[tb])
            dve_cp(tB[0:n64, 1:2], PB[0:n64, L - 1:L], [pbb], [tb])
            tr(PA[0:1, 0:n64], tB[0:n64, 0:1], identf[0:n64, 0:n64], [tb, cb_], [pab])
            tr(PA[0:1, 64:64 + n64], tB[0:n64, 1:2], identf[0:n64, 0:n64], [tb, cb_], [pab])
            dve_cp(R[0:1, 0, 0:n64], PA[0:1, 0:n64], [pab], [tb])
            dve_cp(R[0:1, 1, 0:n64], PA[0:1, 64:64 + n64], [pab], [tb])
            dve_ts(R[0:1, 2, 0:n64], R[0:1, 1, 0:n64], -1.0, None, ALU.mult, None, [tb], [tb])
            if st is Sm:
                k.dma("sp", dmisc, lambda h: h.dma_start(out=R[0:1, 7, 0:4], in_=d["sm"][e]), w=[tb])
            else:
                k.op("dve", lambda h: h.memset(R[0:1, 7, 0:4], 0.0), w=[tb])

            def rv(i):
                return R[0:1, i, 0:n64].rearrange("p (c h) -> p c h", h=4)
            for hh in range(4):
                k.op("dve", lambda h, hh=hh: h.tensor_tensor_scan(out=rv(3)[:, :, hh], data0=rv(0)[:, :, hh], data1=rv(2)[:, :, hh],
                                                                initial=R[0:1, 7, hh:hh + 1], op0=ALU.max, op1=ALU.add), r=[tb], w=[tb])
            dve_tt(R[0:1, 4, 0:n64], R[0:1, 3, 0:n64], R[0:1, 1, 0:n64], ALU.add, [tb], [tb])
            dve_cp(rv(5)[:, 0, :], R[0:1, 7, 0:4], [tb], [tb])
            if nch > 1:
                dve_cp(rv(5)[:, 1:nch, :], rv(3)[:, 0:nch - 1, :], [tb], [tb])
            dve_tt(R[0:1, 6, 0:n64], R[0:1, 5, 0:n64], R[0:1, 4, 0:n64], ALU.subtract, [tb], [tb])
            act(R[0:1, 6, 0:n64], R[0:1, 6, 0:n64], AF.Exp, [tb], [tb])
            mm(PA[0:128, 0:n64], onesf[0:1, 0:128], R[0:1, 6, 0:n64], True, True, [tb, cb_], [pab])
            dve_cp(st.gom[:, 0:n64], PA[0:128, 0:n64], [pab], [gb])
            mm(PB[0:L, 0:n64], onesf[0:1, 0:L], R[0:1, 4, 0:n64], True, True, [tb, cb_], [pbb])
            dve_tt(t2(4), t2(3), PB[0:L, 0:n64], ALU.subtract, [tb, pbb], [tb])
            act(st.gw[0:L, 0:n64], t2(4), AF.Exp, [tb], [gb], bias=math.log(1.0 / 16.0))
            dve_tt(t2(5), t2(2), PB[0:L, 0:n64], ALU.subtract, [tb, pbb], [tb])
            act(st.gf[0:L, 0:n64], t2(5), AF.Exp, [tb], [gb])
            dst = d["pm" if st is P else "smo"][e]
            k.dma("sp", dmisc, lambda h: h.dma_start(out=dst, in_=rv(3)[:, nch - 1, :]), r=[tb])

        def gates_phase(e):
            with Phase() as ph:
                wap, wbuf, _ = wnext("gif")
                for st in streams:
                    L, nch = st.L, st.nch
                    for c in range(nch):
                        gi = (c * L) // 512
                        for kc in range(16):
                            mm(PA[0:L, c * 8:(c + 1) * 8], st.xT[:, kc, c * L:(c + 1) * L], wap[:, kc, 0:8], kc == 0, kc == 15, [wbuf, st.xb[gi]], [pab])
                    gate_math(ph, st, e)

        def mlstm_head(ph, st, e, hd, q2, k2, v2, qb, tl):
            L, nch = st.L, st.nch
            C32, Cb, ktl, vaug2, Stt, hn, sm_ = tl["C32"], tl["Cb"], tl["ktl"], tl["vaug2"], tl["St"], tl["hn"], tl["sm"]
            cbuf, tb = tl["cbuf"], tl["tb"]
            if st is P:
                k.op("dve", lambda h: h.memset(C32[:], 0.0), w=[cbuf])
            else:
                for dc in range(2):
                    k.dma("sp", dmisc, lambda h, dc=dc: h.dma_start(out=C32[:, dc, 0:256], in_=d["sC"][e, hd, dc * 128:(dc + 1) * 128, :]), w=[cbuf])
                    with nc.allow_non_contiguous_dma(reason="tiny"):
                        k.dma("sp", dmisc, lambda h, dc=dc: h.dma_start(out=C32[:, dc, 256:257], in_=d["sn"][e, hd, dc * 128:(dc + 1) * 128, :]), w=[cbuf])
            ktl2, vaug22, Stt2 = tl["ktl2"], tl["vaug22"], tl["St2"]
            kt_ = [ktl, ktl2]
            va_ = [vaug2, vaug22]
            st_ = [Stt, Stt2]
            tba = tl["tba"]

            def stage_a(c):
                cs = slice(c * L, (c + 1) * L)
                col = c * 4 + hd
                wv = st.gw[0:L, col:col + 1]
                p = c % 2
                for dc in range(2):
                    tr(PT[0:L, dc, :], k2[:, dc, cs], identb[:], [qb], [ptb])
                    tr(PT2[0:L, dc, :], v2[:, dc, cs], identb[:], [qb], [pt2b])
                act(kt_[p][0:L, :], PT[0:L, 0:2, :].rearrange("p a b -> p (a b)"), AF.Copy, [ptb, st.gateb], [tba[p]], scale=wv)
                act(va_[p][0:L, 0:256], PT2[0:L, 0:2, :].rearrange("p a b -> p (a b)"), AF.Copy, [pt2b], [tba[p]])
                for dc in range(2):
                    mm(PA[0:L, 0:L], k2[:, dc, cs], q2[:, dc, cs], dc == 0, dc == 1, [qb], [pab])
                dve_stt(st_[p][0:L, 0:L], PA[0:L, 0:L], wv, trilT[0:L, 0:L], ALU.mult, ALU.mult, [pab, st.gateb, cb_], [tba[p]])

            stage_a(0)
            for c in range(nch):
                cs = slice(c * L, (c + 1) * L)
                col = c * 4 + hd
                p = c % 2
                om = st.gom[:, col:col + 1]
                fv = st.gf[0:L, col:col + 1]
                dve_ts(Cb[:], C32[:], om, None, ALU.mult, None, [cbuf, st.gateb], [tb])
                if c + 1 < nch:
                    stage_a(c + 1)
                mm(PB[0:L, 0:257], st_[p][0:L, 0:L], va_[p][0:L, 0:257], True, False, [tba[p]], [pbb])
                for dc in range(2):
                    mm(PB[0:L, 0:257], q2[:, dc, cs], Cb[:, dc, :], False, dc == 1, [qb, tb], [pbb])
                act(sm_[0:L, 0:1], PB[0:L, 256:257], AF.Abs, [pbb], [tb])
                dve_tt(sm_[0:L, 0:1], sm_[0:L, 0:1], fv, ALU.max, [tb, st.gateb], [tb])
                k.op("dve", lambda h: h.bn_stats(out=sm_[0:L, 2:8], in_=PB[0:L, 0:256]), r=[pbb], w=[tb])
                k.op("dve", lambda h: h.bn_aggr(out=sm_[0:L, 8:10], in_=sm_[0:L, 2:8]), r=[tb], w=[tb])
                dve_tt(sm_[0:L, 1:2], sm_[0:L, 0:1], sm_[0:L, 0:1], ALU.mult, [tb], [tb])
                dve_stt(sm_[0:L, 1:2], sm_[0:L, 1:2], LN_EPS, sm_[0:L, 9:10], ALU.mult, ALU.add, [tb], [tb])
                act(sm_[0:L, 1:2], sm_[0:L, 1:2], AF.Sqrt, [tb], [tb])
                k.op("dve", lambda h: h.reciprocal(out=sm_[0:L, 1:2], in_=sm_[0:L, 1:2]), r=[tb], w=[tb])
                dve_ts(hn[0:L, :], PB[0:L, 0:256], sm_[0:L, 8:9], sm_[0:L, 1:2], ALU.subtract, ALU.mult, [pbb, tb], [tb])
                for dc in range(2):
                    tr(PT[:, 4 + dc, 0:L], hn[0:L, dc * 128:(dc + 1) * 128], identb[0:L, 0:L], [tb], [ptb])
                    fbi = 8 + 2 * hd + dc
                    ms = st.mT[:, fbi, cs]
                    dve_stt(ms, PT[:, 4 + dc, 0:L], ngT[:, e, 2 * hd + dc:2 * hd + dc + 1], ms, ALU.mult, ALU.mult, [ptb, cb_, st.mb[fbi]], [st.mb[fbi]])
                for dc in range(2):
                    mm(P4[:, dc, 0:257], kt_[p][0:L, dc * 128:(dc + 1) * 128], va_[p][0:L, 0:257], True, True, [tba[p]], [p4b[dc]])
                    dve_stt(C32[:, dc, :], C32[:, dc, :], om, P4[:, dc, 0:257], ALU.mult, ALU.add, [cbuf, p4b[dc], st.gateb], [cbuf])
            dst = d["pCa" if st is P else "sCa"][e, hd]
            k.dma("sp", dmisc, lambda h: h.dma_start(out=dst, in_=C32[:]), r=[cbuf])

        def b_phase(e):
            with Phase() as ph:
                q2 = {P: ph.sb("q2p", [128, 2, T], BF16), Sm: ph.sb("q2s", [128, 2, S], BF16)}
                k2 = {P: ph.sb("k2p", [128, 2, T], BF16), Sm: ph.sb("k2s", [128, 2, S], BF16)}
                v2 = {P: ph.sb("v2p", [128, 2, T], BF16), Sm: ph.sb("v2s", [128, 2, S], BF16)}
                qb = {P: Buf("qkv2p"), Sm: Buf("qkv2s")}
                tl = {"C32": ph.sb("C32", [128, 2, 257]), "Cb": ph.sb("Cb", [128, 2, 257], BF16), "ktl": ph.sb("ktl", [128, 256], BF16),
                      "vaug2": ph.sb("vaug2", [128, 260], BF16), "St": ph.sb("St", [128, 128], BF16), "hn": ph.sb("hn", [128, 256], BF16),
                      "sm": ph.sb("sm", [128, 16]), "cbuf": Buf("C32"), "tb": Buf("mlt"),
                      "ktl2": ph.sb("ktl2", [128, 256], BF16), "vaug22": ph.sb("vaug22", [128, 260], BF16), "St2": ph.sb("St2", [128, 128], BF16),
                      "tba": [Buf("mlta0"), Buf("mlta1")]}
                k.op("dve", lambda h: h.memset(tl["vaug2"][:, 256:257], 1.0), w=[tl["tba"][0]])
                k.op("dve", lambda h: h.memset(tl["vaug22"][:, 256:257], 1.0), w=[tl["tba"][1]])
                for hd in range(4):
                    for dc in range(2):
                        fbi = 8 + 2 * hd + dc
                        proj("bo", "x", streams, lambda st, gi, g0, gs, pin, pbuf, fbi=fbi: act(st.mT[:, fbi, g0:g0 + gs], pin, AF.Sigmoid, [pbuf], [st.mb[fbi]]))
                    for dc in range(2):
                        fbi = 8 + 2 * hd + dc

                        def cbg(st, gi, g0, gs, pin, pbuf, fbi=fbi):
                            wi = 0 if st is P else 1
                            act(work[:, wi, 0:gs], pin, AF.Silu, [pbuf], [workb[wi]])
                            ms = st.mT[:, fbi, g0:g0 + gs]
                            dve_tt(ms, ms, work[:, wi, 0:gs], ALU.mult, [workb[wi], st.mb[fbi]], [st.mb[fbi]])
                        proj("bg", "x", streams, cbg)
                    for nm, dst_ in (("bq", q2), ("bk", k2), ("bv", v2)):
                        for dc in range(2):
                            proj(nm, "x", streams, lambda st, gi, g0, gs, pin, pbuf, dst_=dst_, dc=dc: act(dst_[st][:, dc, g0:g0 + gs], pin, AF.Copy, [pbuf], [qb[st]]))
                    for st in streams:
                        mlstm_head(ph, st, e, hd, q2[st], k2[st], v2[st], qb[st], tl)

        def a_phase(e):
            scale = 128.0 ** -0.5
            with Phase() as ph:
                qT = ph.sb("qT", [128, T], BF16)
                kT = ph.sb("kT", [128, T], BF16)
                vp = ph.sb("vp", [128, T], BF16)
                PTs = vp[:].rearrange("p (a b) -> p a b", b=128)
                vaug = ph.sb("vaug", [128, 16, 132], BF16)
                Pm = ph.sb("Pm", [128, T], BF16)
                ksum = ph.sb("ksum", [128, 8])
                kmhl = ph.sb("kmhl", [128, 2, 8], BF16)
                Gs = ph.sb("Gs", [128, 16, 8])
                sel = ph.sb("sel", [128, 16, 8])
                sm_ = ph.sb("sma", [128, 32])
                oh = ph.sb("oh", [128, 128], BF16)
                qb_, kb_, vb_, vab, pmb, gb_, tb = Buf("qT"), Buf("kT"), Buf("vp"), Buf("vaug"), Buf("Pm"), Buf("Gs"), Buf("smA")
                k.op("dve", lambda h: h.memset(vaug[:, :, 128:129], 1.0), w=[vab])
                asub = cfg.get("a_sub", 99)
                for hd in range(cfg.get("a_heads", 8)):
                    def cbg(st, gi, g0, gs, pin, pbuf, hd=hd):
                        if st is P:
                            act(st.mT[:, hd, g0:g0 + gs], pin, AF.Silu, [pbuf], [st.mb[hd]])
                        else:
                            act(work[:, 1, 0:gs], pin, AF.Silu, [pbuf], [workb[1]])
                            dve_tt(st.mT[:, hd, g0:g0 + gs], work[:, 1, 0:gs], attnS[:, hd, :], ALU.mult, [workb[1], attnSb], [st.mb[hd]])
                    proj("ag", "x", streams, cbg)
                    if asub < 1:
                        break
                    proj("aq", "x", [P], lambda st, gi, g0, gs, pin, pbuf: act(qT[:, g0:g0 + gs], pin, AF.Copy, [pbuf], [qb_], scale=scale))
                    if asub < 2:
                        break

                    def cbk(st, gi, g0, gs, pin, pbuf, hd=hd):
                        if cfg.get("kskip") not in ("store", "both"):
                            store(d["pkT"][e, hd * 128:(hd + 1) * 128, g0:g0 + gs], pin, pbuf)
                        for hf in range(2):
                            act(kT[:, g0 + hf * 256:g0 + (hf + 1) * 256], pin[:, hf * 256:(hf + 1) * 256], AF.Copy, [pbuf], [kb_, gb_], accum=ksum[:, 2 * gi + hf:2 * gi + hf + 1])
                    proj("ak", "x", [P], cbk)
                    if asub < 3:
                        break

                    def cbv(st, gi, g0, gs, pin, pbuf, hd=hd):
                        store(d["pvT"][e, hd * 128:(hd + 1) * 128, g0:g0 + gs], pin, pbuf)
                        act(vp[:, g0:g0 + gs], pin, AF.Copy, [pbuf], [vb_])
                    proj("av", "x", [P], cbv)
                    if asub < 4:
                        break
                    for half in range(2):
                        for j in range(8):
                            ti = half * 8 + j
                            tr(PT[:, j, :], vp[:, ti * 128:(ti + 1) * 128], identb[:], [vb_], [ptb])
                        act(vaug[:, half * 8:(half + 1) * 8, 0:128], PT[:, :, :], AF.Copy, [ptb], [vab])
                    if cfg.get("a_stop") == "proj":
                        continue
                    dve_cp(kmhl[:, 0, :], ksum[:], [gb_], [gb_])
                    dve_tt(kmhl[:, 1, :], ksum[:], kmhl[:, 0, :], ALU.subtract, [gb_], [gb_])
                    for i in range(8, 16):
                        mm(PA[:, i * 8:(i + 1) * 8], qT[:, i * 128:(i + 1) * 128], kmhl[:, 0, :], True, False, [qb_, gb_], [pab])
                        mm(PA[:, i * 8:(i + 1) * 8], qT[:, i * 128:(i + 1) * 128], kmhl[:, 1, :], False, True, [qb_, gb_], [pab])
                    dve_tt(Gs[:, 8:16, :], PA[:, 64:128].rearrange("p (a b) -> p a b", b=8), gmask[:, 8:16, :], ALU.add, [pab, cb_], [gb_])
                    for i in range(8, 16):
                        k.op("dve", lambda h, i=i: h.max(out=sm_[:, 8:16], in_=Gs[:, i, :]), r=[gb_], w=[tb])
                        dve_ts(sel[:, i, :], Gs[:, i, :], sm_[:, 10:11], None, ALU.is_ge, None, [gb_, tb], [gb_])
                    if cfg.get("a_stop") == "gate":
                        continue
                    for i in range(cfg.get("a_tiles", 16)):
                        j = i // 2
                        nk = (i + 1) * 128
                        t0 = i * 128
                        qs_ = qT[:, t0:t0 + 128]
                        nb_ = (nk + 511) // 512
                        for c in range(nb_):
                            w_ = min(512, nk - c * 512)
                            mm(P4[:, c, 0:w_], qs_, kT[:, c * 512:c * 512 + w_], True, True, [qb_, kb_], [p4b[c]])
                        banks = p4b[0:nb_]
                        if i == 0:
                            dve_tt(P4f[:, 0:128], P4f[:, 0:128], BW[:, hd, 128:256], ALU.add, banks + [cb_], banks)
                        else:
                            dve_tt(P4f[:, t0 - 128:t0], P4f[:, t0 - 128:t0], BW[:, hd, 0:128], ALU.add, banks + [cb_], banks)
                            dve_tt(P4f[:, t0:t0 + 128], P4f[:, t0:t0 + 128], BW[:, hd, 128:256], ALU.add, banks + [cb_], banks)
                        k.op("dve", lambda h, nk=nk: h.tensor_reduce(out=sm_[:, 0:1], in_=P4f[:, 0:nk], axis=AX.X, op=ALU.max, negate=True), r=banks, w=[tb])
                        if j < 4:
                            act(Pm[:, 0:nk], P4f[:, 0:nk], AF.Exp, banks + [tb], [pmb], bias=sm_[:, 0:1])
                        else:
                            dve_ts(sm_[:, 1:2], sm_[:, 0:1], NEG, None, ALU.add, None, [tb], [tb])
                            dve_ts(sm_[:, 16:16 + j], sel[:, i, 0:j], -NEG, sm_[:, 1:2], ALU.mult, ALU.add, [gb_, tb], [tb])
                            for n in range(j + 1):
                                c0, c1 = n * 256, min(n * 256 + 256, nk)
                                bcol = sm_[:, 16 + n:17 + n] if n < j else sm_[:, 0:1]
                                act(Pm[:, c0:c1], P4f[:, c0:c1], AF.Exp, banks + [tb], [pmb], bias=bcol)
                        for kt in range(i + 1):
                            tr(PT[:, kt % 8, :], Pm[:, kt * 128:(kt + 1) * 128], identb[:], [pmb], [ptb])
                            if kt % 8 == 7 or kt == i:
                                k0 = (kt // 8) * 8
                                act(PTs[:, k0:kt + 1, :], PT[:, 0:kt + 1 - k0, :], AF.Copy, [ptb], [vb_])
                        for kt in range(i + 1):
                            mm(PB[:, 0:129], PTs[:, kt, :], vaug[:, kt, 0:129], kt == 0, kt == i, [vb_, vab], [pbb])
                        k.op("dve", lambda h: h.reciprocal(out=sm_[:, 2:3], in_=PB[:, 128:129]), r=[pbb], w=[tb])
                        dve_ts(oh[:], PB[:, 0:128], sm_[:, 2:3], None, ALU.mult, None, [pbb, tb], [tb])
                        tr(PT2[:, 0, :], oh[:], identb[:], [tb], [pt2b])
                        ms = P.mT[:, hd, t0:t0 + 128]
                        dve_tt(ms, PT2[:, 0, :], ms, ALU.mult, [pt2b, P.mb[hd]], [P.mb[hd]])

        def sample_moba(e):
            scale = 128.0 ** -0.5
            for hd in range(8):
                proj("sq", "x", [Sm], lambda st, gi, g0, gs, pin, pbuf, hd=hd: act(qs[:, hd, :], pin, AF.Copy, [pbuf], [qkvb], scale=scale))

                def cbk(st, gi, g0, gs, pin, pbuf, hd=hd):
                    with nc.allow_non_contiguous_dma(reason="tiny"):
                        store(d["skT"][e, hd * 128:(hd + 1) * 128, :], pin, pbuf)
                    act(ks[:, hd, :], pin, AF.Copy, [pbuf], [qkvb])
                proj("sk", "x", [Sm], cbk)

                def cbv(st, gi, g0, gs, pin, pbuf, hd=hd):
                    with nc.allow_non_contiguous_dma(reason="tiny"):
                        store(d["svT"][e, hd * 128:(hd + 1) * 128, :], pin, pbuf)
                    act(vs[:, hd, :], pin, AF.Copy, [pbuf], [qkvb])
                proj("sv", "x", [Sm], cbv)
            if not do_moba_s:
                k.op("dve", lambda h: h.memset(attnS[:], 0.0), w=[attnSb])
                return
            with Phase() as ph:
                Sall = big[0:64, :]
                Pall = big[0:64, :].bitcast(BF16)
                sab = Buf("Sall")
                Kpg = ph.sb("Kpg", [128, 4, 1024])
                kpb = [Buf(f"kpg{i}") for i in range(4)]
                kpd = [k.dsem(f"d_kpg{e}_{i}") for i in range(4)]
                Vpg = Kpg[:, 2:4, :].rearrange("p a b -> p (a b)").bitcast(BF16).rearrange("p (a b) -> p a b", b=1024)
                Kb = Kpg[:, 0:2, :].rearrange("p a b -> p (a b)").bitcast(BF16).rearrange("p (a b) -> p a b", b=1024)
                vpb = [kpb[2], kpb[2], kpb[3], kpb[3]]
                vpb = [Buf(f"vpg{i}") for i in range(4)]
                vpd = kpd
                KT = ph.sb("KT", [128, 2, 256], BF16)
                ktb = [Buf("KT0"), Buf("KT1")]
                QP = ph.sb("QP", [128, 8, 64], BF16)
                ksS = ph.sb("ksS", [128, 8, 64])
                kmS = ph.sb("kmS", [128, 8, 2, 64], BF16)
                ptf = ph.sb("ptf", [128, 128])
                pti = ph.sb("pti", [128, 128], I32)
                idx = ph.sb("idx", [128, 128], I32)
                gS = ph.sb("gS", [64, 64])
                selS = ph.sb("selS", [64, 64])
                bcS = ph.sb("bcS", [64, 64])
                b63 = ph.sb("b63", [64, 256])
                bown = ph.sb("bown", [64, 8])
                cms = ph.sb("cms", [64, 8])
                c31s = ph.sb("c31s", [64, 1])
                bdg = Kpg[0:64, 1, :].rearrange("p (h d) -> p h d", d=128)
                Sown = ph.sb("Sown", [64, 8])
                Pown = ph.sb("Pown", [64, 8], BF16)
                PownT = ph.sb("PownT", [8, 64], BF16)
                dens = ph.sb("dens", [64, 66])
                smS = ph.sb("smS", [64, 16])
                PTp = ph.sb("PTp", [128, 2, 64], BF16)
                ptpb = [Buf("ptp0"), Buf("ptp1")]
                vstm = ph.sb("vstm", [8, 1024], BF16)
                Of = Kpg[0:64, 0, :].rearrange("p (h d) -> p h d", d=128)
                Od = ph.sb("Od", [64, 128])
                Odb = ph.sb("Odb", [64, 128], BF16)
                tb = Buf("smS")
                with nc.allow_non_contiguous_dma(reason="bcast"):
                    k.dma("sp", dmisc, lambda h: h.dma_start(out=pti[:], in_=d["pt"][0:1, :].partition_broadcast(128)), w=[tb])
                    k.dma("sp", dmisc, lambda h: h.dma_start(out=b63[:], in_=d["rels63"]), w=[tb])
                    k.dma("sp", dmisc, lambda h: h.dma_start(out=bown[:], in_=d["relso"]), w=[tb])
                    k.dma("sp", dmisc, lambda h: h.dma_start(out=cms[:], in_=d["c_cmasks"]), w=[tb])
                    k.dma("sp", dmisc, lambda h: h.dma_start(out=c31s[:], in_=d["c31s"]), w=[tb])
                dve_cp(ptf[:], pti[:], [tb], [tb])
                ptv = ptf[:, :].rearrange("p (n two) -> p n two", two=2)
                psel = ph.sb("psel", [128, 64])
                dve_cp(psel[0:64, :], ptv[0:64, :, 0], [tb], [tb])
                dve_cp(psel[64:128, :], ptv[64:128, :, 1], [tb], [tb])
                dve_ts(psel[:], psel[:], float(e * npool), 64.0, ALU.add, ALU.mult, [tb], [tb])
                dve_ts(psel[:], psel[:], iota[:, 0:1], None, ALU.add, None, [tb, cb_], [tb])
                dve_cp(idx[:, 0:64], psel[:], [tb], [tb])
                ck2 = d["ck"].rearrange("(r two) c -> r (two c)", two=2)
                cv2 = d["cv"].rearrange("(r two) c -> r (two c)", two=2)
                dve_ts(b63[:], b63[:], c31s[:, 0:1], None, ALU.subtract, None, [tb], [tb])
                dve_stt(bown[:], bown[:], c31s[:, 0:1], cms[:], ALU.subtract, ALU.add, [tb], [tb])
                k.op("dve", lambda h: h.memset(QP[:], 0.0), w=[tb])
                for hd in range(8):
                    dve_cp(QP[:, hd, hd * 8:(hd + 1) * 8], qs[:, hd, :], [qkvb, tb], [tb])
                idxb = Buf("idx")
                ksb = Buf("ksS")
                dve_cp(idx[:, 0:1], idx[:, 0:1], [tb], [idxb])

                def kload(n):
                    b0 = (n % 2) * 2
                    k.dma("pool", kpd[b0], lambda h, b0=b0, n=n: h.indirect_dma_start(
                        out=Kb[:, b0:b0 + 2, :].rearrange("p a b -> p (a b)"), out_offset=None, in_=ck2,
                        in_offset=bass.IndirectOffsetOnAxis(ap=idx[:, n:n + 1], axis=0)), r=[idxb], w=[kpb[b0], kpb[b0 + 1]])
                kload(0)
                for n in range(64):
                    if n + 1 < 64:
                        kload(n + 1)
                    for hd in range(8):
                        ptx, ptxb = (PT, ptb) if hd < 4 else (PT2, pt2b)
                        for half in range(2):
                            bi_ = (n % 2) * 2 + half
                            tr(ptx[:, 2 * (hd % 4) + half, :], Kb[:, bi_, hd * 128:(hd + 1) * 128], identb[:], [kpb[bi_], cb_], [ptxb])
                    for hd in range(8):
                        ptx, ptxb = (PT, ptb) if hd < 4 else (PT2, pt2b)
                        act(KT[:, hd % 2, :], ptx[:, 2 * (hd % 4):2 * (hd % 4) + 2, :].rearrange("p a b -> p (a b)"), AF.Copy, [ptxb], [ktb[hd % 2], ksb], accum=ksS[:, hd, n:n + 1])
                        mm(PA[0:64, 0:256], QP[:, hd, :], KT[:, hd % 2, :], hd == 0, hd == 7, [tb, ktb[hd % 2]], [pab])
                    dve_cp(Sall[:, n * 256:(n + 1) * 256], PA[0:64, 0:256], [pab], [sab])
                dve_cp(ksS[:, 0, 0:1], ksS[:, 0, 0:1], [ksb, tb], [tb])
                for hd in range(8):
                    dve_cp(kmS[:, hd, 0, :], ksS[:, hd, :], [tb], [tb])
                    dve_tt(kmS[:, hd, 1, :], ksS[:, hd, :], kmS[:, hd, 0, :], ALU.subtract, [tb], [tb])
                for hd in range(8):
                    mm(PB[0:64, 0:64], QP[:, hd, :], kmS[:, hd, 0, :], hd == 0, False, [tb], [pbb])
                    mm(PB[0:64, 0:64], QP[:, hd, :], kmS[:, hd, 1, :], False, hd == 7, [tb], [pbb])
                dve_cp(gS[:], PB[0:64, 0:64], [pbb], [tb])
                k.op("dve", lambda h: h.max(out=smS[:, 8:16], in_=gS[:]), r=[tb], w=[tb])
                dve_ts(selS[:], gS[:], smS[:, 10:11], None, ALU.is_ge, None, [tb], [tb])
                for hd in range(8):
                    mm(PB[0:64, 64:72], QP[:, hd, :], ks[:, hd, :], hd == 0, hd == 7, [tb, qkvb], [pbb])
                dve_tt(Sown[:], PB[0:64, 64:72], bown[:], ALU.add, [pbb, tb], [tb])
                dve_tt(Sall[:, 63 * 256:64 * 256], Sall[:, 63 * 256:64 * 256], b63[:], ALU.add, [sab, tb], [sab])
                k.op("dve", lambda h: h.tensor_reduce(out=smS[:, 0:1], in_=Sall[:, :], axis=AX.X, op=ALU.max), r=[sab], w=[tb])
                k.op("dve", lambda h: h.tensor_reduce(out=smS[:, 1:2], in_=Sown[:], axis=AX.X, op=ALU.max), r=[tb], w=[tb])
                dve_tt(smS[:, 0:1], smS[:, 0:1], smS[:, 1:2], ALU.max, [tb], [tb])
                dve_ts(smS[:, 0:1], smS[:, 0:1], -1.0, None, ALU.mult, None, [tb], [tb])
                dve_ts(smS[:, 1:2], smS[:, 0:1], NEG, None, ALU.add, None, [tb], [tb])
                dve_ts(bcS[:], selS[:], -NEG, smS[:, 1:2], ALU.mult, ALU.add, [tb], [tb])
                k.op("dve", lambda h: h.memset(dens[:], 0.0), w=[tb])
                for n in range(64):
                    act(Pall[:, n * 256:(n + 1) * 256], Sall[:, n * 256:(n + 1) * 256], AF.Exp, [sab, tb], [sab], bias=bcS[:, n:n + 1], accum=dens[:, n:n + 1])
                act(Pown[:], Sown[:], AF.Exp, [tb], [tb], bias=smS[:, 0:1], accum=dens[:, 64:65])
                k.op("dve", lambda h: h.tensor_reduce(out=smS[:, 2:3], in_=dens[:, 0:65], axis=AX.X, op=ALU.add), r=[tb], w=[tb])
                k.op("dve", lambda h: h.reciprocal(out=smS[:, 3:4], in_=smS[:, 2:3]), r=[tb], w=[tb])
                k.op("dve", lambda h: h.memset(smS[:, 15:16], 0.0), w=[kpb[2], kpb[3]] + vpb)

                def vload(n):
                    v0 = (n % 2) * 2
                    k.dma("pool", vpd[v0], lambda h, v0=v0, n=n: h.indirect_dma_start(
                        out=Vpg[:, v0:v0 + 2, :].rearrange("p a b -> p (a b)"), out_offset=None, in_=cv2,
                        in_offset=bass.IndirectOffsetOnAxis(ap=idx[:, n:n + 1], axis=0)), r=[idxb], w=[vpb[v0], vpb[v0 + 1]])
                vload(0)
                for pg in range(128):
                    b2 = pg % 2
                    b4 = ((pg // 2) % 2) * 2 + (pg % 2)
                    if pg % 2 == 0 and pg // 2 + 1 < 64:
                        vload(pg // 2 + 1)
                    tr(PT[:, b2, 0:64], Pall[:, pg * 128:(pg + 1) * 128], identb[0:64, 0:64], [sab, cb_], [ptb if b2 == 0 else pt2b])
                    act(PTp[:, b2, :], PT[:, b2, 0:64], AF.Copy, [ptb if b2 == 0 else pt2b], [ptpb[b2]])
                    for c in range(2):
                        mm(P4[0:64, c, :], PTp[:, b2, :], Vpg[:, b4, c * 512:(c + 1) * 512], pg == 0, False, [ptpb[b2], vpb[b4]], [p4b[c]])
                k.dma("sp", dmisc, lambda h: h.dma_start(out=bdg, in_=d["c_bdiag"]), r=[kpb[2], kpb[3]], w=[kpb[2], kpb[3]])
                tr(PT2[0:8, 0, 0:64], Pown[:], identb[0:64, 0:64], [tb, cb_], [pt2b])
                act(PownT[:], PT2[0:8, 0, 0:64], AF.Copy, [pt2b], [tb])
                for hd in range(8):
                    tr(PT[0:8, hd, :], vs[:, hd, :], identb[:], [qkvb, cb_], [ptb])
                act(vstm[:], PT[0:8, :, :].rearrange("p a b -> p (a b)"), AF.Copy, [ptb], [tb])
                for c in range(2):
                    mm(P4[0:64, c, :], PownT[:], vstm[:, c * 512:(c + 1) * 512], False, True, [tb], [p4b[c]])
                dve_tt(Of, P4[0:64, 0:2, :].rearrange("p a (b c) -> p (a b) c", c=128), bdg, ALU.mult, [p4b[0], p4b[1], tb, kpb[2], kpb[3]], [kpb[0], kpb[1]])
                k.op("dve", lambda h: h.tensor_reduce(out=Od[:], in_=Of.rearrange("p h d -> p d h"), axis=AX.X, op=ALU.add), r=[kpb[0], kpb[1]], w=[tb])
                dve_ts(Odb[:], Od[:], smS[:, 3:4], None, ALU.mult, None, [tb], [tb])
                tr(PT2[:, 1, 0:64], Odb[:], identb[0:64, 0:64], [tb, cb_], [pt2b])
                dve_cp(attnS[:], PT2[:, 1, 0:64].rearrange("p (h q) -> p h q", q=8), [pt2b], [attnSb])

        def even_layer(e):
            stop = cfg.get("even_stop")
            gates_phase(e)
            if stop == "gates":
                return False
            if do_sample:
                sample_moba(e)
            if stop == "smoba":
                return False
            a_phase(e)
            if stop == "a":
                return False
            b_phase(e)
            return True

        for li, l in enumerate(layers):
            last = li == len(layers) - 1
            if l % 2 == 0:
                if not even_layer(l // 2):
                    break
            else:
                odd_layer(l // 2)
            out_ln(l, last)
        k.barrier()
    return nc


def _consts():
    c = {}
    c["c_identf"] = np.eye(128, dtype=np.float32)
    s_ = np.arange(128)
    c["c_trilT"] = (s_[:, None] <= s_[None, :]).astype(np.float32)
    ql = np.arange(128)[:, None]
    kw = np.arange(256)[None, :]
    dd = ql + 128 - kw
    c["c_cmask"] = np.where(dd >= 0, 0.0, NEG).astype(np.float32)
    gm = np.zeros((128, 16, 8), np.float32)
    for i in range(16):
        gm[:, i, (i // 2):] = -1e30
    c["c_gmask"] = gm
    c["c_iota"] = (np.arange(128) % 64).astype(np.float32)[:, None]
    bd = np.zeros((64, 8, 128), np.float32)
    for p in range(64):
        bd[p, p // 8, :] = 1.0
    c["c_bdiag"] = bd
    qq = (np.arange(64) % 8)[:, None]
    kk = np.arange(8)[None, :]
    c["c_cmasks"] = np.where(kk <= qq, 0.0, NEG).astype(np.float32)
    return c


def prep_inputs(inp, core, cfg):
    npool = cfg.get("npool", NPOOL)
    b = core % 4
    s = core
    m = {}
    m["xpT"] = np.ascontiguousarray(inp["x_prompt"][b].T)
    m["xsT"] = np.ascontiguousarray(inp["x_sample"][s].T)
    m["ck"] = inp["cache_k"].reshape(2 * npool * 128, 1024)
    m["cv"] = inp["cache_v"].reshape(2 * npool * 128, 1024)
    m["pt"] = np.ascontiguousarray(inp["page_table"][s:s + 1]).astype(np.int32)
    m["sC"] = np.ascontiguousarray(inp["state_C"][:, s])
    m["sn"] = np.ascontiguousarray(inp["state_n"][:, s]).reshape(2, 4, 256, 1)
    m["sm"] = np.ascontiguousarray(inp["state_m"][:, s]).reshape(2, 1, 4)
    m["sconv"] = np.ascontiguousarray(inp["state_conv"][:, s].reshape(2, 2, 16, 128).transpose(0, 3, 2, 1))
    m["wie"] = inp["w_in_even"]
    m["woe"] = inp["w_out_even"]
    m["gbrep"] = np.ascontiguousarray(np.broadcast_to(inp["mlstm_gate_bias"][None], (128, 2, 8)))
    m["ngT"] = np.ascontiguousarray(inp["mlstm_norm_g"].reshape(2, 8, 128).transpose(0, 2, 1))
    rb = inp["rel_bias"]
    ql = np.arange(128)[:, None]
    kw = np.arange(256)[None, :]
    bk = t5_bucket_np(ql + 128 - kw)
    m["relp"] = np.ascontiguousarray(rb[:, bk].transpose(1, 0, 2))
    hq = np.arange(64)
    dist63 = 256 + (hq % 8)[:, None] - np.arange(256)[None, :]
    r63 = rb[(hq // 8)[:, None], t5_bucket_np(dist63)]
    jj = np.arange(128)
    perm = np.concatenate([np.where(jj < 64, 2 * jj + t2, 128 + 2 * (jj - 64) + t2) for t2 in range(2)])
    m["rels63"] = np.ascontiguousarray(r63[:, perm])
    disto = (hq % 8)[:, None] - np.arange(8)[None, :]
    m["relso"] = np.ascontiguousarray(rb[(hq // 8)[:, None], t5_bucket_np(disto)])
    m["c31"] = np.ascontiguousarray(np.broadcast_to(rb[:, 31][None, :], (128, 8)))
    m["c31s"] = np.ascontiguousarray(rb[hq // 8, 31][:, None])
    m["wio"] = inp["w_in_odd"]
    m["woo"] = inp["w_out_odd"]
    m["cwT"] = np.ascontiguousarray(inp["conv_w"].reshape(2, 3, 16, 128).transpose(0, 3, 2, 1))
    m["lngT"] = np.ascontiguousarray(inp["ln_g"].reshape(4, 16, 128).transpose(0, 2, 1))
    m["lnbT"] = np.ascontiguousarray(inp["ln_b"].reshape(4, 16, 128).transpose(0, 2, 1))
    m.update(_consts())
    return {kk: np.ascontiguousarray(vv) for kk, vv in m.items()}


_NC_CACHE = {}


def kernel(**inputs):
    cfg = {}
    inp = {kk: np.asarray(vv) for kk, vv in inputs.items()}
    if "full" not in _NC_CACHE:
        _NC_CACHE["full"] = build(cfg)
    nc = _NC_CACHE["full"]
    in_maps = [prep_inputs(inp, c, cfg) for c in range(8)]
    res = run_bass_kernel_spmd(nc, in_maps, core_ids=list(range(8))).results
    f = np.float32
    y_prompt = np.stack([res[b]["ypT"].T for b in range(4)]).astype(f)
    y_sample = np.stack([res[s]["ysT"].T for s in range(8)]).astype(f)

    def kv(name, cores, n):
        return np.stack([res[c][name].transpose(0, 2, 1).reshape(2, n, 8, 128) for c in cores], axis=1).astype(f)

    def cst(name, cores):
        a = np.stack([res[c][name] for c in cores], axis=1)
        a = a.transpose(0, 1, 2, 4, 3, 5).reshape(2, len(cores), 4, 256, 257)
        return np.ascontiguousarray(a[..., :256]).astype(f), np.ascontiguousarray(a[..., 256]).astype(f)

    def conv(name, cores):
        a = np.stack([res[c][name] for c in cores], axis=1)
        return np.ascontiguousarray(a.transpose(0, 1, 4, 3, 2).reshape(2, len(cores), 2, 2048)).astype(f)

    pc, pn = cst("pCa", range(4))
    sc, sn = cst("sCa", range(8))
    pm = np.stack([res[c]["pm"][:, 0, :] for c in range(4)], axis=1).astype(f)
    sm = np.stack([res[c]["smo"][:, 0, :] for c in range(8)], axis=1).astype(f)
    return (y_prompt, y_sample, kv("pkT", range(4), T), kv("pvT", range(4), T), pc, pn, pm, conv("pbT", range(4)),
            kv("skT", range(8), S), kv("svT", range(8), S), sc, sn, sm, conv("sbT", range(8)))
```
